# Optimizing a Trainium2 kernel written in Bass

```python
import jax, jax.numpy as jnp
from jax import lax
import numpy as np

D_MODEL = 1024
BATCH = 8
SEQ = 2048
DEPTH = 2
DEC_BATCH = 128
DEC_SEQ = 4
PAST_LEN = 16384
PAGE_SIZE = 128

N_META = 16
D_MIX = D_MODEL
D_MLSTM = D_MIX // 2
D_LRU = D_MIX - D_MLSTM
MLSTM_HEADS = 4
MLSTM_HEAD_DIM = D_MLSTM // MLSTM_HEADS
LRU_BLOCKS = 8
LRU_BLOCK_DIM = D_LRU // LRU_BLOCKS
CONV_WIDTH = 4
LRU_C = 8.0
CHUNK = 64
D_FF = ((8 * D_MODEL // 3 + 127) // 128) * 128
EPS = 1e-6
SPLITS = [D_MLSTM, 2 * D_MLSTM, 3 * D_MLSTM, 4 * D_MLSTM, 4 * D_MLSTM + MLSTM_HEADS,
          4 * D_MLSTM + 2 * MLSTM_HEADS, 4 * D_MLSTM + 2 * MLSTM_HEADS + D_LRU]
D_IN = 4 * D_MLSTM + 2 * MLSTM_HEADS + 2 * D_LRU

kernel_name = "hymba_mlstm_rglru_macaron_step"


def rmsnorm(x, g):
    xf = x.astype(jnp.float32)
    y = xf * lax.rsqrt(jnp.mean(xf * xf, axis=-1, keepdims=True) + EPS) * g.astype(jnp.float32)
    return y.astype(x.dtype)


def swiglu(x, w_gate, w_up, w_down):
    return (jax.nn.silu(x @ w_gate) * (x @ w_up)) @ w_down


def mlstm_chunk(q, k, v, ig, logf, C0, n0, m0):
    L = q.shape[2]
    b = jnp.cumsum(logf, axis=-1)
    causal = jnp.tril(jnp.ones((L, L), dtype=bool))
    D = jnp.where(causal, b[..., :, None] - b[..., None, :] + ig[..., None, :], -jnp.inf)
    inter = b + m0[..., None]
    m = jnp.maximum(jnp.max(D, axis=-1), inter)
    w_intra = jnp.exp(D - m[..., None])
    w_inter = jnp.exp(inter - m)
    s = jnp.einsum('bhtd,bhsd->bhts', q, k) * w_intra
    num = jnp.einsum('bhts,bhse->bhte', s, v) + w_inter[..., None] * jnp.einsum('bhed,bhtd->bhte', C0, q)
    den = jnp.sum(s, axis=-1) + w_inter * jnp.einsum('bhd,bhtd->bht', n0, q)
    h = num / jnp.maximum(jnp.abs(den), jnp.exp(-m))[..., None]
    m_last = m[..., -1]
    w_state = jnp.exp(b[..., -1:] - b + ig - m_last[..., None])
    decay = jnp.exp(inter[..., -1] - m_last)
    C = decay[..., None, None] * C0 + jnp.einsum('bhs,bhse,bhsd->bhed', w_state, v, k)
    n = decay[..., None] * n0 + jnp.einsum('bhs,bhsd->bhd', w_state, k)
    return h, (C, n, m_last)


def mlstm_prompt(q, k, v, ig, logf):
    B, H, T, dh = q.shape
    C0 = jnp.zeros((B, H, dh, dh), jnp.float32)
    n0 = jnp.zeros((B, H, dh), jnp.float32)
    m0 = jnp.zeros((B, H), jnp.float32)
    h_meta, state = mlstm_chunk(q[:, :, :N_META], k[:, :, :N_META], v[:, :, :N_META],
                                ig[:, :, :N_META], logf[:, :, :N_META], C0, n0, m0)
    n_chunks = (T - N_META) // CHUNK

    def blocks(a):
        r = a[:, :, N_META:]
        return jnp.moveaxis(r.reshape(r.shape[:2] + (n_chunks, CHUNK) + r.shape[3:]), 2, 0)

    def step(carry, xs):
        h, carry = mlstm_chunk(*xs, *carry)
        return carry, h

    state, hs = lax.scan(step, state, (blocks(q), blocks(k), blocks(v), blocks(ig), blocks(logf)))
    hs = jnp.moveaxis(hs, 0, 2).reshape(B, H, T - N_META, dh)
    return jnp.concatenate([h_meta, hs], axis=2), state


def linear_scan_combine(left, right):
    a_l, b_l = left
    a_r, b_r = right
    return a_l * a_r, a_r * b_l + b_r


def mixer(u, W, l, st):
    f32 = jnp.float32
    B, T, _ = u.shape
    H, dh = MLSTM_HEADS, MLSTM_HEAD_DIM
    z = u @ W['w_in'][l]
    q, k, v, o, ig, fg, xr, yg = jnp.split(z, SPLITS, axis=-1)

    def heads(a):
        return a.astype(f32).reshape(B, T, H, dh).transpose(0, 2, 1, 3)

    q = heads(q)
    k = heads(k) * (dh ** -0.5)
    v = heads(v)
    gates = jnp.concatenate([ig, fg], axis=-1).astype(f32) + W['b_gates'][l].astype(f32)
    ig = gates[..., :H].transpose(0, 2, 1)
    logf = jax.nn.log_sigmoid(gates[..., H:]).transpose(0, 2, 1)
    if st is None:
        h, (C, n, m) = mlstm_prompt(q, k, v, ig, logf)
        conv_buf = jnp.zeros((B, CONV_WIDTH - 1, D_LRU), f32)
        h0 = jnp.zeros((B, D_LRU), f32)
    else:
        C0, n0, m0, h0, conv_buf = (s.astype(f32) for s in st)
        h, (C, n, m) = mlstm_chunk(q, k, v, ig, logf, C0, n0, m0)
    h = h.transpose(0, 2, 1, 3)
    h = h * lax.rsqrt(jnp.mean(h * h, axis=-1, keepdims=True) + EPS) * W['g_mlstm_out'][l].astype(f32).reshape(H, dh)
    h_m = (jax.nn.sigmoid(o.astype(f32)).reshape(B, T, H, dh) * h).reshape(B, T, D_MLSTM)

    cw = W['conv_w'][l].astype(f32)
    xp = jnp.concatenate([conv_buf, xr.astype(f32)], axis=1)
    xc = W['conv_b'][l].astype(f32) + sum(xp[:, j:j + T] * cw[j] for j in range(CONV_WIDTH))
    new_buf = xp[:, T:]
    xb = xc.reshape(B, T, LRU_BLOCKS, LRU_BLOCK_DIM)
    r = jax.nn.sigmoid(jnp.einsum('btnd,nde->btne', xb, W['w_rg_a'][l].astype(f32)).reshape(B, T, D_LRU)
                       + W['b_rg_a'][l].astype(f32))
    i = jax.nn.sigmoid(jnp.einsum('btnd,nde->btne', xb, W['w_rg_x'][l].astype(f32)).reshape(B, T, D_LRU)
                       + W['b_rg_x'][l].astype(f32))
    log_a = -LRU_C * r * jax.nn.softplus(-W['lru_lambda'][l].astype(f32))
    a = jnp.exp(log_a)
    b_in = jnp.sqrt(-jnp.expm1(2.0 * log_a)) * (i * xc)
    A, Hc = lax.associative_scan(linear_scan_combine, (a, b_in), axis=1)
    h_l = Hc + A * h0[:, None]
    new_h = h_l[:, -1]
    h_l = h_l * lax.rsqrt(jnp.mean(h_l * h_l, axis=-1, keepdims=True) + EPS) * W['g_lru_out'][l].astype(f32)
    y_l = h_l * jax.nn.gelu(yg.astype(f32))

    mix = jnp.concatenate([h_m, y_l], axis=-1).astype(u.dtype) @ W['w_out'][l]
    return mix, (C, n, m, new_h, new_buf)


def trunk(x, W, states):
    collected = ([], [], [], [], [])
    for l in range(DEPTH):
        x = x + 0.5 * swiglu(rmsnorm(x, W['g_ff1'][l]), W['w_ff1_gate'][l], W['w_ff1_up'][l], W['w_ff1_down'][l])
        st = None if states is None else tuple(s[l] for s in states)
        mix, new_st = mixer(rmsnorm(x, W['g_mix'][l]), W, l, st)
        x = x + mix
        x = x + 0.5 * swiglu(rmsnorm(x, W['g_ff2'][l]), W['w_ff2_gate'][l], W['w_ff2_up'][l], W['w_ff2_down'][l])
        for lst, s in zip(collected, new_st):
            lst.append(s.astype(x.dtype))
    y = rmsnorm(x, W['g_final'])
    return y, tuple(jnp.stack(lst) for lst in collected)


def setup_inputs(seed: int = 0) -> dict:
    key = jax.random.key(seed)
    ks = jax.random.split(key, 40)
    f32 = jnp.float32

    def nrm(k, shape, scale):
        return jax.random.normal(k, shape, f32) * scale

    H, dh = MLSTM_HEADS, MLSTM_HEAD_DIM
    a_c = jax.random.uniform(ks[0], (DEPTH, D_LRU), f32, minval=0.9, maxval=0.999)
    s = a_c ** (1.0 / LRU_C)
    lru_lambda = jnp.log(s) - jnp.log1p(-s)
    b_gates = jnp.concatenate([nrm(ks[1], (DEPTH, H), 0.1),
                               3.0 + 3.0 * jax.random.uniform(ks[2], (DEPTH, H), f32)], axis=-1)
    return {
        'x_prompt': nrm(ks[3], (BATCH, SEQ, D_MODEL), 1.0),
        'x_sample': nrm(ks[4], (DEC_BATCH, DEC_SEQ, D_MODEL), 1.0),
        'state_mlstm_C': nrm(ks[5], (DEPTH, DEC_BATCH, H, dh, dh), 0.1),
        'state_mlstm_n': nrm(ks[6], (DEPTH, DEC_BATCH, H, dh), 0.5),
        'state_mlstm_m': nrm(ks[7], (DEPTH, DEC_BATCH, H), 1.0),
        'state_lru_h': nrm(ks[8], (DEPTH, DEC_BATCH, D_LRU), 0.5),
        'state_conv': nrm(ks[9], (DEPTH, DEC_BATCH, CONV_WIDTH - 1, D_LRU), 1.0),
        'meta_tokens': nrm(ks[10], (N_META, D_MODEL), 1.0),
        'g_ff1': 1.0 + nrm(ks[11], (DEPTH, D_MODEL), 0.02),
        'w_ff1_gate': nrm(ks[12], (DEPTH, D_MODEL, D_FF), D_MODEL ** -0.5),
        'w_ff1_up': nrm(ks[13], (DEPTH, D_MODEL, D_FF), D_MODEL ** -0.5),
        'w_ff1_down': nrm(ks[14], (DEPTH, D_FF, D_MODEL), D_FF ** -0.5),
        'g_mix': 1.0 + nrm(ks[15], (DEPTH, D_MODEL), 0.02),
        'w_in': nrm(ks[16], (DEPTH, D_MODEL, D_IN), D_MODEL ** -0.5),
        'b_gates': b_gates,
        'conv_w': nrm(ks[17], (DEPTH, CONV_WIDTH, D_LRU), CONV_WIDTH ** -0.5),
        'conv_b': nrm(ks[18], (DEPTH, D_LRU), 0.02),
        'w_rg_a': nrm(ks[19], (DEPTH, LRU_BLOCKS, LRU_BLOCK_DIM, LRU_BLOCK_DIM), LRU_BLOCK_DIM ** -0.5),
        'b_rg_a': nrm(ks[20], (DEPTH, D_LRU), 0.02),
        'w_rg_x': nrm(ks[21], (DEPTH, LRU_BLOCKS, LRU_BLOCK_DIM, LRU_BLOCK_DIM), LRU_BLOCK_DIM ** -0.5),
        'b_rg_x': nrm(ks[22], (DEPTH, D_LRU), 0.02),
        'lru_lambda': lru_lambda,
        'g_mlstm_out': 1.0 + nrm(ks[23], (DEPTH, D_MLSTM), 0.02),
        'g_lru_out': 1.0 + nrm(ks[24], (DEPTH, D_LRU), 0.02),
        'w_out': nrm(ks[25], (DEPTH, D_MIX, D_MODEL), D_MIX ** -0.5),
        'g_ff2': 1.0 + nrm(ks[26], (DEPTH, D_MODEL), 0.02),
        'w_ff2_gate': nrm(ks[27], (DEPTH, D_MODEL, D_FF), D_MODEL ** -0.5),
        'w_ff2_up': nrm(ks[28], (DEPTH, D_MODEL, D_FF), D_MODEL ** -0.5),
        'w_ff2_down': nrm(ks[29], (DEPTH, D_FF, D_MODEL), D_FF ** -0.5),
        'g_final': 1.0 + nrm(ks[30], (D_MODEL,), 0.02),
    }


def reference(x_prompt, x_sample, state_mlstm_C, state_mlstm_n, state_mlstm_m, state_lru_h, state_conv,
              meta_tokens, g_ff1, w_ff1_gate, w_ff1_up, w_ff1_down, g_mix, w_in, b_gates, conv_w, conv_b,
              w_rg_a, b_rg_a, w_rg_x, b_rg_x, lru_lambda, g_mlstm_out, g_lru_out, w_out,
              g_ff2, w_ff2_gate, w_ff2_up, w_ff2_down, g_final):
    W = dict(g_ff1=g_ff1, w_ff1_gate=w_ff1_gate, w_ff1_up=w_ff1_up, w_ff1_down=w_ff1_down,
             g_mix=g_mix, w_in=w_in, b_gates=b_gates, conv_w=conv_w, conv_b=conv_b,
             w_rg_a=w_rg_a, b_rg_a=b_rg_a, w_rg_x=w_rg_x, b_rg_x=b_rg_x, lru_lambda=lru_lambda,
             g_mlstm_out=g_mlstm_out, g_lru_out=g_lru_out, w_out=w_out,
             g_ff2=g_ff2, w_ff2_gate=w_ff2_gate, w_ff2_up=w_ff2_up, w_ff2_down=w_ff2_down, g_final=g_final)
    B = x_prompt.shape[0]
    meta = jnp.broadcast_to(meta_tokens.astype(x_prompt.dtype)[None], (B, N_META, D_MODEL))
    xp = jnp.concatenate([meta, x_prompt], axis=1)
    yp, (p_C, p_n, p_m, p_h, p_conv) = trunk(xp, W, None)
    y_prompt = yp[:, N_META:]
    y_sample, (s_C, s_n, s_m, s_h, s_conv) = trunk(
        x_sample, W, (state_mlstm_C, state_mlstm_n, state_mlstm_m, state_lru_h, state_conv))
    return (y_prompt, y_sample, p_C, p_n, p_m, p_h, p_conv, s_C, s_n, s_m, s_h, s_conv)
```

```python
import os
import numpy as np
import ml_dtypes
KDBG = int(os.environ.get('KDBG', '9'))
KNOS = int(os.environ.get('KNOS', '0'))
KSKIP = int(os.environ.get('KSKIP', '0'))
KCH = int(os.environ.get('KCH', '99'))
KSELF = int(os.environ.get('KSELF', '1'))
from contextlib import ExitStack
import concourse.bass as bass
import concourse.mybir as mybir
from concourse.bass_utils import run_bass_kernel_spmd

F32 = mybir.dt.float32
BF16 = mybir.dt.bfloat16
AF = mybir.ActivationFunctionType
ALU = mybir.AluOpType

NCORES = 8
D = 1024
NKC = 8
SEQ = 2048
NMETA = 16
TP = NMETA + SEQ
NSQ = 16
NST = 4
NS = NSQ * NST
T = TP + NS
DFF = 2816
NJ = DFF // 128
NJP = NJ // 2
H = 4
DH = 128
DLRU = 512
DEPTH = 2
EPS = 1e-6
TILES = [(0, 512), (512, 512), (1024, 512), (1536, 512), (2048, 80)]
CHUNKS = [(128 * c, 128) for c in range(16)] + [(2048, 16), (TP, NS)]
NCH = len(CHUNKS)
GROUPS = [[0, 1, 2], [3, 4, 5], [6, 7, 8], [9, 10]]
KSCALE = DH ** -0.5

def _vec_layout():
    off = {}
    n = 0
    for l in range(DEPTH):
        for nm, w in (("g_ff1", 8), ("g_mix", 8), ("g_ff2", 8), ("conv_w", 16), ("conv_b", 4),
                      ("b_rg_a", 4), ("b_rg_x", 4), ("lru_lambda", 4), ("g_lru_out", 4),
                      ("b_i", 1), ("b_f", 1)):
            off[(nm, l)] = n
            n += w
    off[("g_final", 0)] = n
    n += 8
    return off, n

VOFF, NV = _vec_layout()


class Res:
    __slots__ = ("w", "r")

    def __init__(self):
        self.w = None
        self.r = []


def RL(*dims):
    if len(dims) == 0:
        return Res()
    return [RL(*dims[1:]) for _ in range(dims[0])]


def flat(x):
    if isinstance(x, Res):
        return [x]
    out = []
    for y in x:
        out.extend(flat(y))
    return out


class Eng:
    def __init__(self, e, sem, name):
        self.e = e
        self.sem = sem
        self.n = 0
        self.seen = {}
        self.name = name


class Ctx:
    def __init__(self, nc, es):
        self.nc = nc
        self.es = es
        mk = lambda nm: es.enter_context(nc.semaphore(nm))
        self.pe = Eng(nc.tensor, mk("c_pe"), "pe")
        self.act = Eng(nc.scalar, mk("c_act"), "act")
        self.dve = Eng(nc.vector, mk("c_dve"), "dve")
        self.pool = Eng(nc.gpsimd, mk("c_pool"), "pool")
        self.sp = Eng(nc.sync, None, "sp")
        self.compute = [self.pe, self.act, self.dve]
        self.dsems = {}
        for q, nq in ((self.sp, 16), (self.pool, 16)):
            self.dsems[q.name] = [[mk(f"d_{q.name}{i}"), 0] for i in range(nq)]
        self.dptr = {"sp": 0, "pool": 0}
        self.semid = {}
        self.ps = []
        self.psr = []
        for i in range(8):
            self.ps.append(es.enter_context(nc.psum_tensor(f"ps{i}", [128, 512], F32)))
            self.psr.append(Res())
        self.psi = 0

    def sid(self, sem):
        k = id(sem)
        if k not in self.semid:
            self.semid[k] = sem
        return k

    def psn(self):
        i = self.psi
        self.psi = (self.psi + 1) % 8
        return self.ps[i], self.psr[i]

    def _waits(self, eng, R, W):
        deps = {}
        for r in R:
            if r.w is not None:
                k = self.sid(r.w[0])
                deps[k] = max(deps.get(k, 0), r.w[1])
        for w in W:
            if w.w is not None:
                k = self.sid(w.w[0])
                deps[k] = max(deps.get(k, 0), w.w[1])
            for t in w.r:
                k = self.sid(t[0])
                deps[k] = max(deps.get(k, 0), t[1])
        for k, v in deps.items():
            if not KSELF and eng.sem is not None and k == id(eng.sem):
                continue
            if eng.seen.get(k, 0) < v:
                eng.e.wait_ge(self.semid[k], v)
                eng.seen[k] = v

    def _done(self, tok, R, W):
        for r in R:
            r.r.append(tok)
        for w in W:
            w.w = tok
            w.r = []

    def op(self, eng, fn, R=(), W=()):
        R = flat(R)
        W = flat(W)
        self._waits(eng, R, W)
        ins = fn(eng.e)
        eng.n += 1
        ins.then_inc(eng.sem, 1)
        self._done((eng.sem, eng.n), R, W)

    def mm(self, out, parts, R, W):
        R = flat(R)
        W = flat(W)
        eng = self.pe
        self._waits(eng, R, W)
        n = len(parts)
        ins = None
        for i, (l, r) in enumerate(parts):
            ins = eng.e.matmul(out, lhsT=l, rhs=r, start=(i == 0), stop=(i == n - 1))
        eng.n += 1
        ins.then_inc(eng.sem, 1)
        self._done((eng.sem, eng.n), R, W)

    def mm_multi(self, groups, R, W):
        R = flat(R)
        W = flat(W)
        eng = self.pe
        self._waits(eng, R, W)
        ins = None
        for (o, l, r, st, sp) in groups:
            ins = eng.e.matmul(o, lhsT=l, rhs=r, start=st, stop=sp)
        eng.n += 1
        ins.then_inc(eng.sem, 1)
        self._done((eng.sem, eng.n), R, W)

    def tr(self, out, in_, ident, R, W):
        R = flat(R)
        W = flat(W)
        eng = self.pe
        self._waits(eng, R, W)
        ins = eng.e.transpose(out, in_, ident)
        eng.n += 1
        ins.then_inc(eng.sem, 1)
        self._done((eng.sem, eng.n), R, W)

    def dma(self, q, out, in_, R=(), W=()):
        R = flat(R)
        W = flat(W)
        self._waits(q, R, W)
        pool = self.dsems[q.name]
        i = self.dptr[q.name]
        self.dptr[q.name] = (i + 1) % len(pool)
        sem, cnt = pool[i]
        k = self.sid(sem)
        if cnt > 0 and q.seen.get(k, 0) < 16 * cnt:
            q.e.wait_ge(sem, 16 * cnt)
            q.seen[k] = 16 * cnt
        q.e.dma_start(out=out, in_=in_).then_inc(sem, 16)
        pool[i][1] = cnt + 1
        self._done((sem, 16 * (cnt + 1)), R, W)

    def barrier(self):
        toks = [(e.sem, e.n) for e in (self.pe, self.act, self.dve, self.pool) if e.n > 0]
        for q in ("sp", "pool"):
            for sem, cnt in self.dsems[q]:
                if cnt > 0:
                    toks.append((sem, 16 * cnt))
        for eng in (self.pe, self.act, self.dve, self.sp, self.pool):
            for sem, v in toks:
                k = self.sid(sem)
                if eng.seen.get(k, 0) < v:
                    eng.e.wait_ge(sem, v)
                    eng.seen[k] = v

    def finish(self):
        self.barrier()


def build_program(stage=99, skip_ffn=False):
    _uc = [0]

    def UN(nm):
        _uc[0] += 1
        return f"sb{_uc[0]}_{nm}"

    nc = bass.Bass("TRN2", target_bir_lowering=False)
    I = lambda nm, shp: nc.dram_tensor(nm, list(shp), F32, kind="ExternalInput").ap()
    O = lambda nm, shp: nc.dram_tensor(nm, list(shp), F32, kind="ExternalOutput").ap()
    xT_d = I("xT", [128, NKC, T])
    vec_d = I("vec", [128, NV])
    gml_d = I("gml", [128, DEPTH, 512])
    cf_d = I("cf", [128, 512])
    wgu_d = I("wgu", [DEPTH, 2, NJP, 128, NKC, 512])
    wdn_d = I("wdn", [DEPTH, 2, NJP, 128, 2, 1024])
    win_d = I("win", [DEPTH, H, 128, NKC, 512])
    wgt_d = I("wgt", [DEPTH, 128, NKC, 8])
    wlr_d = I("wlr", [DEPTH, 2, 128, NKC, 512])
    wrg_d = I("wrg", [DEPTH, 128, 2, 4, 128])
    wout_d = I("wout", [DEPTH, 2, 128, NKC, 512])
    sCT_d = I("sCT", [DEPTH, H, 128, NSQ, 128])
    sC_d = I("sC", [DEPTH, H, 128, NSQ, 128])
    snT_d = I("snT", [DEPTH, H, 128, NSQ])
    sn_d = I("sn", [DEPTH, H, NSQ, 128])
    sm_d = I("sm", [DEPTH, H, NSQ])
    sh_d = I("sh", [DEPTH, 128, 4, NSQ])
    scv_d = I("scv", [DEPTH, 128, 4, 3, NSQ])
    yT_d = O("yT", [128, NKC, T])
    pCT_d = O("pCT", [DEPTH, H, 128, 129])
    pm_d = O("pm", [DEPTH, H, 1])
    ph_d = O("ph", [DEPTH, 128, 4])
    pcv_d = O("pcv", [DEPTH, 128, 4, 3])
    oC_d = O("oC", [DEPTH, H, 128, NSQ, 128])
    on_d = O("on", [DEPTH, H, NSQ, 128])
    om_d = O("om", [DEPTH, H, NSQ])
    oh_d = O("oh", [DEPTH, 128, 4, NSQ])
    ocv_d = O("ocv", [DEPTH, 128, 4, 3, NSQ])
    dbg_d = nc.dram_tensor("dbg", [128, NKC, T], BF16, kind="ExternalOutput").ap() if os.environ.get('KDBGOUT') else None

    with ExitStack() as es:
        K = Ctx(nc, es)
        pe, act, dve, pool, sp = K.pe, K.act, K.dve, K.pool, K.sp
        SB = lambda nm, shp, dt=F32: es.enter_context(nc.sbuf_tensor(UN(nm), list(shp), dt))

        x = SB("x", [128, NKC, T])
        xr_ = RL(NKC, 5)
        xn = SB("xn", [128, NKC, T], BF16)
        xnr = RL(NKC, 5)
        vec = SB("vec", [128, NV])
        vecr = Res()
        cf = SB("cf", [128, 512])
        cfr = Res()
        cb = SB("cb", [128, 512], BF16)
        cbr = Res()
        NR8 = 2
        R8 = [SB(f"r8_{i}", [128, NKC, 512], BF16) for i in range(NR8)]
        R8r = [Res() for _ in range(NR8)]
        w_items = []
        for l_ in range(DEPTH):
            if stage >= 1 and not skip_ffn:
                w_items += [wgu_d[l_, 0, jp] for jp in range(NJP)]
            if stage >= 3:
                w_items += [win_d[l_, h_] for h_ in range(H)]
            if stage >= 4:
                w_items += [wlr_d[l_, i_] for i_ in range(2)]
            if stage >= 5:
                w_items += [wout_d[l_, i_] for i_ in range(2)]
            if stage >= 6 and not skip_ffn:
                w_items += [wgu_d[l_, 1, jp] for jp in range(NJP)]
            if stage < 7:
                break
        wst = {"issued": 0, "consumed": 0, "released": 0}

        def _r8issue(upto):
            while wst["issued"] < min(len(w_items), upto) and wst["issued"] - NR8 < wst["released"]:
                i = wst["issued"]
                K.dma(pool, R8[i % NR8][:], w_items[i], W=[R8r[i % NR8]])
                wst["issued"] += 1

        def r8():
            k = wst["consumed"]
            wst["consumed"] += 1
            _r8issue(k + NR8)
            assert wst["issued"] > k
            return R8[k % NR8], R8r[k % NR8]

        def r8rel():
            wst["released"] += 1
            _r8issue(wst["consumed"] + NR8 - 1)

        ident_f = cf[:, 0:128]
        ident_b = cb[:, 0:128]
        causal_b = cb[:, 128:256]
        smask_b = cb[0:64, 256:320]
        ones_b = cb[:, 320:448]
        ind_f = cf[0:64, 320:336]

        def V(nm, l, j=0, n=1, rows=128):
            o = VOFF[(nm, l)] + j
            return vec[0:rows, o:o + n]

        K.dma(sp, vec[:], vec_d, W=[vecr])
        K.dma(sp, cf[:], cf_d, W=[cfr])
        for kc in range(NKC):
            K.dma(sp, x[:, kc, :], xT_d[:, kc, :], W=xr_[kc])
        K.op(dve, lambda e: e.tensor_copy(out=cb[:, 0:320], in_=cf[:, 0:320]), R=[cfr], W=[cbr])
        K.op(dve, lambda e: e.memset(cb[:, 320:448], 1.0), W=[cbr])

        def rmsnorm(gname, l, out_t, out_r, sq, sqr, rs, rsr, tiles=None):
            for ti, (c0, n) in enumerate(TILES):
                if tiles is not None and ti not in tiles:
                    continue
                K.op(act, lambda e: e.activation(out=sq[:, :, 0:n], in_=x[:, :, c0:c0 + n], func=AF.Square),
                     R=[xr_[kc][ti] for kc in range(NKC)], W=[sqr])
                pt, pr = K.psn()
                K.mm(pt[:, 0:n], [(ones_b, sq[:, kc, 0:n]) for kc in range(NKC)], R=[sqr, cbr], W=[pr])
                b = ti % 2
                K.op(act, lambda e: e.activation(out=rs[b][:, 0:n], in_=pt[:, 0:n], func=AF.Ln,
                                                 scale=1.0 / D, bias=epsc[:, 0:1]), R=[pr, cbr], W=[rsr[b]])
                K.op(act, lambda e: e.activation(out=rs[b][:, 0:n], in_=rs[b][:, 0:n], func=AF.Exp, scale=-0.5),
                     R=[rsr[b]], W=[rsr[b]])
                for kc in range(NKC):
                    K.op(dve, lambda e: e.scalar_tensor_tensor(
                        out=out_t[:, kc, c0:c0 + n], in0=x[:, kc, c0:c0 + n], scalar=V(gname, l, kc),
                        in1=rs[b][:, 0:n], op0=ALU.mult, op1=ALU.mult),
                        R=[xr_[kc][ti], rsr[b], vecr], W=[out_r[kc][ti]])

        epsc = SB("epsc", [128, 1])
        K.op(dve, lambda e: e.memset(epsc[:], EPS), W=[cbr])

        def ffn(l, f, pre_norm=True, post=None):
            gname = "g_ff1" if f == 0 else "g_ff2"
            with ExitStack() as fs:
                S = lambda nm, shp, dt=F32: fs.enter_context(nc.sbuf_tensor(UN(nm), list(shp), dt))
                sq = S("f_sq", [128, NKC, 512], BF16)
                sqr = Res()
                rs = [S("f_rs0", [128, 512]), S("f_rs1", [128, 512])]
                rsr = [Res(), Res()]
                hg = S("f_h", [128, 6, T], BF16)
                hgr = RL(6, 5)
                sg = [S("f_sg0", [128, 512]), S("f_sg1", [128, 512])]
                sgr = [Res(), Res()]
                R4 = [S(f"f_r4_{i}", [128, 2, 1024], BF16) for i in range(3)]
                R4r = [Res() for _ in range(3)]
                r4p = [0]

                def r4():
                    i = r4p[0]
                    r4p[0] = (i + 1) % 3
                    return R4[i], R4r[i]
                if pre_norm:
                    rmsnorm(gname, l, xn, xnr, sq, sqr, rs, rsr)
                cnt = 0
                for g, pairs in enumerate(GROUPS):
                    wds = []
                    for pi, jp in enumerate(pairs):
                        wt, wr = r8()
                        dt_, dr = r4()
                        K.dma(pool, dt_[:], wdn_d[l, f, jp], W=[dr])
                        wds.append((dt_, dr))
                        for jj in range(2):
                            jl = 2 * pi + jj
                            for ti, (c0, n) in enumerate(TILES):
                                pg, pgr = K.psn()
                                pu, pur = K.psn()
                                rr = [xnr[kc][ti] for kc in range(NKC)] + [wr]
                                K.mm(pg[:, 0:n], [(wt[:, kc, jj * 128:(jj + 1) * 128], xn[:, kc, c0:c0 + n])
                                                  for kc in range(NKC)], R=rr, W=[pgr])
                                K.mm(pu[:, 0:n], [(wt[:, kc, 256 + jj * 128:256 + (jj + 1) * 128], xn[:, kc, c0:c0 + n])
                                                  for kc in range(NKC)], R=rr, W=[pur])
                                b = cnt % 2
                                cnt += 1
                                K.op(act, lambda e: e.activation(out=sg[b][:, 0:n], in_=pg[:, 0:n], func=AF.Silu),
                                     R=[pgr], W=[sgr[b]])
                                K.op(dve, lambda e: e.tensor_tensor(out=hg[:, jl, c0:c0 + n], in0=sg[b][:, 0:n],
                                                                    in1=pu[:, 0:n], op=ALU.mult),
                                     R=[sgr[b], pur], W=[hgr[jl][ti]])
                        r8rel()
                    nj = 2 * len(pairs)
                    for ti, (c0, n) in enumerate(TILES):
                        for m in range(NKC):
                            pt, pr = K.psn()
                            K.mm(pt[:, 0:n], [(wds[jl // 2][0][:, jl % 2, m * 128:(m + 1) * 128], hg[:, jl, c0:c0 + n])
                                              for jl in range(nj)],
                                 R=[hgr[jl][ti] for jl in range(nj)] + [w[1] for w in wds], W=[pr])
                            K.op(dve, lambda e: e.scalar_tensor_tensor(
                                out=x[:, m, c0:c0 + n], in0=pt[:, 0:n], scalar=0.5, in1=x[:, m, c0:c0 + n],
                                op0=ALU.mult, op1=ALU.add), R=[pr, xr_[m][ti]], W=[xr_[m][ti]])
                        if post is not None and g == len(GROUPS) - 1 and ti >= 1:
                            post(sq, sqr, rs, rsr, ti - 1)
                if post is not None:
                    post(sq, sqr, rs, rsr, len(TILES) - 1)
            K.barrier()

        def mixer(l, pre_norm=True, post_norm=False):
            with ExitStack() as ms:
                S = lambda nm, shp, dt=F32: ms.enter_context(nc.sbuf_tensor(UN(nm), list(shp), dt))
                mixin = S("m_mixin", [128, NKC, T], BF16)
                mixr = RL(NKC, 5)
                colr = Res()
                cols = S("m_cols", [128, NCH, 12])
                dcp = S("m_dcp", [128, H, 17])
                dcs = S("m_dcs", [128, H, NSQ])
                dcsT = S("m_dcsT", [NSQ, H])
                smallr = Res()
                with ExitStack() as rs_:
                    S2 = lambda nm, shp, dt=F32: rs_.enter_context(nc.sbuf_tensor(UN(nm), list(shp), dt))
                    with ExitStack() as ns_:
                        S3 = lambda nm, shp, dt=F32: ns_.enter_context(nc.sbuf_tensor(UN(nm), list(shp), dt))
                        sq = S3("r_sq", [128, NKC, 512], BF16)
                        sqr = Res()
                        rs = [S3("r_rs0", [128, 512]), S3("r_rs1", [128, 512])]
                        rsr = [Res(), Res()]
                        if pre_norm:
                            rmsnorm("g_mix", l, xn, xnr, sq, sqr, rs, rsr)
                    if pre_norm:
                        K.barrier()
                    wg = S2("r_wg", [128, NKC, 8], BF16)
                    wgr = Res()
                    K.dma(pool, wg[:], wgt_d[l], W=[wgr])
                    R1 = S2("r_R1", [H, T])
                    R2 = S2("r_R2", [H, T])
                    R3 = S2("r_R3", [H, T])
                    r1, r2, r3 = Res(), Res(), Res()
                    nbf = S2("r_nbf", [H, 1])
                    sm = S2("r_sm", [H, 24])
                    mref = S2("r_mref", [H, 18])
                    m0 = S2("r_m0", [H, NSQ])
                    mnx = S2("r_mnx", [H, NSQ])
                    dcr = S2("r_dcr", [H, 17 + NSQ])
                    dce = S2("r_dce", [H, H, 17 + NSQ])
                    K.dma(sp, m0[:], sm_d[l], W=[smallr])
                    K.op(act, lambda e: e.mul(out=nbf[:], in_=V("b_f", l, rows=H), mul=-1.0), R=[vecr], W=[smallr])
                    for ti, (c0, n) in enumerate(TILES):
                        pi_, pir = K.psn()
                        pf_, pfr = K.psn()
                        rr = [xnr[kc][ti] for kc in range(NKC)] + [wgr]
                        K.mm(pi_[0:H, 0:n], [(wg[:, kc, 0:4], xn[:, kc, c0:c0 + n]) for kc in range(NKC)], R=rr, W=[pir])
                        K.mm(pf_[0:H, 0:n], [(wg[:, kc, 4:8], xn[:, kc, c0:c0 + n]) for kc in range(NKC)], R=rr, W=[pfr])
                        K.op(act, lambda e: e.activation(out=R1[:, c0:c0 + n], in_=pi_[0:H, 0:n], func=AF.Identity,
                                                         bias=V("b_i", l, rows=H), scale=1.0), R=[pir, vecr], W=[r1])
                        K.op(act, lambda e: e.activation(out=R2[:, c0:c0 + n], in_=pf_[0:H, 0:n], func=AF.Exp,
                                                         bias=nbf[:], scale=-1.0), R=[pfr, smallr], W=[r2])
                    K.op(act, lambda e: e.activation(out=R2[:], in_=R2[:], func=AF.Ln, bias=1.0, scale=1.0), R=[r2], W=[r2])
                    K.op(dve, lambda e: e.tensor_tensor_scan(out=R3[:, 0:TP], data0=R2[:, 0:TP], data1=R2[:, 0:TP],
                                                             initial=0.0, op0=ALU.add, op1=ALU.max), R=[r2], W=[r3])
                    K.op(dve, lambda e: e.tensor_copy(out=R3[:, TP:TP + 16], in_=R2[:, TP:TP + 16]), R=[r2], W=[r3])
                    for t in range(1, NST):
                        K.op(dve, lambda e: e.tensor_tensor(out=R3[:, TP + 16 * t:TP + 16 * t + 16],
                                                            in0=R3[:, TP + 16 * (t - 1):TP + 16 * t],
                                                            in1=R2[:, TP + 16 * t:TP + 16 * t + 16], op=ALU.add),
                             R=[r2, r3], W=[r3])
                    K.op(dve, lambda e: e.tensor_tensor(out=R1[:], in0=R1[:], in1=R3[:], op=ALU.add), R=[r1, r3], W=[r1])
                    K.op(dve, lambda e: e.tensor_tensor_scan(out=R2[:, 0:TP], data0=R1[:, 0:TP], data1=R1[:, 0:TP],
                                                             initial=0.0, op0=ALU.max, op1=ALU.max), R=[r1, r2], W=[r2])
                    K.op(dve, lambda e: e.tensor_tensor(out=R2[:, TP:TP + 16], in0=R1[:, TP:TP + 16], in1=m0[:],
                                                        op=ALU.max), R=[r1, smallr, r2], W=[r2])
                    for t in range(1, NST):
                        K.op(dve, lambda e: e.tensor_tensor(out=R2[:, TP + 16 * t:TP + 16 * t + 16],
                                                            in0=R2[:, TP + 16 * (t - 1):TP + 16 * t],
                                                            in1=R1[:, TP + 16 * t:TP + 16 * t + 16], op=ALU.max),
                             R=[r1, r2], W=[r2])
                    K.op(dve, lambda e: e.memset(mref[:, 0:1], 0.0), W=[smallr])
                    K.op(dve, lambda e: e.tensor_copy(
                        out=mref[:, 1:17], in_=R2[:, 0:2048].rearrange("p (c t) -> p c t", t=128)[:, :, 127]),
                        R=[r2, smallr], W=[smallr])
                    K.op(dve, lambda e: e.tensor_copy(out=mref[:, 17:18], in_=R2[:, TP - 1:TP]), R=[r2, smallr], W=[smallr])
                    K.op(dve, lambda e: e.tensor_copy(out=mnx[:], in_=R2[:, TP + 48:TP + 64]), R=[r2, smallr], W=[smallr])
                    K.op(dve, lambda e: e.tensor_tensor(out=sm[:, 0:1], in0=R2[:, TP - 1:TP], in1=R3[:, TP - 1:TP],
                                                        op=ALU.subtract), R=[r2, r3, smallr], W=[smallr])
                    K.op(dve, lambda e: e.tensor_tensor(out=sm[:, 1:17], in0=R2[:, TP + 48:TP + 64],
                                                        in1=R3[:, TP + 48:TP + 64], op=ALU.subtract),
                         R=[r2, r3, smallr], W=[smallr])
                    K.dma(sp, pm_d[l], sm[:, 0:1], R=[smallr])
                    K.dma(sp, om_d[l], sm[:, 1:17], R=[smallr])
                    K.op(dve, lambda e: e.tensor_tensor(out=dcr[:, 0:17], in0=mref[:, 0:17], in1=mref[:, 1:18],
                                                        op=ALU.subtract), R=[smallr], W=[smallr])
                    K.op(dve, lambda e: e.tensor_tensor(out=dcr[:, 17:33], in0=m0[:], in1=mnx[:], op=ALU.subtract),
                         R=[smallr], W=[smallr])
                    K.op(act, lambda e: e.activation(out=dcr[:], in_=dcr[:], func=AF.Exp), R=[smallr], W=[smallr])
                    pv = lambda Rt: Rt[:, 0:2048].rearrange("p (c t) -> p c t", t=128)
                    sv = lambda Rt: Rt[:, TP:T].rearrange("p (t b) -> p t b", b=NSQ)
                    bc = lambda ap_, n: ap_.unsqueeze(2).to_broadcast([H, ap_.shape[1], n])
                    bs = lambda ap_: ap_.unsqueeze(1).to_broadcast([H, NST, NSQ])
                    K.op(dve, lambda e: e.tensor_tensor(out=pv(R2), in0=pv(R1), in1=bc(mref[:, 0:16], 128), op=ALU.subtract),
                         R=[r1, smallr, r2], W=[r2])
                    K.op(dve, lambda e: e.tensor_scalar(out=R2[:, 2048:TP], in0=R1[:, 2048:TP], scalar1=mref[:, 16:17],
                                                        scalar2=None, op0=ALU.subtract), R=[r1, smallr, r2], W=[r2])
                    K.op(dve, lambda e: e.tensor_tensor(out=sv(R2), in0=sv(R1), in1=bs(m0[:]), op=ALU.subtract),
                         R=[r1, smallr, r2], W=[r2])
                    K.op(act, lambda e: e.activation(out=R2[:], in_=R2[:], func=AF.Exp), R=[r2], W=[r2])
                    K.op(dve, lambda e: e.tensor_tensor(out=pv(R3), in0=pv(R3), in1=bc(mref[:, 0:16], 128), op=ALU.subtract),
                         R=[r3, smallr], W=[r3])
                    K.op(dve, lambda e: e.tensor_scalar(out=R3[:, 2048:TP], in0=R3[:, 2048:TP], scalar1=mref[:, 16:17],
                                                        scalar2=None, op0=ALU.subtract), R=[r3, smallr], W=[r3])
                    K.op(dve, lambda e: e.tensor_tensor(out=sv(R3), in0=sv(R3), in1=bs(m0[:]), op=ALU.subtract),
                         R=[r3, smallr], W=[r3])
                    K.op(act, lambda e: e.activation(out=R3[:], in_=R3[:], func=AF.Exp, scale=2.0), R=[r3], W=[r3])
                    K.op(dve, lambda e: e.tensor_tensor(out=pv(R1), in0=pv(R1), in1=bc(mref[:, 1:17], 128), op=ALU.subtract),
                         R=[r1, smallr], W=[r1])
                    K.op(dve, lambda e: e.tensor_scalar(out=R1[:, 2048:TP], in0=R1[:, 2048:TP], scalar1=mref[:, 17:18],
                                                        scalar2=None, op0=ALU.subtract), R=[r1, smallr], W=[r1])
                    K.op(dve, lambda e: e.tensor_tensor(out=sv(R1), in0=sv(R1), in1=bs(mnx[:]), op=ALU.subtract),
                         R=[r1, smallr], W=[r1])
                    K.op(act, lambda e: e.activation(out=R1[:], in_=R1[:], func=AF.Exp), R=[r1], W=[r1])
                    for ci, (c0, n) in enumerate(CHUNKS):
                        pt, pr = K.psn()
                        for qi, (Rt, rr) in enumerate(((R2, r2), (R1, r1), (R3, r3))):
                            K.tr(pt[0:n, 4 * qi:4 * qi + 4], Rt[:, c0:c0 + n], ident_f[0:H, 0:H], R=[rr, cfr], W=[pr])
                        K.op(act, lambda e: e.activation(out=cols[0:n, ci, :], in_=pt[0:n, 0:12], func=AF.Copy),
                             R=[pr], W=[colr])
                    for hh in range(H):
                        K.op(dve, lambda e: e.tensor_scalar(out=dce[:, hh, :], in0=dcr[:], scalar1=ident_f[0:H, hh:hh + 1],
                                                            scalar2=None, op0=ALU.mult), R=[smallr, cfr], W=[smallr])
                    pt, pr = K.psn()
                    K.mm(pt[:, 0:H * 33], [(ones_f4[:], dce[:].rearrange("p h j -> p (h j)"))], R=[smallr, cbr], W=[pr])
                    ptv = pt[:, 0:H * 33].rearrange("p (h j) -> p h j", j=33)
                    K.op(act, lambda e: e.activation(out=dcp[:], in_=ptv[:, :, 0:17], func=AF.Copy), R=[pr], W=[colr])
                    K.op(act, lambda e: e.activation(out=dcs[:], in_=ptv[:, :, 17:33], func=AF.Copy), R=[pr], W=[colr])
                    pt2, pr2 = K.psn()
                    K.tr(pt2[0:NSQ, 0:H], dcr[:, 17:33], ident_f[0:H, 0:H], R=[smallr, cfr], W=[pr2])
                    K.op(act, lambda e: e.activation(out=dcsT[:], in_=pt2[0:NSQ, 0:H], func=AF.Copy), R=[pr2], W=[colr])
                K.barrier()
                if stage < 3:
                    return
                with ExitStack() as hs:
                    S2 = lambda nm, shp, dt=F32: hs.enter_context(nc.sbuf_tensor(UN(nm), list(shp), dt))
                    NB = 5
                    gml = S2("h_gml", [128, 512])
                    gmlr = Res()
                    K.dma(sp, gml[:], gml_d[:, l, :], W=[gmlr])
                    qk = [S2(f"h_qk{i}", [128, 2, 512], BF16) for i in range(3)]
                    qkr = [Res(), Res(), Res()]
                    ktok = [S2(f"h_kt{i}", [128, 128], BF16) for i in range(NB)]
                    v1 = [S2(f"h_v1{i}", [128, 129], BF16) for i in range(NB)]
                    vwx = [S2(f"h_vw{i}", [128, 129], BF16) for i in range(NB)]
                    sgo = [S2(f"h_so{i}", [128, 128]) for i in range(NB)]
                    eo = [S2(f"h_eo{i}", [128, 128]) for i in range(2)]
                    eor = [Res(), Res()]
                    stw = [S2(f"h_sw{i}", [128, 128], BF16) for i in range(NB)]
                    tokr = [RL(5) for _ in range(NB)]
                    hgt = [S2(f"h_hg{i}", [128, 128]) for i in range(2)]
                    hmt = [S2(f"h_hm{i}", [128, 128], BF16) for i in range(2)]
                    junk = [S2(f"h_jk{i}", [128, 128], BF16) for i in range(2)]
                    pcol = [S2(f"h_pc{i}", [128, 8]) for i in range(3)]
                    pcr = [Res() for _ in range(3)]
                    postr = [RL(4), RL(4)]
                    CTn = S2("h_CTn", [128, 129])
                    CTb = S2("h_CTb", [128, 129], BF16)
                    ctr, ctbr = Res(), Res()
                    C0 = S2("h_C0", [128, NSQ, 128])
                    c0r = RL(NSQ)
                    CT0 = S2("h_CT0", [128, NSQ, 130], BF16)
                    ct0r = Res()
                    n0T = S2("h_n0T", [128, NSQ])
                    n0 = S2("h_n0", [NSQ, 128])
                    n0r = Res()
                    qd = S2("h_qd", [128, 16 * 65], BF16)
                    qdr = Res()
                    vwb = [S2(f"h_vwb{i}", [64, 128], BF16) for i in range(NSQ)]
                    vwbr = [Res() for _ in range(NSQ)]
                    ewb = S2("h_ewb", [64, NSQ], BF16)
                    ewbr = Res()
                    for i in range(NB):
                        K.op(dve, lambda e: e.memset(v1[i][:, 128:129], 1.0), W=[tokr[i][1]])
                    K.op(dve, lambda e: e.memset(qd[:], 0.0), W=[qdr])
                    qd_rows = qd[:, 0:1024].rearrange("p (b t) -> p b t", t=64)
                    qd_diag = qd[:, 0:1040].rearrange("p (b u) -> p b u", u=65)[:, :, 0:64:16]
                    rot = {"a": 0, "n": 0}

                    def bank_a():
                        i = rot["a"]
                        rot["a"] = (i + 1) % 2
                        return K.ps[i], K.psr[i]

                    def bank_n():
                        i = 3 + rot["n"]
                        rot["n"] = (rot["n"] + 1) % 3
                        return K.ps[i], K.psr[i]

                    def load_states(h):
                        K.dma(pool, CT0[:, :, 0:128], sCT_d[l, h], W=[ct0r])
                        K.dma(sp, C0[:], sC_d[l, h], W=c0r)
                        K.dma(sp, n0T[:], snT_d[l, h], W=[n0r])
                        K.dma(sp, n0[:], sn_d[l, h], W=[n0r])
                        K.op(dve, lambda e: e.tensor_copy(out=CT0[:, :, 128], in_=n0T[:]), R=[n0r, ct0r], W=[ct0r])

                    its = [(h, ci) for h in range(H) for ci in range(NCH)]
                    ctx = {}
                    wts = {}

                    def stageA(i):
                        h, ci = its[i]
                        c0, n = CHUNKS[ci]
                        issample = (ci == NCH - 1)
                        if ci == 0:
                            wts[h] = r8()
                        wt, wr = wts[h]
                        ti = min(c0 // 512, 4)
                        tc0, tn = TILES[ti]
                        qb = ti % 3

                        def qk_group(tj, part):
                            jc0, jn = TILES[tj]
                            if jn > 256:
                                lo_, hi_ = (0, 256) if part % 2 == 0 else (256, jn)
                            else:
                                if part % 2 == 1:
                                    return
                                lo_, hi_ = 0, jn
                            isk = part >= 2
                            wc = 128 if isk else 0
                            jb = tj % 3
                            pq, pqr = bank_a()
                            K.mm(pq[:, 0:hi_ - lo_], [(wt[:, kc, wc:wc + 128], xn[:, kc, jc0 + lo_:jc0 + hi_]) for kc in range(NKC)],
                                 R=[xnr[kc][tj] for kc in range(NKC)] + [wr], W=[pqr])
                            if isk:
                                K.op(dve, lambda e: e.tensor_scalar(out=qk[jb][:, 1, lo_:hi_], in0=pq[:, 0:hi_ - lo_], scalar1=KSCALE,
                                                                    scalar2=None, op0=ALU.mult), R=[pqr], W=[qkr[jb]])
                            else:
                                K.op(act, lambda e: e.activation(out=qk[jb][:, 0, lo_:hi_], in_=pq[:, 0:hi_ - lo_], func=AF.Copy),
                                     R=[pqr], W=[qkr[jb]])

                        if ci == 0:
                            for part in range(4):
                                qk_group(0, part)
                        if ci < 16:
                            qk_group(ti + 1, ci % 4)
                        lo = c0 - tc0
                        qT = qk[qb][:, 0, lo:lo + n]
                        kT = qk[qb][:, 1, lo:lo + n]
                        b = i % NB
                        tr_ = tokr[b]
                        ea = cols[0:n, ci, h:h + 1]
                        ew = cols[0:n, ci, 4 + h:5 + h]
                        fl = cols[0:n, ci, 8 + h:9 + h]
                        ctx[i] = dict(h=h, ci=ci, c0=c0, n=n, issample=issample, ti=ti, qb=qb, qT=qT, kT=kT, b=b, tr_=tr_,
                                      ea=ea, ew=ew, fl=fl)
                        pt, pr = bank_a()
                        K.mm(pt[0:n, 0:384], [(xn[:, kc, c0:c0 + n], wt[:, kc, 128:512]) for kc in range(NKC)],
                             R=[xnr[kc][ti] for kc in range(NKC)] + [wr], W=[pr])
                        K.op(act, lambda e: e.mul(out=ktok[b][0:n, :], in_=pt[0:n, 0:128], mul=KSCALE), R=[pr], W=[tr_[0]])
                        K.op(act, lambda e: e.activation(out=v1[b][0:n, 0:128], in_=pt[0:n, 128:256], func=AF.Copy), R=[pr], W=[tr_[1]])
                        eb = i % 2
                        K.op(act, lambda e: e.activation(out=eo[eb][0:n, :], in_=pt[0:n, 256:384], func=AF.Exp, scale=-1.0), R=[pr], W=[eor[eb]])
                        K.op(act, lambda e: e.activation(out=eo[eb][0:n, :], in_=eo[eb][0:n, :], func=AF.Ln, bias=1.0, scale=1.0),
                             R=[eor[eb]], W=[eor[eb]])
                        K.op(act, lambda e: e.activation(out=sgo[b][0:n, :], in_=eo[eb][0:n, :], func=AF.Exp, scale=-1.0),
                             R=[eor[eb]], W=[tr_[3]])
                        K.op(dve, lambda e: e.tensor_scalar(out=vwx[b][0:n, 0:128], in0=v1[b][0:n, 0:128], scalar1=ew, scalar2=None,
                                                            op0=ALU.mult), R=[tr_[1], colr], W=[tr_[2]])
                        K.op(act, lambda e: e.activation(out=vwx[b][0:n, 128:129], in_=ew, func=AF.Copy), R=[colr], W=[tr_[2]])
                        ps_, psr_ = K.ps[2], K.psr[2]
                        K.mm(ps_[0:n, 0:n], [(kT, qT)], R=[qkr[qb]], W=[psr_])
                        msk = smask_b if issample else causal_b[0:n, 0:n]
                        if n < 128:
                            K.op(dve, lambda e: e.memset(stw[b][:, 0:n], 0.0), W=[tr_[4]])
                        K.op(dve, lambda e: e.scalar_tensor_tensor(out=stw[b][0:n, 0:n], in0=ps_[0:n, 0:n], scalar=ea,
                                                                   in1=msk, op0=ALU.mult, op1=ALU.mult),
                             R=[psr_, colr, cbr], W=[tr_[4]])
                        if issample:
                            K.op(dve, lambda e: e.tensor_copy(out=qd_diag, in_=qT.rearrange("p (t b) -> p b t", b=NSQ)),
                                 R=[qkr[qb], qdr], W=[qdr])
                        if ci == NCH - 1:
                            r8rel()

                    def prompt_state(c):
                        h, ci, n, b, tr_ = c["h"], c["ci"], c["n"], c["b"], c["tr_"]
                        pu_, pur = K.ps[6], K.psr[6]
                        K.mm(pu_[:, 0:129], [(ktok[b][0:n, :], vwx[b][0:n, :])], R=[tr_[0], tr_[2]], W=[pur])
                        if ci == 0:
                            K.op(dve, lambda e: e.tensor_copy(out=CTn[:], in_=pu_[:, 0:129]), R=[pur, ctr], W=[ctr])
                        else:
                            K.op(dve, lambda e: e.scalar_tensor_tensor(out=CTn[:], in0=CTn[:], scalar=dcp[:, h, ci:ci + 1],
                                                                       in1=pu_[:, 0:129], op0=ALU.mult, op1=ALU.add),
                                 R=[pur, ctr, colr], W=[ctr])
                        if ci == NCH - 2:
                            K.dma(sp, pCT_d[l, h], CTn[:], R=[ctr])
                        else:
                            K.op(act, lambda e: e.activation(out=CTb[:], in_=CTn[:], func=AF.Copy), R=[ctr, ctbr], W=[ctbr])

                    def sample_state(c):
                        h, n, b, tr_, ew = c["h"], c["n"], c["b"], c["tr_"], c["ew"]
                        K.op(dve, lambda e: e.tensor_scalar(out=ewb[:], in0=ind_f, scalar1=ew, scalar2=None,
                                                            op0=ALU.mult), R=[cfr, colr], W=[ewbr])
                        pu_, pur = K.ps[6], K.psr[6]
                        K.mm(pu_[0:NSQ, 0:128], [(ewb[:], ktok[b][0:n, :])], R=[ewbr, tr_[0]], W=[pur])
                        K.op(dve, lambda e: e.scalar_tensor_tensor(out=n0[:], in0=n0[:], scalar=dcsT[:, h:h + 1],
                                                                   in1=pu_[0:NSQ, 0:128], op0=ALU.mult, op1=ALU.add),
                             R=[n0r, colr, pur], W=[n0r])
                        K.dma(sp, on_d[l, h], n0[:], R=[n0r])
                        for bb in range(NSQ):
                            K.op(dve, lambda e: e.tensor_scalar(out=vwb[bb][:], in0=vwx[b][0:n, 0:128],
                                                                scalar1=ind_f[:, bb:bb + 1], scalar2=None, op0=ALU.mult),
                                 R=[tr_[2], cfr], W=[vwbr[bb]])
                        bk = lambda bb: (K.ps[6], K.psr[6]) if bb % 2 == 0 else (K.ps[7], K.psr[7])

                        def mmb(bb):
                            pc_, pcr_ = bk(bb)
                            K.mm(pc_[:, 0:128], [(vwb[bb][:], ktok[b][0:n, :])], R=[vwbr[bb], tr_[0]], W=[pcr_])

                        mmb(0)
                        mmb(1)
                        for bb in range(NSQ):
                            pc_, pcr_ = bk(bb)
                            K.op(dve, lambda e: e.scalar_tensor_tensor(out=C0[:, bb, :], in0=C0[:, bb, :],
                                                                       scalar=dcs[:, h, bb:bb + 1], in1=pc_[:, 0:128],
                                                                       op0=ALU.mult, op1=ALU.add),
                                 R=[c0r[bb], colr, pcr_], W=[c0r[bb]])
                            if bb + 2 < NSQ:
                                mmb(bb + 2)
                        K.dma(sp, oC_d[l, h], C0[:], R=c0r)

                    def stageB(i):
                        c = ctx[i]
                        h, ci, n, b, tr_, qT, qb = c["h"], c["ci"], c["n"], c["b"], c["tr_"], c["qT"], c["qb"]
                        pn_, pnr = bank_n()
                        c["pn"] = (pn_, pnr)
                        if c["issample"]:
                            grp = [(pn_[0:n, 0:129], stw[b][:, 0:n], v1[b][:, :], True, False)]
                            for bb in range(NSQ):
                                grp.append((pn_[0:n, 0:129], qd_rows[:, bb, :], CT0[:, bb, 0:129], False, bb == NSQ - 1))
                            K.mm_multi(grp, R=[tr_[4], tr_[1], qdr, ct0r], W=[pnr])
                        elif ci == 0:
                            K.mm(pn_[0:n, 0:129], [(stw[b][:, 0:n], v1[b][:, :])], R=[tr_[4], tr_[1]], W=[pnr])
                        else:
                            K.mm(pn_[0:n, 0:129], [(stw[b][:, 0:n], v1[b][:, :]), (qT, CTb[:])],
                                 R=[tr_[4], tr_[1], qkr[qb], ctbr], W=[pnr])
                        pb3 = i % 3
                        c["pb3"] = pb3
                        jb = i % 2
                        K.op(act, lambda e: e.activation(out=junk[jb][0:n, :], in_=pn_[0:n, 0:128], func=AF.Square,
                                                         accum_out=pcol[pb3][0:n, 2:3]),
                             R=[pnr, postr[jb][2]], W=[pcr[pb3], postr[jb][2]])
                        K.op(act, lambda e: e.activation(out=pcol[pb3][0:n, 0:1], in_=pn_[0:n, 128:129], func=AF.Square),
                             R=[pnr, pcr[pb3]], W=[pcr[pb3]])
                        if not c["issample"]:
                            prompt_state(c)

                    def stageC1(i):
                        c = ctx[i]
                        n, fl = c["n"], c["fl"]
                        pn_, pnr = c["pn"]
                        pc = pcol[c["pb3"]]
                        r_ = pcr[c["pb3"]]
                        K.op(dve, lambda e: e.tensor_tensor(out=pc[0:n, 0:1], in0=pc[0:n, 0:1], in1=fl, op=ALU.max),
                             R=[colr, r_], W=[r_])

                    def stageC1_2(i):
                        c = ctx[i]
                        n = c["n"]
                        pc = pcol[c["pb3"]]
                        r_ = pcr[c["pb3"]]
                        K.op(dve, lambda e: e.scalar_tensor_tensor(out=pc[0:n, 3:4], in0=pc[0:n, 0:1], scalar=DH * EPS, in1=pc[0:n, 2:3],
                                                                   op0=ALU.mult, op1=ALU.add), R=[r_], W=[r_])
                        K.op(act, lambda e: e.activation(out=pc[0:n, 4:5], in_=pc[0:n, 3:4], func=AF.Ln, scale=1.0 / DH), R=[r_], W=[r_])
                        K.op(act, lambda e: e.activation(out=pc[0:n, 5:6], in_=pc[0:n, 4:5], func=AF.Exp, scale=-0.5), R=[r_], W=[r_])

                    def stageC2a(i):
                        c = ctx[i]
                        h, ci, c0, n, b, tr_, ti = c["h"], c["ci"], c["c0"], c["n"], c["b"], c["tr_"], c["ti"]
                        pn_, pnr = c["pn"]
                        pc = pcol[c["pb3"]]
                        r_ = pcr[c["pb3"]]
                        pb = i % 2
                        po_ = postr[pb]
                        K.op(dve, lambda e: e.scalar_tensor_tensor(out=hgt[pb][0:n, :], in0=pn_[0:n, 0:128], scalar=pc[0:n, 5:6],
                                                                   in1=gml[0:n, h * 128:(h + 1) * 128],
                                                                   op0=ALU.mult, op1=ALU.mult),
                             R=[pnr, r_, gmlr, po_[0]], W=[po_[0]])

                    def stageC2a_2(i):
                        c = ctx[i]
                        n, b, tr_ = c["n"], c["b"], c["tr_"]
                        pb = i % 2
                        po_ = postr[pb]
                        K.op(dve, lambda e: e.tensor_tensor(out=hmt[pb][0:n, :], in0=hgt[pb][0:n, :], in1=sgo[b][0:n, :],
                                                            op=ALU.mult), R=[po_[0], tr_[3], po_[1]], W=[po_[1]])

                    def stageC2b(i):
                        c = ctx.pop(i)
                        h, c0, n, ti = c["h"], c["c0"], c["n"], c["ti"]
                        pb = i % 2
                        po_ = postr[pb]
                        ph_, phr = K.ps[7], K.psr[7]
                        phb = ph_[:].bitcast(BF16)
                        K.tr(phb[:, 0:n], hmt[pb][0:n, :], ident_b[0:n, 0:n], R=[po_[1], cbr], W=[phr])
                        K.op(act, lambda e: e.activation(out=mixin[:, h, c0:c0 + n], in_=phb[:, 0:n], func=AF.Copy),
                             R=[phr], W=[mixr[h][ti]])
                        if c["issample"]:
                            sample_state(c)
                            if h + 1 < H:
                                load_states(h + 1)

                    load_states(0)
                    NI = len(its)
                    for i in range(NI + 5):
                        if 0 <= i - 5 < NI:
                            stageC2b(i - 5)
                        if 0 <= i - 2 < NI:
                            stageB(i - 2)
                        if 0 <= i - 3 < NI:
                            stageC1(i - 3)
                        if 0 <= i - 4 < NI:
                            stageC2a(i - 4)
                        if 0 <= i - 3 < NI:
                            stageC1_2(i - 3)
                        if 0 <= i - 4 < NI:
                            stageC2a_2(i - 4)
                        if i < NI:
                            stageA(i)
                K.barrier()
                if stage < 4:
                    return
                with ExitStack() as ls:
                    S2 = lambda nm, shp, dt=F32: ls.enter_context(nc.sbuf_tensor(UN(nm), list(shp), dt))
                    wrg = S2("l_wrg", [128, 2, 4, 128], BF16)
                    wrgr = Res()
                    K.dma(pool, wrg[:], wrg_d[l], W=[wrgr])
                    wl = [r8(), r8()]
                    sc = S2("l_sc", [128, 4])
                    scr = Res()
                    K.op(act, lambda e: e.activation(out=sc[:], in_=V("lru_lambda", l, 0, 4), func=AF.Exp, scale=-1.0),
                         R=[vecr], W=[scr])
                    K.op(act, lambda e: e.activation(out=sc[:], in_=sc[:], func=AF.Ln, bias=1.0, scale=1.0), R=[scr], W=[scr])
                    K.op(act, lambda e: e.mul(out=sc[:], in_=sc[:], mul=-8.0), R=[scr], W=[scr])
                    sc2 = S2("l_sc2", [128, 4])
                    K.op(act, lambda e: e.mul(out=sc2[:], in_=sc[:], mul=2.0), R=[scr], W=[scr])
                    nbg = S2("l_nbg", [128, 8])
                    K.op(act, lambda e: e.mul(out=nbg[:, 0:4], in_=V("b_rg_a", l, 0, 4), mul=-1.0), R=[vecr, scr], W=[scr])
                    K.op(act, lambda e: e.mul(out=nbg[:, 4:8], in_=V("b_rg_x", l, 0, 4), mul=-1.0), R=[vecr, scr], W=[scr])
                    mk = lambda nm, n_, shp, dt=F32: ([S2(f"{nm}{i}", shp, dt) for i in range(n_)], [Res() for _ in range(n_)])
                    xet, xetr = mk("l_xet", 2, [128, 3 + 512])
                    ygf, ygfr = mk("l_ygf", 2, [128, 512])
                    xc, xcr = mk("l_xc", 3, [128, 512])
                    xcb, xcbr = mk("l_xcb", 2, [128, 512], BF16)
                    tmp, tmpr = mk("l_tmp", 2, [128, 512])
                    glb, glbr = mk("l_glb", 3, [128, 512], BF16)
                    gr, grr = mk("l_gr", 2, [128, 512])
                    gi, gir = mk("l_gi", 2, [128, 512])
                    ts_, tsr = mk("l_ts", 2, [128, 512])
                    hcur, hcurr = mk("l_hc", 2, [128, 512])
                    sqh, sqhr = mk("l_sqh", 2, [128, 512], BF16)
                    car = S2("l_car", [128, 4, 3])
                    carr = RL(4)
                    xes = S2("l_xes", [128, 4, 7, NSQ])
                    xesr = RL(4)
                    hcar = S2("l_hcar", [128, 4])
                    hcr = RL(4)
                    h0s = S2("l_h0s", [128, 4, NSQ])
                    h0r = Res()
                    hs_o = S2("l_hso", [128, 4, NSQ])
                    php = S2("l_php", [128, 4])
                    hsor = Res()
                    rsl = S2("l_rsl", [128, 512])
                    rslr = Res()
                    K.dma(sp, xes[:, :, 0:3, :], scv_d[l], W=xesr)
                    K.dma(sp, h0s[:], sh_d[l], W=[h0r])
                    K.op(dve, lambda e: e.memset(car[:], 0.0), W=carr)
                    items = [(ti, cc) for ti in range(5) for cc in range(4)]
                    NIT = len(items)
                    rotx = {"x": 0, "y": 0}

                    def T1(k):
                        ti, cc = items[k]
                        c0, n = TILES[ti]
                        npr = min(n, TP - c0)
                        b2 = k % 2
                        wt, wr = wl[cc // 2]
                        o_ = (cc % 2) * 256
                        px, pxr = K.ps[rotx["x"]], K.psr[rotx["x"]]
                        rotx["x"] = (rotx["x"] + 1) % 2
                        py, pyr = K.ps[2 + rotx["y"]], K.psr[2 + rotx["y"]]
                        rotx["y"] = (rotx["y"] + 1) % 2
                        rr = [xnr[kc][ti] for kc in range(NKC)] + [wr]
                        K.mm(px[:, 0:n], [(wt[:, kc, o_:o_ + 128], xn[:, kc, c0:c0 + n]) for kc in range(NKC)], R=rr, W=[pxr])
                        K.mm(py[:, 0:n], [(wt[:, kc, o_ + 128:o_ + 256], xn[:, kc, c0:c0 + n]) for kc in range(NKC)], R=rr, W=[pyr])
                        K.op(act, lambda e: e.activation(out=xet[b2][:, 3:3 + npr], in_=px[:, 0:npr], func=AF.Copy),
                             R=[pxr], W=[xetr[b2]])
                        K.op(act, lambda e: e.activation(out=xet[b2][:, 0:3], in_=car[:, cc, :], func=AF.Copy),
                             R=[carr[cc]], W=[xetr[b2]])
                        K.op(act, lambda e: e.activation(out=ygf[b2][:, 0:n], in_=py[:, 0:n], func=AF.Copy), R=[pyr], W=[ygfr[b2]])
                        if ti == 4:
                            K.op(act, lambda e: e.activation(
                                out=xes[:, cc, 3:7, :], in_=px[:, npr:n].rearrange("p (t b) -> p t b", b=NSQ), func=AF.Copy),
                                R=[pxr], W=[xesr[cc]])
                            K.dma(sp, pcv_d[l, :, cc, :], xet[b2][:, npr:npr + 3], R=[xetr[b2]])
                            K.dma(sp, ocv_d[l, :, cc], xes[:, cc, 4:7, :], R=[xesr[cc]])
                        else:
                            K.op(act, lambda e: e.activation(out=car[:, cc, :], in_=xet[b2][:, n:n + 3], func=AF.Copy),
                                 R=[xetr[b2]], W=[carr[cc]])

                    def T2(k):
                        ti, cc = items[k]
                        c0, n = TILES[ti]
                        npr = min(n, TP - c0)
                        b2, b3 = k % 2, k % 3
                        cw = lambda j: V("conv_w", l, j * 4 + cc)
                        K.op(dve, lambda e: e.tensor_scalar(out=xc[b3][:, 0:npr], in0=xet[b2][:, 3:3 + npr], scalar1=cw(3),
                                                            scalar2=V("conv_b", l, cc), op0=ALU.mult, op1=ALU.add),
                             R=[xetr[b2], vecr], W=[xcr[b3]])
                        for j in range(1, 4):
                            K.op(dve, lambda e: e.scalar_tensor_tensor(out=xc[b3][:, 0:npr], in0=xet[b2][:, 3 - j:3 - j + npr],
                                                                       scalar=cw(3 - j), in1=xc[b3][:, 0:npr],
                                                                       op0=ALU.mult, op1=ALU.add),
                                 R=[xetr[b2], vecr, xcr[b3]], W=[xcr[b3]])
                        if ti == 4:
                            xcs = xc[b3][:, npr:n].rearrange("p (t b) -> p t b", b=NSQ)
                            K.op(dve, lambda e: e.tensor_scalar(out=xcs, in0=xes[:, cc, 3:7, :], scalar1=cw(3),
                                                                scalar2=V("conv_b", l, cc), op0=ALU.mult, op1=ALU.add),
                                 R=[xesr[cc], vecr, xcr[b3]], W=[xcr[b3]])
                            for j in range(1, 4):
                                K.op(dve, lambda e: e.scalar_tensor_tensor(out=xcs, in0=xes[:, cc, 3 - j:7 - j, :],
                                                                           scalar=cw(3 - j), in1=xcs, op0=ALU.mult, op1=ALU.add),
                                     R=[xesr[cc], vecr, xcr[b3]], W=[xcr[b3]])
                        K.op(dve, lambda e: e.tensor_copy(out=xcb[b2][:, 0:n], in_=xc[b3][:, 0:n]), R=[xcr[b3]], W=[xcbr[b2]])
                        pa, par = K.ps[4], K.psr[4]
                        pi2, pir2 = K.ps[5], K.psr[5]
                        K.mm(pa[:, 0:n], [(wrg[:, 0, cc, :], xcb[b2][:, 0:n])], R=[wrgr, xcbr[b2]], W=[par])
                        K.mm(pi2[:, 0:n], [(wrg[:, 1, cc, :], xcb[b2][:, 0:n])], R=[wrgr, xcbr[b2]], W=[pir2])
                        K.op(dve, lambda e: e.tensor_tensor(out=tmp[b2][:, 0:n], in0=ygf[b2][:, 0:n], in1=ygf[b2][:, 0:n], op=ALU.mult),
                             R=[ygfr[b2]], W=[tmpr[b2]])
                        K.op(dve, lambda e: e.tensor_scalar(out=tmp[b2][:, 0:n], in0=tmp[b2][:, 0:n], scalar1=0.044715, scalar2=1.0,
                                                            op0=ALU.mult, op1=ALU.add), R=[tmpr[b2]], W=[tmpr[b2]])
                        K.op(dve, lambda e: e.tensor_tensor(out=tmp[b2][:, 0:n], in0=tmp[b2][:, 0:n], in1=ygf[b2][:, 0:n], op=ALU.mult),
                             R=[tmpr[b2], ygfr[b2]], W=[tmpr[b2]])
                        K.op(act, lambda e: e.activation(out=tmp[b2][:, 0:n], in_=tmp[b2][:, 0:n], func=AF.Exp,
                                                         scale=-1.5957691216057308), R=[tmpr[b2]], W=[tmpr[b2]])
                        K.op(act, lambda e: e.activation(out=tmp[b2][:, 0:n], in_=tmp[b2][:, 0:n], func=AF.Ln, bias=1.0, scale=1.0),
                             R=[tmpr[b2]], W=[tmpr[b2]])
                        K.op(act, lambda e: e.activation(out=tmp[b2][:, 0:n], in_=tmp[b2][:, 0:n], func=AF.Exp, scale=-1.0),
                             R=[tmpr[b2]], W=[tmpr[b2]])
                        K.op(dve, lambda e: e.tensor_tensor(out=glb[b3][:, 0:n], in0=ygf[b2][:, 0:n], in1=tmp[b2][:, 0:n], op=ALU.mult),
                             R=[tmpr[b2], ygfr[b2]], W=[glbr[b3]])

                    def T3(k):
                        ti, cc = items[k]
                        c0, n = TILES[ti]
                        b2 = k % 2
                        pa, par = K.ps[4], K.psr[4]
                        pi2, pir2 = K.ps[5], K.psr[5]
                        g_, g_r, i_, i_r, t_, t_r = gr[b2], grr[b2], gi[b2], gir[b2], ts_[b2], tsr[b2]
                        K.op(act, lambda e: e.activation(out=g_[:, 0:n], in_=pa[:, 0:n], func=AF.Exp,
                                                         bias=nbg[:, cc:cc + 1], scale=-1.0), R=[par, scr], W=[g_r])
                        K.op(act, lambda e: e.activation(out=i_[:, 0:n], in_=pi2[:, 0:n], func=AF.Exp,
                                                         bias=nbg[:, 4 + cc:5 + cc], scale=-1.0), R=[pir2, scr], W=[i_r])
                        K.op(act, lambda e: e.activation(out=g_[:, 0:n], in_=g_[:, 0:n], func=AF.Ln, bias=1.0, scale=1.0), R=[g_r], W=[g_r])
                        K.op(act, lambda e: e.activation(out=i_[:, 0:n], in_=i_[:, 0:n], func=AF.Ln, bias=1.0, scale=1.0), R=[i_r], W=[i_r])
                        K.op(act, lambda e: e.activation(out=g_[:, 0:n], in_=g_[:, 0:n], func=AF.Exp, scale=-1.0), R=[g_r], W=[g_r])
                        K.op(act, lambda e: e.activation(out=i_[:, 0:n], in_=i_[:, 0:n], func=AF.Exp, scale=-1.0), R=[i_r], W=[i_r])
                        K.op(act, lambda e: e.activation(out=t_[:, 0:n], in_=g_[:, 0:n], func=AF.Exp, scale=sc2[:, cc:cc + 1]),
                             R=[g_r, scr], W=[t_r])
                        K.op(act, lambda e: e.activation(out=g_[:, 0:n], in_=g_[:, 0:n], func=AF.Exp, scale=sc[:, cc:cc + 1]),
                             R=[g_r, scr], W=[g_r])
                        K.op(act, lambda e: e.activation(out=t_[:, 0:n], in_=t_[:, 0:n], func=AF.Ln, scale=-1.0, bias=1.0), R=[t_r], W=[t_r])
                        K.op(act, lambda e: e.activation(out=t_[:, 0:n], in_=t_[:, 0:n], func=AF.Exp, scale=0.5), R=[t_r], W=[t_r])

                    def T4(k):
                        ti, cc = items[k]
                        c0, n = TILES[ti]
                        npr = min(n, TP - c0)
                        b2, b3 = k % 2, k % 3
                        g_, g_r, i_, i_r, t_, t_r = gr[b2], grr[b2], gi[b2], gir[b2], ts_[b2], tsr[b2]
                        hc, hcr_ = hcur[b2], hcurr[b2]
                        K.op(dve, lambda e: e.tensor_tensor(out=i_[:, 0:n], in0=i_[:, 0:n], in1=xc[b3][:, 0:n], op=ALU.mult),
                             R=[i_r, xcr[b3]], W=[i_r])
                        K.op(dve, lambda e: e.tensor_tensor(out=i_[:, 0:n], in0=i_[:, 0:n], in1=t_[:, 0:n], op=ALU.mult),
                             R=[i_r, t_r], W=[i_r])
                        init = 0.0 if ti == 0 else hcar[:, cc:cc + 1]
                        K.op(dve, lambda e: e.tensor_tensor_scan(out=hc[:, 0:npr], data0=g_[:, 0:npr], data1=i_[:, 0:npr],
                                                                 initial=init, op0=ALU.mult, op1=ALU.add),
                             R=[g_r, i_r, hcr[cc]], W=[hcr_])
                        if ti < 4:
                            K.op(act, lambda e: e.activation(out=hcar[:, cc:cc + 1], in_=hc[:, n - 1:n], func=AF.Copy),
                                 R=[hcr_], W=[hcr[cc]])
                        else:
                            K.op(act, lambda e: e.activation(out=php[:, cc:cc + 1], in_=hc[:, npr - 1:npr], func=AF.Copy),
                                 R=[hcr_], W=[hsor])
                            for t in range(NST):
                                s0 = npr + 16 * t
                                prev = h0s[:, cc, :] if t == 0 else hc[:, s0 - 16:s0]
                                K.op(dve, lambda e: e.tensor_tensor(out=hc[:, s0:s0 + 16], in0=g_[:, s0:s0 + 16], in1=prev,
                                                                    op=ALU.mult), R=[g_r, h0r, hcr_], W=[hcr_])
                                K.op(dve, lambda e: e.tensor_tensor(out=hc[:, s0:s0 + 16], in0=hc[:, s0:s0 + 16],
                                                                    in1=i_[:, s0:s0 + 16], op=ALU.add), R=[i_r, hcr_], W=[hcr_])
                            K.op(act, lambda e: e.activation(out=hs_o[:, cc, :], in_=hc[:, npr + 48:npr + 64], func=AF.Copy),
                                 R=[hcr_], W=[hsor])
                        K.op(dve, lambda e: e.tensor_tensor(out=sqh[b2][:, 0:n], in0=hc[:, 0:n], in1=hc[:, 0:n], op=ALU.mult), R=[hcr_], W=[sqhr[b2]])
                        pt, pr = K.ps[6], K.psr[6]
                        K.mm_multi([(pt[:, 0:n], ones_b, sqh[b2][:, 0:n], cc == 0, cc == 3)], R=[sqhr[b2], cbr], W=[pr])
                        K.op(dve, lambda e: e.tensor_tensor(out=mixin[:, 4 + cc, c0:c0 + n], in0=hc[:, 0:n], in1=glb[b3][:, 0:n],
                                                            op=ALU.mult), R=[hcr_, glbr[b3]], W=[mixr[4 + cc][ti]])
                        if cc == 3:
                            K.op(act, lambda e: e.activation(out=rsl[:, 0:n], in_=pt[:, 0:n], func=AF.Ln, scale=1.0 / DLRU,
                                                             bias=epsc[:, 0:1]), R=[pr, cbr], W=[rslr])
                            K.op(act, lambda e: e.activation(out=rsl[:, 0:n], in_=rsl[:, 0:n], func=AF.Exp, scale=-0.5),
                                 R=[rslr], W=[rslr])

                    def T5(k):
                        ti, cc = items[k]
                        if cc != 3:
                            return
                        c0, n = TILES[ti]
                        for c2 in range(4):
                            K.op(dve, lambda e: e.scalar_tensor_tensor(out=mixin[:, 4 + c2, c0:c0 + n], in0=mixin[:, 4 + c2, c0:c0 + n],
                                                                       scalar=V("g_lru_out", l, c2), in1=rsl[:, 0:n],
                                                                       op0=ALU.mult, op1=ALU.mult),
                                 R=[mixr[4 + c2][ti], vecr, rslr], W=[mixr[4 + c2][ti]])

                    for m in range(NIT + 4):
                        if 0 <= m - 4 < NIT:
                            T5(m - 4)
                        if 0 <= m - 3 < NIT:
                            T4(m - 3)
                        if 0 <= m - 2 < NIT:
                            T3(m - 2)
                        if 0 <= m - 1 < NIT:
                            T2(m - 1)
                        if m < NIT:
                            T1(m)
                    K.dma(sp, oh_d[l], hs_o[:], R=[hsor])
                    K.dma(sp, ph_d[l], php[:], R=[hsor])
                    r8rel()
                    r8rel()
                K.barrier()
                if dbg_d is not None and l == 0:
                    K.dma(sp, dbg_d, mixin[:], R=mixr)
                if stage < 5:
                    return
                wo = [r8(), r8()]
                with ExitStack() as os_:
                    S2 = lambda nm, shp, dt=F32: os_.enter_context(nc.sbuf_tensor(UN(nm), list(shp), dt))
                    if post_norm:
                        sq = S2("w_sq", [128, NKC, 512], BF16)
                        sqr = Res()
                        rs = [S2("w_rs0", [128, 512]), S2("w_rs1", [128, 512])]
                        rsr = [Res(), Res()]
                    for ti, (c0, n) in enumerate(TILES):
                        for m in range(NKC):
                            wt, wr = wo[m // 4]
                            mo = m % 4
                            pt, pr = K.psn()
                            K.mm(pt[:, 0:n], [(wt[:, kc, mo * 128:(mo + 1) * 128], mixin[:, kc, c0:c0 + n]) for kc in range(NKC)],
                                 R=[mixr[kc][ti] for kc in range(NKC)] + [wr], W=[pr])
                            K.op(dve, lambda e: e.tensor_tensor(out=x[:, m, c0:c0 + n], in0=pt[:, 0:n], in1=x[:, m, c0:c0 + n],
                                                                op=ALU.add), R=[pr, xr_[m][ti]], W=[xr_[m][ti]])
                        if post_norm and ti >= 1:
                            rmsnorm("g_ff2", l, xn, xnr, sq, sqr, rs, rsr, tiles=[ti - 1])
                    r8rel()
                    r8rel()
                    if post_norm:
                        rmsnorm("g_ff2", l, xn, xnr, sq, sqr, rs, rsr, tiles=[len(TILES) - 1])
            K.barrier()

        ones_f4 = SB("ones_f4", [H, 128])
        K.op(dve, lambda e: e.memset(ones_f4[:], 1.0), W=[cbr])

        MERGE = (stage >= 99) and not skip_ffn
        if MERGE:
            def post_norm_fn(gname, l_):
                return lambda sq, sqr, rs, rsr, ti: rmsnorm(gname, l_, xn, xnr, sq, sqr, rs, rsr, tiles=[ti])

            def post_final(sq, sqr, rs, rsr, ti):
                rmsnorm("g_final", 0, x, xr_, sq, sqr, rs, rsr, tiles=[ti])
                c0, n = TILES[ti]
                for kc in range(NKC):
                    K.dma(sp, yT_d[:, kc, c0:c0 + n], x[:, kc, c0:c0 + n], R=[xr_[kc][ti]])

            for l in range(DEPTH):
                ffn(l, 0, pre_norm=(l == 0), post=post_norm_fn("g_mix", l))
                mixer(l, pre_norm=False, post_norm=True)
                ffn(l, 1, pre_norm=False, post=(post_norm_fn("g_ff1", l + 1) if l + 1 < DEPTH else post_final))
            K.finish()
        else:
            for l in range(DEPTH):
                if stage >= 1 and not skip_ffn:
                    ffn(l, 0)
                if stage >= 2:
                    mixer(l)
                if stage >= 6 and not skip_ffn:
                    ffn(l, 1)
                if stage < 7:
                    break
            with ExitStack() as fs:
                S = lambda nm, shp, dt=F32: fs.enter_context(nc.sbuf_tensor(UN(nm), list(shp), dt))
                sq = S("o_sq", [128, NKC, 512], BF16)
                sqr = Res()
                rs = [S("o_rs0", [128, 512]), S("o_rs1", [128, 512])]
                rsr = [Res(), Res()]
                rmsnorm("g_final", 0, x, xr_, sq, sqr, rs, rsr)
                for kc in range(NKC):
                    K.dma(sp, yT_d[:, kc, :], x[:, kc, :], R=xr_[kc])
                K.finish()
    return nc


def _consts():
    cf = np.zeros((128, 512), np.float32)
    cf[:, 0:128] = np.eye(128, dtype=np.float32)
    s = np.arange(128)
    cf[:, 128:256] = (s[:, None] <= s[None, :]).astype(np.float32)
    i = np.arange(64)
    same = (i[:, None] % 16) == (i[None, :] % 16)
    caus = (i[:, None] // 16) <= (i[None, :] // 16)
    cf[0:64, 256:320] = (same & caus).astype(np.float32)
    cf[0:64, 320:336] = ((i[:, None] % 16) == np.arange(16)[None, :]).astype(np.float32)
    return cf


def _prep_shared(W):
    f = lambda a: np.ascontiguousarray(a, dtype=np.float32)
    out = {}
    wgu = np.empty((DEPTH, 2, NJP, 128, NKC, 512), np.float32)
    wdn = np.empty((DEPTH, 2, NJP, 128, 2, 1024), np.float32)
    for fi, (gn, un, dn) in enumerate((("w_ff1_gate", "w_ff1_up", "w_ff1_down"), ("w_ff2_gate", "w_ff2_up", "w_ff2_down"))):
        g = W[gn].reshape(DEPTH, NKC, 128, NJP, 256).transpose(0, 3, 2, 1, 4)
        u = W[un].reshape(DEPTH, NKC, 128, NJP, 256).transpose(0, 3, 2, 1, 4)
        wgu[:, fi, :, :, :, 0:256] = g
        wgu[:, fi, :, :, :, 256:512] = u
        wdn[:, fi] = W[dn].reshape(DEPTH, NJP, 2, 128, 1024).transpose(0, 1, 3, 2, 4)
    out["wgu"] = wgu
    out["wdn"] = wdn
    win = W["w_in"].reshape(DEPTH, NKC, 128, 3080)
    wq = np.empty((DEPTH, H, 128, NKC, 512), np.float32)
    for h in range(H):
        for qi in range(4):
            wq[:, h, :, :, qi * 128:(qi + 1) * 128] = win[:, :, :, qi * 512 + h * 128: qi * 512 + (h + 1) * 128].transpose(0, 2, 1, 3)
    out["win"] = wq
    out["wgt"] = f(win[:, :, :, 2048:2056].transpose(0, 2, 1, 3))
    wl = np.empty((DEPTH, 2, 128, NKC, 512), np.float32)
    for cc in range(4):
        o = (cc % 2) * 256
        wl[:, cc // 2, :, :, o:o + 128] = win[:, :, :, 2056 + cc * 128:2056 + (cc + 1) * 128].transpose(0, 2, 1, 3)
        wl[:, cc // 2, :, :, o + 128:o + 256] = win[:, :, :, 2568 + cc * 128:2568 + (cc + 1) * 128].transpose(0, 2, 1, 3)
    out["wlr"] = wl
    wrg = np.zeros((DEPTH, 128, 2, 4, 128), np.float32)
    for gi, nm in enumerate(("w_rg_a", "w_rg_x")):
        for nb in range(8):
            cc, half = nb // 2, nb % 2
            wrg[:, half * 64:(half + 1) * 64, gi, cc, half * 64:(half + 1) * 64] = W[nm][:, nb]
    out["wrg"] = wrg
    out["wout"] = f(W["w_out"].reshape(DEPTH, NKC, 128, 2, 512).transpose(0, 3, 2, 1, 4))
    vec = np.zeros((128, NV), np.float32)
    for l in range(DEPTH):
        for nm in ("g_ff1", "g_mix", "g_ff2"):
            vec[:, VOFF[(nm, l)]:VOFF[(nm, l)] + 8] = W[nm][l].reshape(8, 128).T
        o = VOFF[("conv_w", l)]
        vec[:, o:o + 16] = W["conv_w"][l].reshape(4, 4, 128).transpose(2, 0, 1).reshape(128, 16)
        for nm in ("conv_b", "b_rg_a", "b_rg_x", "lru_lambda", "g_lru_out"):
            vec[:, VOFF[(nm, l)]:VOFF[(nm, l)] + 4] = W[nm][l].reshape(4, 128).T
        vec[0:4, VOFF[("b_i", l)]] = W["b_gates"][l, 0:4]
        vec[0:4, VOFF[("b_f", l)]] = W["b_gates"][l, 4:8]
    vec[:, VOFF[("g_final", 0)]:VOFF[("g_final", 0)] + 8] = W["g_final"].reshape(8, 128).T
    out["vec"] = vec
    out["gml"] = f(np.broadcast_to(W["g_mlstm_out"][None], (128, DEPTH, 512)))
    out["cf"] = _consts()
    return out


def _prep_core(c, A):
    sl = slice(NSQ * c, NSQ * (c + 1))
    X = np.concatenate([A["meta_tokens"], A["x_prompt"][c],
                        A["x_sample"][sl].transpose(1, 0, 2).reshape(NS, D)], axis=0)
    m = {}
    m["xT"] = np.ascontiguousarray(X.T.reshape(NKC, 128, T).transpose(1, 0, 2))
    C = A["state_mlstm_C"][:, sl]
    m["sCT"] = np.ascontiguousarray(C.transpose(0, 2, 4, 1, 3))
    m["sC"] = np.ascontiguousarray(C.transpose(0, 2, 3, 1, 4))
    n = A["state_mlstm_n"][:, sl]
    m["snT"] = np.ascontiguousarray(n.transpose(0, 2, 3, 1))
    m["sn"] = np.ascontiguousarray(n.transpose(0, 2, 1, 3))
    m["sm"] = np.ascontiguousarray(A["state_mlstm_m"][:, sl].transpose(0, 2, 1))
    m["sh"] = np.ascontiguousarray(A["state_lru_h"][:, sl].reshape(DEPTH, NSQ, 4, 128).transpose(0, 3, 2, 1))
    m["scv"] = np.ascontiguousarray(A["state_conv"][:, sl].reshape(DEPTH, NSQ, 3, 4, 128).transpose(0, 4, 3, 2, 1))
    return m


_NC_CACHE = {}


def kernel(**inputs):
    A = {k: np.asarray(v, dtype=np.float32) for k, v in inputs.items()}
    shared = _prep_shared(A)
    in_maps = []
    for c in range(NCORES):
        m = dict(shared)
        m.update(_prep_core(c, A))
        in_maps.append(m)
    if "nc" not in _NC_CACHE:
        _NC_CACHE["nc"] = build_program()
    nc = _NC_CACHE["nc"]
    res = run_bass_kernel_spmd(nc, in_maps, core_ids=list(range(NCORES)))
    R = res.results
    B = NCORES
    y_prompt = np.empty((B, SEQ, D), np.float32)
    y_sample = np.empty((B * NSQ, NST, D), np.float32)
    pC = np.empty((DEPTH, B, H, DH, DH), np.float32)
    pn = np.empty((DEPTH, B, H, DH), np.float32)
    pm = np.empty((DEPTH, B, H), np.float32)
    ph = np.empty((DEPTH, B, DLRU), np.float32)
    pcv = np.empty((DEPTH, B, 3, DLRU), np.float32)
    sC = np.empty((DEPTH, B * NSQ, H, DH, DH), np.float32)
    sn = np.empty((DEPTH, B * NSQ, H, DH), np.float32)
    sm = np.empty((DEPTH, B * NSQ, H), np.float32)
    sh = np.empty((DEPTH, B * NSQ, DLRU), np.float32)
    scv = np.empty((DEPTH, B * NSQ, 3, DLRU), np.float32)
    for c in range(B):
        r = R[c]
        sl = slice(NSQ * c, NSQ * (c + 1))
        Y = r["yT"].transpose(1, 0, 2).reshape(D, T).T
        y_prompt[c] = Y[NMETA:TP]
        y_sample[sl] = Y[TP:].reshape(NST, NSQ, D).transpose(1, 0, 2)
        pct = r["pCT"]
        pC[:, c] = pct[:, :, :, 0:128].transpose(0, 1, 3, 2)
        pn[:, c] = pct[:, :, :, 128]
        pm[:, c] = r["pm"][:, :, 0]
        ph[:, c] = r["ph"].transpose(0, 2, 1).reshape(DEPTH, DLRU)
        pcv[:, c] = r["pcv"].transpose(0, 3, 2, 1).reshape(DEPTH, 3, DLRU)
        sC[:, sl] = r["oC"].transpose(0, 3, 1, 2, 4)
        sn[:, sl] = r["on"].transpose(0, 2, 1, 3)
        sm[:, sl] = r["om"].transpose(0, 2, 1)
        sh[:, sl] = r["oh"].transpose(0, 3, 2, 1).reshape(DEPTH, NSQ, DLRU)
        scv[:, sl] = r["ocv"].transpose(0, 4, 3, 2, 1).reshape(DEPTH, NSQ, 3, DLRU)
    return (y_prompt, y_sample, pC, pn, pm, ph, pcv, sC, sn, sm, sh, scv)
```

```python
import os
import numpy as np
import ml_dtypes
KDBG = int(os.environ.get('KDBG', '9'))
KNOS = int(os.environ.get('KNOS', '0'))
KSKIP = int(os.environ.get('KSKIP', '0'))
KCH = int(os.environ.get('KCH', '99'))
KSELF = int(os.environ.get('KSELF', '1'))
from contextlib import ExitStack
import concourse.bass as bass
import concourse.mybir as mybir
from concourse.bass_utils import run_bass_kernel_spmd

F32 = mybir.dt.float32
BF16 = mybir.dt.bfloat16
AF = mybir.ActivationFunctionType
ALU = mybir.AluOpType

NCORES = 8
D = 1024
NKC = 8
SEQ = 2048
NMETA = 16
TP = NMETA + SEQ
NSQ = 16
NST = 4
NS = NSQ * NST
T = TP + NS
DFF = 2816
NJ = DFF // 128
NJP = NJ // 2
H = 4
DH = 128
DLRU = 512
DEPTH = 2
EPS = 1e-6
TILES = [(0, 512), (512, 512), (1024, 512), (1536, 512), (2048, 80)]
CHUNKS = [(128 * c, 128) for c in range(16)] + [(2048, 16), (TP, NS)]
NCH = len(CHUNKS)
GROUPS = [[0, 1, 2], [3, 4, 5], [6, 7, 8], [9, 10]]
KSCALE = DH ** -0.5

def _vec_layout():
    off = {}
    n = 0
    for l in range(DEPTH):
        for nm, w in (("g_ff1", 8), ("g_mix", 8), ("g_ff2", 8), ("conv_w", 16), ("conv_b", 4),
                      ("b_rg_a", 4), ("b_rg_x", 4), ("lru_lambda", 4), ("g_lru_out", 4),
                      ("b_i", 1), ("b_f", 1)):
            off[(nm, l)] = n
            n += w
    off[("g_final", 0)] = n
    n += 8
    return off, n

VOFF, NV = _vec_layout()


class Res:
    __slots__ = ("w", "r")

    def __init__(self):
        self.w = None
        self.r = []


def RL(*dims):
    if len(dims) == 0:
        return Res()
    return [RL(*dims[1:]) for _ in range(dims[0])]


def flat(x):
    if isinstance(x, Res):
        return [x]
    out = []
    for y in x:
        out.extend(flat(y))
    return out


class Eng:
    def __init__(self, e, sem, name):
        self.e = e
        self.sem = sem
        self.n = 0
        self.seen = {}
        self.name = name


class Ctx:
    def __init__(self, nc, es):
        self.nc = nc
        self.es = es
        mk = lambda nm: es.enter_context(nc.semaphore(nm))
        self.pe = Eng(nc.tensor, mk("c_pe"), "pe")
        self.act = Eng(nc.scalar, mk("c_act"), "act")
        self.dve = Eng(nc.vector, mk("c_dve"), "dve")
        self.pool = Eng(nc.gpsimd, mk("c_pool"), "pool")
        self.sp = Eng(nc.sync, None, "sp")
        self.compute = [self.pe, self.act, self.dve]
        self.dsems = {}
        for q, nq in ((self.sp, 16), (self.pool, 16)):
            self.dsems[q.name] = [[mk(f"d_{q.name}{i}"), 0] for i in range(nq)]
        self.dptr = {"sp": 0, "pool": 0}
        self.semid = {}
        self.ps = []
        self.psr = []
        for i in range(8):
            self.ps.append(es.enter_context(nc.psum_tensor(f"ps{i}", [128, 512], F32)))
            self.psr.append(Res())
        self.psi = 0

    def sid(self, sem):
        k = id(sem)
        if k not in self.semid:
            self.semid[k] = sem
        return k

    def psn(self):
        i = self.psi
        self.psi = (self.psi + 1) % 8
        return self.ps[i], self.psr[i]

    def _waits(self, eng, R, W):
        deps = {}
        for r in R:
            if r.w is not None:
                k = self.sid(r.w[0])
                deps[k] = max(deps.get(k, 0), r.w[1])
        for w in W:
            if w.w is not None:
                k = self.sid(w.w[0])
                deps[k] = max(deps.get(k, 0), w.w[1])
            for t in w.r:
                k = self.sid(t[0])
                deps[k] = max(deps.get(k, 0), t[1])
        for k, v in deps.items():
            if not KSELF and eng.sem is not None and k == id(eng.sem):
                continue
            if eng.seen.get(k, 0) < v:
                eng.e.wait_ge(self.semid[k], v)
                eng.seen[k] = v

    def _done(self, tok, R, W):
        for r in R:
            r.r.append(tok)
        for w in W:
            w.w = tok
            w.r = []

    def op(self, eng, fn, R=(), W=()):
        R = flat(R)
        W = flat(W)
        self._waits(eng, R, W)
        ins = fn(eng.e)
        eng.n += 1
        ins.then_inc(eng.sem, 1)
        self._done((eng.sem, eng.n), R, W)

    def mm(self, out, parts, R, W):
        R = flat(R)
        W = flat(W)
        eng = self.pe
        self._waits(eng, R, W)
        n = len(parts)
        ins = None
        for i, (l, r) in enumerate(parts):
            ins = eng.e.matmul(out, lhsT=l, rhs=r, start=(i == 0), stop=(i == n - 1))
        eng.n += 1
        ins.then_inc(eng.sem, 1)
        self._done((eng.sem, eng.n), R, W)

    def mm_multi(self, groups, R, W):
        R = flat(R)
        W = flat(W)
        eng = self.pe
        self._waits(eng, R, W)
        ins = None
        for (o, l, r, st, sp) in groups:
            ins = eng.e.matmul(o, lhsT=l, rhs=r, start=st, stop=sp)
        eng.n += 1
        ins.then_inc(eng.sem, 1)
        self._done((eng.sem, eng.n), R, W)

    def tr(self, out, in_, ident, R, W):
        R = flat(R)
        W = flat(W)
        eng = self.pe
        self._waits(eng, R, W)
        ins = eng.e.transpose(out, in_, ident)
        eng.n += 1
        ins.then_inc(eng.sem, 1)
        self._done((eng.sem, eng.n), R, W)

    def tr_multi(self, items, R, W):
        R = flat(R)
        W = flat(W)
        eng = self.pe
        self._waits(eng, R, W)
        ins = None
        for (o, i_, idn) in items:
            ins = eng.e.transpose(o, i_, idn)
        eng.n += 1
        ins.then_inc(eng.sem, 1)
        self._done((eng.sem, eng.n), R, W)

    def dma(self, q, out, in_, R=(), W=()):
        R = flat(R)
        W = flat(W)
        self._waits(q, R, W)
        pool = self.dsems[q.name]
        i = self.dptr[q.name]
        self.dptr[q.name] = (i + 1) % len(pool)
        sem, cnt = pool[i]
        k = self.sid(sem)
        if cnt > 0 and q.seen.get(k, 0) < 16 * cnt:
            q.e.wait_ge(sem, 16 * cnt)
            q.seen[k] = 16 * cnt
        q.e.dma_start(out=out, in_=in_).then_inc(sem, 16)
        pool[i][1] = cnt + 1
        self._done((sem, 16 * (cnt + 1)), R, W)

    def barrier(self):
        toks = [(e.sem, e.n) for e in (self.pe, self.act, self.dve, self.pool) if e.n > 0]
        for q in ("sp", "pool"):
            for sem, cnt in self.dsems[q]:
                if cnt > 0:
                    toks.append((sem, 16 * cnt))
        for eng in (self.pe, self.act, self.dve, self.sp, self.pool):
            for sem, v in toks:
                k = self.sid(sem)
                if eng.seen.get(k, 0) < v:
                    eng.e.wait_ge(sem, v)
                    eng.seen[k] = v

    def finish(self):
        self.barrier()


def build_program(stage=99, skip_ffn=False):
    _uc = [0]

    def UN(nm):
        _uc[0] += 1
        return f"sb{_uc[0]}_{nm}"

    nc = bass.Bass("TRN2", target_bir_lowering=False)
    I = lambda nm, shp: nc.dram_tensor(nm, list(shp), F32, kind="ExternalInput").ap()
    O = lambda nm, shp: nc.dram_tensor(nm, list(shp), F32, kind="ExternalOutput").ap()
    xT_d = I("xT", [128, NKC, T])
    vec_d = I("vec", [128, NV])
    gml_d = I("gml", [128, DEPTH, 512])
    cf_d = I("cf", [128, 512])
    wgu_d = I("wgu", [DEPTH, 2, NJP, 128, NKC, 512])
    wdn_d = I("wdn", [DEPTH, 2, NJP, 128, 2, 1024])
    win_d = I("win", [DEPTH, H, 128, NKC, 512])
    wgt_d = I("wgt", [DEPTH, 128, NKC, 8])
    wlr_d = I("wlr", [DEPTH, 2, 128, NKC, 512])
    wrg_d = I("wrg", [DEPTH, 128, 2, 4, 128])
    wout_d = I("wout", [DEPTH, 2, 128, NKC, 512])
    sCT_d = I("sCT", [DEPTH, H, 128, NSQ, 128])
    sC_d = I("sC", [DEPTH, H, 128, NSQ, 128])
    snT_d = I("snT", [DEPTH, H, 128, NSQ])
    sn_d = I("sn", [DEPTH, H, NSQ, 128])
    sm_d = I("sm", [DEPTH, H, NSQ])
    sh_d = I("sh", [DEPTH, 128, 4, NSQ])
    scv_d = I("scv", [DEPTH, 128, 4, 3, NSQ])
    yT_d = O("yT", [128, NKC, T])
    pCT_d = O("pCT", [DEPTH, H, 128, 129])
    pm_d = O("pm", [DEPTH, H, 1])
    ph_d = O("ph", [DEPTH, 128, 4])
    pcv_d = O("pcv", [DEPTH, 128, 4, 3])
    oC_d = O("oC", [DEPTH, H, 128, NSQ, 128])
    on_d = O("on", [DEPTH, H, NSQ, 128])
    om_d = O("om", [DEPTH, H, NSQ])
    oh_d = O("oh", [DEPTH, 128, 4, NSQ])
    ocv_d = O("ocv", [DEPTH, 128, 4, 3, NSQ])
    dbg_d = nc.dram_tensor("dbg", [128, NKC, T], BF16, kind="ExternalOutput").ap() if os.environ.get('KDBGOUT') else None

    with ExitStack() as es:
        K = Ctx(nc, es)
        pe, act, dve, pool, sp = K.pe, K.act, K.dve, K.pool, K.sp
        SB = lambda nm, shp, dt=F32: es.enter_context(nc.sbuf_tensor(UN(nm), list(shp), dt))

        x = SB("x", [128, NKC, T])
        xr_ = RL(NKC, 5)
        xn = SB("xn", [128, NKC, T], BF16)
        xnr = RL(NKC, 5)
        vec = SB("vec", [128, NV])
        vecr = Res()
        cf = SB("cf", [128, 512])
        cfr = Res()
        cb = SB("cb", [128, 512], BF16)
        cbr = Res()
        NR8 = 2
        R8 = [SB(f"r8_{i}", [128, NKC, 512], BF16) for i in range(NR8)]
        R8r = [Res() for _ in range(NR8)]
        w_items = []
        for l_ in range(DEPTH):
            if stage >= 1 and not skip_ffn:
                w_items += [wgu_d[l_, 0, jp] for jp in range(NJP)]
            if stage >= 3:
                w_items += [win_d[l_, h_] for h_ in range(H)]
            if stage >= 4:
                w_items += [wlr_d[l_, i_] for i_ in range(2)]
            if stage >= 5:
                w_items += [wout_d[l_, i_] for i_ in range(2)]
            if stage >= 6 and not skip_ffn:
                w_items += [wgu_d[l_, 1, jp] for jp in range(NJP)]
            if stage < 7:
                break
        wst = {"issued": 0, "consumed": 0, "released": 0}

        def _r8issue(upto):
            while wst["issued"] < min(len(w_items), upto) and wst["issued"] - NR8 < wst["released"]:
                i = wst["issued"]
                K.dma(pool, R8[i % NR8][:], w_items[i], W=[R8r[i % NR8]])
                wst["issued"] += 1

        def r8():
            k = wst["consumed"]
            wst["consumed"] += 1
            _r8issue(k + NR8)
            assert wst["issued"] > k
            return R8[k % NR8], R8r[k % NR8]

        def r8rel():
            wst["released"] += 1
            _r8issue(wst["consumed"] + NR8 - 1)

        ident_f = cf[:, 0:128]
        ident_b = cb[:, 0:128]
        causal_b = cb[:, 128:256]
        smask_b = cb[0:64, 256:320]
        ones_b = cb[:, 320:448]
        ind_f = cf[0:64, 320:336]

        def V(nm, l, j=0, n=1, rows=128):
            o = VOFF[(nm, l)] + j
            return vec[0:rows, o:o + n]

        K.dma(sp, vec[:], vec_d, W=[vecr])
        K.dma(sp, cf[:], cf_d, W=[cfr])
        for kc in range(NKC):
            K.dma(sp, x[:, kc, :], xT_d[:, kc, :], W=xr_[kc])
        K.op(dve, lambda e: e.tensor_copy(out=cb[:, 0:320], in_=cf[:, 0:320]), R=[cfr], W=[cbr])
        K.op(dve, lambda e: e.memset(cb[:, 320:448], 1.0), W=[cbr])

        def rmsnorm(gname, l, out_t, out_r, sq, sqr, rs, rsr, tiles=None):
            for ti, (c0, n) in enumerate(TILES):
                if tiles is not None and ti not in tiles:
                    continue
                K.op(act, lambda e: e.activation(out=sq[:, :, 0:n], in_=x[:, :, c0:c0 + n], func=AF.Square),
                     R=[xr_[kc][ti] for kc in range(NKC)], W=[sqr])
                pt, pr = K.psn()
                K.mm(pt[:, 0:n], [(ones_b, sq[:, kc, 0:n]) for kc in range(NKC)], R=[sqr, cbr], W=[pr])
                b = ti % 2
                K.op(act, lambda e: e.activation(out=rs[b][:, 0:n], in_=pt[:, 0:n], func=AF.Ln,
                                                 scale=1.0 / D, bias=epsc[:, 0:1]), R=[pr, cbr], W=[rsr[b]])
                K.op(act, lambda e: e.activation(out=rs[b][:, 0:n], in_=rs[b][:, 0:n], func=AF.Exp, scale=-0.5),
                     R=[rsr[b]], W=[rsr[b]])
                for kc in range(NKC):
                    K.op(dve, lambda e: e.scalar_tensor_tensor(
                        out=out_t[:, kc, c0:c0 + n], in0=x[:, kc, c0:c0 + n], scalar=V(gname, l, kc),
                        in1=rs[b][:, 0:n], op0=ALU.mult, op1=ALU.mult),
                        R=[xr_[kc][ti], rsr[b], vecr], W=[out_r[kc][ti]])

        epsc = SB("epsc", [128, 1])
        K.op(dve, lambda e: e.memset(epsc[:], EPS), W=[cbr])

        def ffn(l, f, pre_norm=True, post=None):
            gname = "g_ff1" if f == 0 else "g_ff2"
            with ExitStack() as fs:
                S = lambda nm, shp, dt=F32: fs.enter_context(nc.sbuf_tensor(UN(nm), list(shp), dt))
                sq = S("f_sq", [128, NKC, 512], BF16)
                sqr = Res()
                rs = [S("f_rs0", [128, 512]), S("f_rs1", [128, 512])]
                rsr = [Res(), Res()]
                hg = S("f_h", [128, 6, T], BF16)
                hgr = RL(6, 5)
                sg = [S("f_sg0", [128, 512]), S("f_sg1", [128, 512])]
                sgr = [Res(), Res()]
                R4 = [S(f"f_r4_{i}", [128, 2, 1024], BF16) for i in range(3)]
                R4r = [Res() for _ in range(3)]
                r4p = [0]

                def r4():
                    i = r4p[0]
                    r4p[0] = (i + 1) % 3
                    return R4[i], R4r[i]
                if pre_norm:
                    rmsnorm(gname, l, xn, xnr, sq, sqr, rs, rsr)
                cnt = 0
                for g, pairs in enumerate(GROUPS):
                    wds = []
                    for pi, jp in enumerate(pairs):
                        wt, wr = r8()
                        dt_, dr = r4()
                        K.dma(pool, dt_[:], wdn_d[l, f, jp], W=[dr])
                        wds.append((dt_, dr))
                        for jj in range(2):
                            jl = 2 * pi + jj
                            for ti, (c0, n) in enumerate(TILES):
                                pg, pgr = K.psn()
                                pu, pur = K.psn()
                                rr = [xnr[kc][ti] for kc in range(NKC)] + [wr]
                                K.mm(pg[:, 0:n], [(wt[:, kc, jj * 128:(jj + 1) * 128], xn[:, kc, c0:c0 + n])
                                                  for kc in range(NKC)], R=rr, W=[pgr])
                                K.mm(pu[:, 0:n], [(wt[:, kc, 256 + jj * 128:256 + (jj + 1) * 128], xn[:, kc, c0:c0 + n])
                                                  for kc in range(NKC)], R=rr, W=[pur])
                                b = cnt % 2
                                cnt += 1
                                K.op(act, lambda e: e.activation(out=sg[b][:, 0:n], in_=pg[:, 0:n], func=AF.Silu),
                                     R=[pgr], W=[sgr[b]])
                                K.op(dve, lambda e: e.tensor_tensor(out=hg[:, jl, c0:c0 + n], in0=sg[b][:, 0:n],
                                                                    in1=pu[:, 0:n], op=ALU.mult),
                                     R=[sgr[b], pur], W=[hgr[jl][ti]])
                        r8rel()
                    nj = 2 * len(pairs)
                    for ti, (c0, n) in enumerate(TILES):
                        for m in range(NKC):
                            pt, pr = K.psn()
                            K.mm(pt[:, 0:n], [(wds[jl // 2][0][:, jl % 2, m * 128:(m + 1) * 128], hg[:, jl, c0:c0 + n])
                                              for jl in range(nj)],
                                 R=[hgr[jl][ti] for jl in range(nj)] + [w[1] for w in wds], W=[pr])
                            K.op(dve, lambda e: e.scalar_tensor_tensor(
                                out=x[:, m, c0:c0 + n], in0=pt[:, 0:n], scalar=0.5, in1=x[:, m, c0:c0 + n],
                                op0=ALU.mult, op1=ALU.add), R=[pr, xr_[m][ti]], W=[xr_[m][ti]])
                        if post is not None and g == len(GROUPS) - 1 and ti >= 1:
                            post(sq, sqr, rs, rsr, ti - 1)
                if post is not None:
                    post(sq, sqr, rs, rsr, len(TILES) - 1)
            K.barrier()

        def mixer(l, pre_norm=True, post_norm=False):
            with ExitStack() as ms:
                S = lambda nm, shp, dt=F32: ms.enter_context(nc.sbuf_tensor(UN(nm), list(shp), dt))
                mixin = S("m_mixin", [128, NKC, T], BF16)
                mixr = RL(NKC, 5)
                colr = Res()
                cols = S("m_cols", [128, NCH, 12])
                dcp = S("m_dcp", [128, H, 17])
                dcs = S("m_dcs", [128, H, NSQ])
                dcsT = S("m_dcsT", [NSQ, H])
                smallr = Res()
                with ExitStack() as rs_:
                    S2 = lambda nm, shp, dt=F32: rs_.enter_context(nc.sbuf_tensor(UN(nm), list(shp), dt))
                    with ExitStack() as ns_:
                        S3 = lambda nm, shp, dt=F32: ns_.enter_context(nc.sbuf_tensor(UN(nm), list(shp), dt))
                        sq = S3("r_sq", [128, NKC, 512], BF16)
                        sqr = Res()
                        rs = [S3("r_rs0", [128, 512]), S3("r_rs1", [128, 512])]
                        rsr = [Res(), Res()]
                        if pre_norm:
                            rmsnorm("g_mix", l, xn, xnr, sq, sqr, rs, rsr)
                    if pre_norm:
                        K.barrier()
                    wg = S2("r_wg", [128, NKC, 8], BF16)
                    wgr = Res()
                    K.dma(pool, wg[:], wgt_d[l], W=[wgr])
                    R1 = S2("r_R1", [H, T])
                    R2 = S2("r_R2", [H, T])
                    R3 = S2("r_R3", [H, T])
                    r1, r2, r3 = Res(), Res(), Res()
                    nbf = S2("r_nbf", [H, 1])
                    sm = S2("r_sm", [H, 24])
                    mref = S2("r_mref", [H, 18])
                    m0 = S2("r_m0", [H, NSQ])
                    mnx = S2("r_mnx", [H, NSQ])
                    dcr = S2("r_dcr", [H, 17 + NSQ])
                    dce = S2("r_dce", [H, H, 17 + NSQ])
                    K.dma(sp, m0[:], sm_d[l], W=[smallr])
                    K.op(act, lambda e: e.mul(out=nbf[:], in_=V("b_f", l, rows=H), mul=-1.0), R=[vecr], W=[smallr])
                    for ti, (c0, n) in enumerate(TILES):
                        pi_, pir = K.psn()
                        pf_, pfr = K.psn()
                        rr = [xnr[kc][ti] for kc in range(NKC)] + [wgr]
                        K.mm(pi_[0:H, 0:n], [(wg[:, kc, 0:4], xn[:, kc, c0:c0 + n]) for kc in range(NKC)], R=rr, W=[pir])
                        K.mm(pf_[0:H, 0:n], [(wg[:, kc, 4:8], xn[:, kc, c0:c0 + n]) for kc in range(NKC)], R=rr, W=[pfr])
                        K.op(act, lambda e: e.activation(out=R1[:, c0:c0 + n], in_=pi_[0:H, 0:n], func=AF.Identity,
                                                         bias=V("b_i", l, rows=H), scale=1.0), R=[pir, vecr], W=[r1])
                        K.op(act, lambda e: e.activation(out=R2[:, c0:c0 + n], in_=pf_[0:H, 0:n], func=AF.Exp,
                                                         bias=nbf[:], scale=-1.0), R=[pfr, smallr], W=[r2])
                    K.op(act, lambda e: e.activation(out=R2[:], in_=R2[:], func=AF.Ln, bias=1.0, scale=1.0), R=[r2], W=[r2])
                    K.op(dve, lambda e: e.tensor_tensor_scan(out=R3[:, 0:TP], data0=R2[:, 0:TP], data1=R2[:, 0:TP],
                                                             initial=0.0, op0=ALU.add, op1=ALU.max), R=[r2], W=[r3])
                    K.op(dve, lambda e: e.tensor_copy(out=R3[:, TP:TP + 16], in_=R2[:, TP:TP + 16]), R=[r2], W=[r3])
                    for t in range(1, NST):
                        K.op(dve, lambda e: e.tensor_tensor(out=R3[:, TP + 16 * t:TP + 16 * t + 16],
                                                            in0=R3[:, TP + 16 * (t - 1):TP + 16 * t],
                                                            in1=R2[:, TP + 16 * t:TP + 16 * t + 16], op=ALU.add),
                             R=[r2, r3], W=[r3])
                    K.op(dve, lambda e: e.tensor_tensor(out=R1[:], in0=R1[:], in1=R3[:], op=ALU.add), R=[r1, r3], W=[r1])
                    K.op(dve, lambda e: e.tensor_tensor_scan(out=R2[:, 0:TP], data0=R1[:, 0:TP], data1=R1[:, 0:TP],
                                                             initial=0.0, op0=ALU.max, op1=ALU.max), R=[r1, r2], W=[r2])
                    K.op(dve, lambda e: e.tensor_tensor(out=R2[:, TP:TP + 16], in0=R1[:, TP:TP + 16], in1=m0[:],
                                                        op=ALU.max), R=[r1, smallr, r2], W=[r2])
                    for t in range(1, NST):
                        K.op(dve, lambda e: e.tensor_tensor(out=R2[:, TP + 16 * t:TP + 16 * t + 16],
                                                            in0=R2[:, TP + 16 * (t - 1):TP + 16 * t],
                                                            in1=R1[:, TP + 16 * t:TP + 16 * t + 16], op=ALU.max),
                             R=[r1, r2], W=[r2])
                    K.op(dve, lambda e: e.memset(mref[:, 0:1], 0.0), W=[smallr])
                    K.op(dve, lambda e: e.tensor_copy(
                        out=mref[:, 1:17], in_=R2[:, 0:2048].rearrange("p (c t) -> p c t", t=128)[:, :, 127]),
                        R=[r2, smallr], W=[smallr])
                    K.op(dve, lambda e: e.tensor_copy(out=mref[:, 17:18], in_=R2[:, TP - 1:TP]), R=[r2, smallr], W=[smallr])
                    K.op(dve, lambda e: e.tensor_copy(out=mnx[:], in_=R2[:, TP + 48:TP + 64]), R=[r2, smallr], W=[smallr])
                    K.op(dve, lambda e: e.tensor_tensor(out=sm[:, 0:1], in0=R2[:, TP - 1:TP], in1=R3[:, TP - 1:TP],
                                                        op=ALU.subtract), R=[r2, r3, smallr], W=[smallr])
                    K.op(dve, lambda e: e.tensor_tensor(out=sm[:, 1:17], in0=R2[:, TP + 48:TP + 64],
                                                        in1=R3[:, TP + 48:TP + 64], op=ALU.subtract),
                         R=[r2, r3, smallr], W=[smallr])
                    K.dma(sp, pm_d[l], sm[:, 0:1], R=[smallr])
                    K.dma(sp, om_d[l], sm[:, 1:17], R=[smallr])
                    K.op(dve, lambda e: e.tensor_tensor(out=dcr[:, 0:17], in0=mref[:, 0:17], in1=mref[:, 1:18],
                                                        op=ALU.subtract), R=[smallr], W=[smallr])
                    K.op(dve, lambda e: e.tensor_tensor(out=dcr[:, 17:33], in0=m0[:], in1=mnx[:], op=ALU.subtract),
                         R=[smallr], W=[smallr])
                    K.op(act, lambda e: e.activation(out=dcr[:], in_=dcr[:], func=AF.Exp), R=[smallr], W=[smallr])
                    pv = lambda Rt: Rt[:, 0:2048].rearrange("p (c t) -> p c t", t=128)
                    sv = lambda Rt: Rt[:, TP:T].rearrange("p (t b) -> p t b", b=NSQ)
                    bc = lambda ap_, n: ap_.unsqueeze(2).to_broadcast([H, ap_.shape[1], n])
                    bs = lambda ap_: ap_.unsqueeze(1).to_broadcast([H, NST, NSQ])
                    K.op(dve, lambda e: e.tensor_tensor(out=pv(R2), in0=pv(R1), in1=bc(mref[:, 0:16], 128), op=ALU.subtract),
                         R=[r1, smallr, r2], W=[r2])
                    K.op(dve, lambda e: e.tensor_scalar(out=R2[:, 2048:TP], in0=R1[:, 2048:TP], scalar1=mref[:, 16:17],
                                                        scalar2=None, op0=ALU.subtract), R=[r1, smallr, r2], W=[r2])
                    K.op(dve, lambda e: e.tensor_tensor(out=sv(R2), in0=sv(R1), in1=bs(m0[:]), op=ALU.subtract),
                         R=[r1, smallr, r2], W=[r2])
                    K.op(act, lambda e: e.activation(out=R2[:], in_=R2[:], func=AF.Exp), R=[r2], W=[r2])
                    K.op(dve, lambda e: e.tensor_tensor(out=pv(R3), in0=pv(R3), in1=bc(mref[:, 0:16], 128), op=ALU.subtract),
                         R=[r3, smallr], W=[r3])
                    K.op(dve, lambda e: e.tensor_scalar(out=R3[:, 2048:TP], in0=R3[:, 2048:TP], scalar1=mref[:, 16:17],
                                                        scalar2=None, op0=ALU.subtract), R=[r3, smallr], W=[r3])
                    K.op(dve, lambda e: e.tensor_tensor(out=sv(R3), in0=sv(R3), in1=bs(m0[:]), op=ALU.subtract),
                         R=[r3, smallr], W=[r3])
                    K.op(act, lambda e: e.activation(out=R3[:], in_=R3[:], func=AF.Exp, scale=2.0), R=[r3], W=[r3])
                    K.op(dve, lambda e: e.tensor_tensor(out=pv(R1), in0=pv(R1), in1=bc(mref[:, 1:17], 128), op=ALU.subtract),
                         R=[r1, smallr], W=[r1])
                    K.op(dve, lambda e: e.tensor_scalar(out=R1[:, 2048:TP], in0=R1[:, 2048:TP], scalar1=mref[:, 17:18],
                                                        scalar2=None, op0=ALU.subtract), R=[r1, smallr], W=[r1])
                    K.op(dve, lambda e: e.tensor_tensor(out=sv(R1), in0=sv(R1), in1=bs(mnx[:]), op=ALU.subtract),
                         R=[r1, smallr], W=[r1])
                    K.op(act, lambda e: e.activation(out=R1[:], in_=R1[:], func=AF.Exp), R=[r1], W=[r1])
                    for ci, (c0, n) in enumerate(CHUNKS):
                        pt, pr = K.psn()
                        K.tr_multi([(pt[0:n, 4 * qi:4 * qi + 4], Rt[:, c0:c0 + n], ident_f[0:H, 0:H])
                                    for qi, Rt in enumerate((R2, R1, R3))], R=[r1, r2, r3, cfr], W=[pr])
                        K.op(act, lambda e: e.activation(out=cols[0:n, ci, :], in_=pt[0:n, 0:12], func=AF.Copy),
                             R=[pr], W=[colr])
                    for hh in range(H):
                        K.op(dve, lambda e: e.tensor_scalar(out=dce[:, hh, :], in0=dcr[:], scalar1=ident_f[0:H, hh:hh + 1],
                                                            scalar2=None, op0=ALU.mult), R=[smallr, cfr], W=[smallr])
                    pt, pr = K.psn()
                    K.mm(pt[:, 0:H * 33], [(ones_f4[:], dce[:].rearrange("p h j -> p (h j)"))], R=[smallr, cbr], W=[pr])
                    ptv = pt[:, 0:H * 33].rearrange("p (h j) -> p h j", j=33)
                    K.op(act, lambda e: e.activation(out=dcp[:], in_=ptv[:, :, 0:17], func=AF.Copy), R=[pr], W=[colr])
                    K.op(act, lambda e: e.activation(out=dcs[:], in_=ptv[:, :, 17:33], func=AF.Copy), R=[pr], W=[colr])
                    pt2, pr2 = K.psn()
                    K.tr(pt2[0:NSQ, 0:H], dcr[:, 17:33], ident_f[0:H, 0:H], R=[smallr, cfr], W=[pr2])
                    K.op(act, lambda e: e.activation(out=dcsT[:], in_=pt2[0:NSQ, 0:H], func=AF.Copy), R=[pr2], W=[colr])
                K.barrier()
                if stage < 3:
                    return
                with ExitStack() as hs:
                    S2 = lambda nm, shp, dt=F32: hs.enter_context(nc.sbuf_tensor(UN(nm), list(shp), dt))
                    NB = 5
                    gml = S2("h_gml", [128, 512])
                    gmlr = Res()
                    K.dma(sp, gml[:], gml_d[:, l, :], W=[gmlr])
                    qk = [S2(f"h_qk{i}", [128, 2, 512], BF16) for i in range(3)]
                    qkr = [Res(), Res(), Res()]
                    ktok = [S2(f"h_kt{i}", [128, 128], BF16) for i in range(NB)]
                    v1 = [S2(f"h_v1{i}", [128, 129], BF16) for i in range(NB)]
                    vwx = [S2(f"h_vw{i}", [128, 129], BF16) for i in range(NB)]
                    sgo = [S2(f"h_so{i}", [128, 128]) for i in range(NB)]
                    eo = [S2(f"h_eo{i}", [128, 128]) for i in range(2)]
                    eor = [Res(), Res()]
                    stw = [S2(f"h_sw{i}", [128, 128], BF16) for i in range(NB)]
                    tokr = [RL(5) for _ in range(NB)]
                    hgt = [S2(f"h_hg{i}", [128, 128]) for i in range(2)]
                    hmt = [S2(f"h_hm{i}", [128, 128], BF16) for i in range(2)]
                    junk = [S2(f"h_jk{i}", [128, 128], BF16) for i in range(2)]
                    pcol = [S2(f"h_pc{i}", [128, 8]) for i in range(3)]
                    pcr = [Res() for _ in range(3)]
                    postr = [RL(4), RL(4)]
                    CTn = S2("h_CTn", [128, 129])
                    CTb = S2("h_CTb", [128, 129], BF16)
                    ctr, ctbr = Res(), Res()
                    C0 = S2("h_C0", [128, NSQ, 128])
                    c0r = RL(NSQ)
                    CT0 = S2("h_CT0", [128, NSQ, 130], BF16)
                    ct0r = Res()
                    n0T = S2("h_n0T", [128, NSQ])
                    n0 = S2("h_n0", [NSQ, 128])
                    n0r = Res()
                    qd = S2("h_qd", [128, 16 * 65], BF16)
                    qdr = Res()
                    vwb = [S2(f"h_vwb{i}", [64, 128], BF16) for i in range(NSQ)]
                    vwbr = [Res() for _ in range(NSQ)]
                    ewb = S2("h_ewb", [64, NSQ], BF16)
                    ewbr = Res()
                    for i in range(NB):
                        K.op(dve, lambda e: e.memset(v1[i][:, 128:129], 1.0), W=[tokr[i][1]])
                    K.op(dve, lambda e: e.memset(qd[:], 0.0), W=[qdr])
                    qd_rows = qd[:, 0:1024].rearrange("p (b t) -> p b t", t=64)
                    qd_diag = qd[:, 0:1040].rearrange("p (b u) -> p b u", u=65)[:, :, 0:64:16]
                    rot = {"a": 0, "n": 0}

                    def bank_a():
                        i = rot["a"]
                        rot["a"] = (i + 1) % 2
                        return K.ps[i], K.psr[i]

                    def bank_n():
                        i = 3 + rot["n"]
                        rot["n"] = (rot["n"] + 1) % 3
                        return K.ps[i], K.psr[i]

                    def load_states(h):
                        K.dma(pool, CT0[:, :, 0:128], sCT_d[l, h], W=[ct0r])
                        K.dma(sp, C0[:], sC_d[l, h], W=c0r)
                        K.dma(sp, n0T[:], snT_d[l, h], W=[n0r])
                        K.dma(sp, n0[:], sn_d[l, h], W=[n0r])
                        K.op(dve, lambda e: e.tensor_copy(out=CT0[:, :, 128], in_=n0T[:]), R=[n0r, ct0r], W=[ct0r])

                    its = [(h, ci) for h in range(H) for ci in range(NCH)]
                    ctx = {}
                    wts = {}

                    def stageA(i):
                        h, ci = its[i]
                        c0, n = CHUNKS[ci]
                        issample = (ci == NCH - 1)
                        if ci == 0:
                            wts[h] = r8()
                        wt, wr = wts[h]
                        ti = min(c0 // 512, 4)
                        tc0, tn = TILES[ti]
                        qb = ti % 3

                        def qk_group(tj, part):
                            jc0, jn = TILES[tj]
                            if jn > 256:
                                lo_, hi_ = (0, 256) if part % 2 == 0 else (256, jn)
                            else:
                                if part % 2 == 1:
                                    return
                                lo_, hi_ = 0, jn
                            isk = part >= 2
                            wc = 128 if isk else 0
                            jb = tj % 3
                            pq, pqr = bank_a()
                            K.mm(pq[:, 0:hi_ - lo_], [(wt[:, kc, wc:wc + 128], xn[:, kc, jc0 + lo_:jc0 + hi_]) for kc in range(NKC)],
                                 R=[xnr[kc][tj] for kc in range(NKC)] + [wr], W=[pqr])
                            if isk:
                                K.op(dve, lambda e: e.tensor_scalar(out=qk[jb][:, 1, lo_:hi_], in0=pq[:, 0:hi_ - lo_], scalar1=KSCALE,
                                                                    scalar2=None, op0=ALU.mult), R=[pqr], W=[qkr[jb]])
                            else:
                                K.op(act, lambda e: e.activation(out=qk[jb][:, 0, lo_:hi_], in_=pq[:, 0:hi_ - lo_], func=AF.Copy),
                                     R=[pqr], W=[qkr[jb]])

                        if ci == 0:
                            for part in range(4):
                                qk_group(0, part)
                        if ci < 16:
                            qk_group(ti + 1, ci % 4)
                        lo = c0 - tc0
                        qT = qk[qb][:, 0, lo:lo + n]
                        kT = qk[qb][:, 1, lo:lo + n]
                        b = i % NB
                        tr_ = tokr[b]
                        ea = cols[0:n, ci, h:h + 1]
                        ew = cols[0:n, ci, 4 + h:5 + h]
                        fl = cols[0:n, ci, 8 + h:9 + h]
                        ctx[i] = dict(h=h, ci=ci, c0=c0, n=n, issample=issample, ti=ti, qb=qb, qT=qT, kT=kT, b=b, tr_=tr_,
                                      ea=ea, ew=ew, fl=fl)
                        pt, pr = bank_a()
                        K.mm(pt[0:n, 0:384], [(xn[:, kc, c0:c0 + n], wt[:, kc, 128:512]) for kc in range(NKC)],
                             R=[xnr[kc][ti] for kc in range(NKC)] + [wr], W=[pr])
                        K.op(act, lambda e: e.mul(out=ktok[b][0:n, :], in_=pt[0:n, 0:128], mul=KSCALE), R=[pr], W=[tr_[0]])
                        K.op(act, lambda e: e.activation(out=v1[b][0:n, 0:128], in_=pt[0:n, 128:256], func=AF.Copy), R=[pr], W=[tr_[1]])
                        eb = i % 2
                        K.op(act, lambda e: e.activation(out=eo[eb][0:n, :], in_=pt[0:n, 256:384], func=AF.Exp, scale=-1.0), R=[pr], W=[eor[eb]])
                        K.op(act, lambda e: e.activation(out=eo[eb][0:n, :], in_=eo[eb][0:n, :], func=AF.Ln, bias=1.0, scale=1.0),
                             R=[eor[eb]], W=[eor[eb]])
                        K.op(act, lambda e: e.activation(out=sgo[b][0:n, :], in_=eo[eb][0:n, :], func=AF.Exp, scale=-1.0),
                             R=[eor[eb]], W=[tr_[3]])
                        K.op(dve, lambda e: e.tensor_scalar(out=vwx[b][0:n, 0:128], in0=v1[b][0:n, 0:128], scalar1=ew, scalar2=None,
                                                            op0=ALU.mult), R=[tr_[1], colr], W=[tr_[2]])
                        K.op(act, lambda e: e.activation(out=vwx[b][0:n, 128:129], in_=ew, func=AF.Copy), R=[colr], W=[tr_[2]])
                        ps_, psr_ = K.ps[2], K.psr[2]
                        K.mm(ps_[0:n, 0:n], [(kT, qT)], R=[qkr[qb]], W=[psr_])
                        msk = smask_b if issample else causal_b[0:n, 0:n]
                        if n < 128:
                            K.op(dve, lambda e: e.memset(stw[b][:, 0:n], 0.0), W=[tr_[4]])
                        K.op(dve, lambda e: e.scalar_tensor_tensor(out=stw[b][0:n, 0:n], in0=ps_[0:n, 0:n], scalar=ea,
                                                                   in1=msk, op0=ALU.mult, op1=ALU.mult),
                             R=[psr_, colr, cbr], W=[tr_[4]])
                        if issample:
                            K.op(dve, lambda e: e.tensor_copy(out=qd_diag, in_=qT.rearrange("p (t b) -> p b t", b=NSQ)),
                                 R=[qkr[qb], qdr], W=[qdr])
                        if ci == NCH - 1:
                            r8rel()

                    def prompt_state(c):
                        h, ci, n, b, tr_ = c["h"], c["ci"], c["n"], c["b"], c["tr_"]
                        pu_, pur = K.ps[6], K.psr[6]
                        K.mm(pu_[:, 0:129], [(ktok[b][0:n, :], vwx[b][0:n, :])], R=[tr_[0], tr_[2]], W=[pur])
                        if ci == 0:
                            K.op(dve, lambda e: e.tensor_copy(out=CTn[:], in_=pu_[:, 0:129]), R=[pur, ctr], W=[ctr])
                        else:
                            K.op(dve, lambda e: e.scalar_tensor_tensor(out=CTn[:], in0=CTn[:], scalar=dcp[:, h, ci:ci + 1],
                                                                       in1=pu_[:, 0:129], op0=ALU.mult, op1=ALU.add),
                                 R=[pur, ctr, colr], W=[ctr])
                        if ci == NCH - 2:
                            K.dma(sp, pCT_d[l, h], CTn[:], R=[ctr])
                        else:
                            K.op(act, lambda e: e.activation(out=CTb[:], in_=CTn[:], func=AF.Copy), R=[ctr, ctbr], W=[ctbr])

                    def sample_state(c):
                        h, n, b, tr_, ew = c["h"], c["n"], c["b"], c["tr_"], c["ew"]
                        K.op(dve, lambda e: e.tensor_scalar(out=ewb[:], in0=ind_f, scalar1=ew, scalar2=None,
                                                            op0=ALU.mult), R=[cfr, colr], W=[ewbr])
                        pu_, pur = K.ps[6], K.psr[6]
                        K.mm(pu_[0:NSQ, 0:128], [(ewb[:], ktok[b][0:n, :])], R=[ewbr, tr_[0]], W=[pur])
                        K.op(dve, lambda e: e.scalar_tensor_tensor(out=n0[:], in0=n0[:], scalar=dcsT[:, h:h + 1],
                                                                   in1=pu_[0:NSQ, 0:128], op0=ALU.mult, op1=ALU.add),
                             R=[n0r, colr, pur], W=[n0r])
                        K.dma(sp, on_d[l, h], n0[:], R=[n0r])
                        for bb in range(NSQ):
                            K.op(dve, lambda e: e.tensor_scalar(out=vwb[bb][:], in0=vwx[b][0:n, 0:128],
                                                                scalar1=ind_f[:, bb:bb + 1], scalar2=None, op0=ALU.mult),
                                 R=[tr_[2], cfr], W=[vwbr[bb]])
                        bk = lambda bb: (K.ps[6], K.psr[6]) if bb % 2 == 0 else (K.ps[7], K.psr[7])

                        def mmb(bb):
                            pc_, pcr_ = bk(bb)
                            K.mm(pc_[:, 0:128], [(vwb[bb][:], ktok[b][0:n, :])], R=[vwbr[bb], tr_[0]], W=[pcr_])

                        mmb(0)
                        mmb(1)
                        for bb in range(NSQ):
                            pc_, pcr_ = bk(bb)
                            K.op(dve, lambda e: e.scalar_tensor_tensor(out=C0[:, bb, :], in0=C0[:, bb, :],
                                                                       scalar=dcs[:, h, bb:bb + 1], in1=pc_[:, 0:128],
                                                                       op0=ALU.mult, op1=ALU.add),
                                 R=[c0r[bb], colr, pcr_], W=[c0r[bb]])
                            if bb + 2 < NSQ:
                                mmb(bb + 2)
                        K.dma(sp, oC_d[l, h], C0[:], R=c0r)

                    def stageB(i):
                        c = ctx[i]
                        h, ci, n, b, tr_, qT, qb = c["h"], c["ci"], c["n"], c["b"], c["tr_"], c["qT"], c["qb"]
                        pn_, pnr = bank_n()
                        c["pn"] = (pn_, pnr)
                        if c["issample"]:
                            grp = [(pn_[0:n, 0:129], stw[b][:, 0:n], v1[b][:, :], True, False)]
                            for bb in range(NSQ):
                                grp.append((pn_[0:n, 0:129], qd_rows[:, bb, :], CT0[:, bb, 0:129], False, bb == NSQ - 1))
                            K.mm_multi(grp, R=[tr_[4], tr_[1], qdr, ct0r], W=[pnr])
                        elif ci == 0:
                            K.mm(pn_[0:n, 0:129], [(stw[b][:, 0:n], v1[b][:, :])], R=[tr_[4], tr_[1]], W=[pnr])
                        else:
                            K.mm(pn_[0:n, 0:129], [(stw[b][:, 0:n], v1[b][:, :]), (qT, CTb[:])],
                                 R=[tr_[4], tr_[1], qkr[qb], ctbr], W=[pnr])
                        pb3 = i % 3
                        c["pb3"] = pb3
                        jb = i % 2
                        K.op(act, lambda e: e.activation(out=junk[jb][0:n, :], in_=pn_[0:n, 0:128], func=AF.Square,
                                                         accum_out=pcol[pb3][0:n, 2:3]),
                             R=[pnr, postr[jb][2]], W=[pcr[pb3], postr[jb][2]])
                        K.op(act, lambda e: e.activation(out=pcol[pb3][0:n, 0:1], in_=pn_[0:n, 128:129], func=AF.Square),
                             R=[pnr, pcr[pb3]], W=[pcr[pb3]])
                        if not c["issample"]:
                            prompt_state(c)

                    def stageC1(i):
                        c = ctx[i]
                        n, fl = c["n"], c["fl"]
                        pn_, pnr = c["pn"]
                        pc = pcol[c["pb3"]]
                        r_ = pcr[c["pb3"]]
                        K.op(dve, lambda e: e.tensor_tensor(out=pc[0:n, 0:1], in0=pc[0:n, 0:1], in1=fl, op=ALU.max),
                             R=[colr, r_], W=[r_])
                        K.op(dve, lambda e: e.scalar_tensor_tensor(out=pc[0:n, 3:4], in0=pc[0:n, 0:1], scalar=DH * EPS, in1=pc[0:n, 2:3],
                                                                   op0=ALU.mult, op1=ALU.add), R=[r_], W=[r_])
                        K.op(act, lambda e: e.activation(out=pc[0:n, 4:5], in_=pc[0:n, 3:4], func=AF.Ln, scale=1.0 / DH), R=[r_], W=[r_])
                        K.op(act, lambda e: e.activation(out=pc[0:n, 5:6], in_=pc[0:n, 4:5], func=AF.Exp, scale=-0.5), R=[r_], W=[r_])

                    def stageC2a(i):
                        c = ctx[i]
                        h, ci, c0, n, b, tr_, ti = c["h"], c["ci"], c["c0"], c["n"], c["b"], c["tr_"], c["ti"]
                        pn_, pnr = c["pn"]
                        pc = pcol[c["pb3"]]
                        r_ = pcr[c["pb3"]]
                        pb = i % 2
                        po_ = postr[pb]
                        K.op(dve, lambda e: e.scalar_tensor_tensor(out=hgt[pb][0:n, :], in0=pn_[0:n, 0:128], scalar=pc[0:n, 5:6],
                                                                   in1=gml[0:n, h * 128:(h + 1) * 128],
                                                                   op0=ALU.mult, op1=ALU.mult),
                             R=[pnr, r_, gmlr, po_[0]], W=[po_[0]])
                        K.op(dve, lambda e: e.tensor_tensor(out=hmt[pb][0:n, :], in0=hgt[pb][0:n, :], in1=sgo[b][0:n, :],
                                                            op=ALU.mult), R=[po_[0], tr_[3], po_[1]], W=[po_[1]])

                    def stageC2b(i):
                        c = ctx.pop(i)
                        h, c0, n, ti = c["h"], c["c0"], c["n"], c["ti"]
                        pb = i % 2
                        po_ = postr[pb]
                        ph_, phr = K.ps[7], K.psr[7]
                        phb = ph_[:].bitcast(BF16)
                        K.tr(phb[:, 0:n], hmt[pb][0:n, :], ident_b[0:n, 0:n], R=[po_[1], cbr], W=[phr])
                        K.op(act, lambda e: e.activation(out=mixin[:, h, c0:c0 + n], in_=phb[:, 0:n], func=AF.Copy),
                             R=[phr], W=[mixr[h][ti]])
                        if c["issample"]:
                            sample_state(c)
                            if h + 1 < H:
                                load_states(h + 1)

                    load_states(0)
                    NI = len(its)
                    for i in range(NI + 5):
                        if 0 <= i - 5 < NI:
                            stageC2b(i - 5)
                        if 0 <= i - 2 < NI:
                            stageB(i - 2)
                        if 0 <= i - 3 < NI:
                            stageC1(i - 3)
                        if 0 <= i - 4 < NI:
                            stageC2a(i - 4)
                        if i < NI:
                            stageA(i)
                K.barrier()
                if stage < 4:
                    return
                with ExitStack() as ls:
                    S2 = lambda nm, shp, dt=F32: ls.enter_context(nc.sbuf_tensor(UN(nm), list(shp), dt))
                    wrg = S2("l_wrg", [128, 2, 4, 128], BF16)
                    wrgr = Res()
                    K.dma(pool, wrg[:], wrg_d[l], W=[wrgr])
                    wl = [r8(), r8()]
                    sc = S2("l_sc", [128, 4])
                    scr = Res()
                    K.op(act, lambda e: e.activation(out=sc[:], in_=V("lru_lambda", l, 0, 4), func=AF.Exp, scale=-1.0),
                         R=[vecr], W=[scr])
                    K.op(act, lambda e: e.activation(out=sc[:], in_=sc[:], func=AF.Ln, bias=1.0, scale=1.0), R=[scr], W=[scr])
                    K.op(act, lambda e: e.mul(out=sc[:], in_=sc[:], mul=-8.0), R=[scr], W=[scr])
                    sc2 = S2("l_sc2", [128, 4])
                    K.op(act, lambda e: e.mul(out=sc2[:], in_=sc[:], mul=2.0), R=[scr], W=[scr])
                    nbg = S2("l_nbg", [128, 8])
                    K.op(act, lambda e: e.mul(out=nbg[:, 0:4], in_=V("b_rg_a", l, 0, 4), mul=-1.0), R=[vecr, scr], W=[scr])
                    K.op(act, lambda e: e.mul(out=nbg[:, 4:8], in_=V("b_rg_x", l, 0, 4), mul=-1.0), R=[vecr, scr], W=[scr])
                    mk = lambda nm, n_, shp, dt=F32: ([S2(f"{nm}{i}", shp, dt) for i in range(n_)], [Res() for _ in range(n_)])
                    xet, xetr = mk("l_xet", 2, [128, 3 + 512])
                    ygf, ygfr = mk("l_ygf", 2, [128, 512])
                    xc, xcr = mk("l_xc", 3, [128, 512])
                    xcb, xcbr = mk("l_xcb", 2, [128, 512], BF16)
                    tmp, tmpr = mk("l_tmp", 2, [128, 512])
                    glb, glbr = mk("l_glb", 3, [128, 512], BF16)
                    gr, grr = mk("l_gr", 2, [128, 512])
                    gi, gir = mk("l_gi", 2, [128, 512])
                    ts_, tsr = mk("l_ts", 2, [128, 512])
                    hcur, hcurr = mk("l_hc", 2, [128, 512])
                    sqh, sqhr = mk("l_sqh", 2, [128, 512], BF16)
                    car = S2("l_car", [128, 4, 3])
                    carr = RL(4)
                    xes = S2("l_xes", [128, 4, 7, NSQ])
                    xesr = RL(4)
                    hcar = S2("l_hcar", [128, 4])
                    hcr = RL(4)
                    h0s = S2("l_h0s", [128, 4, NSQ])
                    h0r = Res()
                    hs_o = S2("l_hso", [128, 4, NSQ])
                    php = S2("l_php", [128, 4])
                    hsor = Res()
                    rsl = S2("l_rsl", [128, 512])
                    rslr = Res()
                    K.dma(sp, xes[:, :, 0:3, :], scv_d[l], W=xesr)
                    K.dma(sp, h0s[:], sh_d[l], W=[h0r])
                    K.op(dve, lambda e: e.memset(car[:], 0.0), W=carr)
                    items = [(ti, cc) for ti in range(5) for cc in range(4)]
                    NIT = len(items)
                    rotx = {"x": 0, "y": 0}

                    def T1(k):
                        ti, cc = items[k]
                        c0, n = TILES[ti]
                        npr = min(n, TP - c0)
                        b2 = k % 2
                        wt, wr = wl[cc // 2]
                        o_ = (cc % 2) * 256
                        px, pxr = K.ps[rotx["x"]], K.psr[rotx["x"]]
                        rotx["x"] = (rotx["x"] + 1) % 2
                        py, pyr = K.ps[2 + rotx["y"]], K.psr[2 + rotx["y"]]
                        rotx["y"] = (rotx["y"] + 1) % 2
                        rr = [xnr[kc][ti] for kc in range(NKC)] + [wr]
                        K.mm(px[:, 0:n], [(wt[:, kc, o_:o_ + 128], xn[:, kc, c0:c0 + n]) for kc in range(NKC)], R=rr, W=[pxr])
                        K.mm(py[:, 0:n], [(wt[:, kc, o_ + 128:o_ + 256], xn[:, kc, c0:c0 + n]) for kc in range(NKC)], R=rr, W=[pyr])
                        K.op(act, lambda e: e.activation(out=xet[b2][:, 3:3 + npr], in_=px[:, 0:npr], func=AF.Copy),
                             R=[pxr], W=[xetr[b2]])
                        K.op(dve, lambda e: e.tensor_copy(out=xet[b2][:, 0:3], in_=car[:, cc, :]), R=[carr[cc]], W=[xetr[b2]])
                        K.op(act, lambda e: e.activation(out=ygf[b2][:, 0:n], in_=py[:, 0:n], func=AF.Copy), R=[pyr], W=[ygfr[b2]])
                        if ti == 4:
                            K.op(act, lambda e: e.activation(
                                out=xes[:, cc, 3:7, :], in_=px[:, npr:n].rearrange("p (t b) -> p t b", b=NSQ), func=AF.Copy),
                                R=[pxr], W=[xesr[cc]])
                            K.dma(sp, pcv_d[l, :, cc, :], xet[b2][:, npr:npr + 3], R=[xetr[b2]])
                            K.dma(sp, ocv_d[l, :, cc], xes[:, cc, 4:7, :], R=[xesr[cc]])
                        else:
                            K.op(dve, lambda e: e.tensor_copy(out=car[:, cc, :], in_=xet[b2][:, n:n + 3]), R=[xetr[b2]], W=[carr[cc]])

                    def T2(k):
                        ti, cc = items[k]
                        c0, n = TILES[ti]
                        npr = min(n, TP - c0)
                        b2, b3 = k % 2, k % 3
                        cw = lambda j: V("conv_w", l, j * 4 + cc)
                        K.op(dve, lambda e: e.tensor_scalar(out=xc[b3][:, 0:npr], in0=xet[b2][:, 3:3 + npr], scalar1=cw(3),
                                                            scalar2=V("conv_b", l, cc), op0=ALU.mult, op1=ALU.add),
                             R=[xetr[b2], vecr], W=[xcr[b3]])
                        for j in range(1, 4):
                            K.op(dve, lambda e: e.scalar_tensor_tensor(out=xc[b3][:, 0:npr], in0=xet[b2][:, 3 - j:3 - j + npr],
                                                                       scalar=cw(3 - j), in1=xc[b3][:, 0:npr],
                                                                       op0=ALU.mult, op1=ALU.add),
                                 R=[xetr[b2], vecr, xcr[b3]], W=[xcr[b3]])
                        if ti == 4:
                            xcs = xc[b3][:, npr:n].rearrange("p (t b) -> p t b", b=NSQ)
                            K.op(dve, lambda e: e.tensor_scalar(out=xcs, in0=xes[:, cc, 3:7, :], scalar1=cw(3),
                                                                scalar2=V("conv_b", l, cc), op0=ALU.mult, op1=ALU.add),
                                 R=[xesr[cc], vecr, xcr[b3]], W=[xcr[b3]])
                            for j in range(1, 4):
                                K.op(dve, lambda e: e.scalar_tensor_tensor(out=xcs, in0=xes[:, cc, 3 - j:7 - j, :],
                                                                           scalar=cw(3 - j), in1=xcs, op0=ALU.mult, op1=ALU.add),
                                     R=[xesr[cc], vecr, xcr[b3]], W=[xcr[b3]])
                        K.op(dve, lambda e: e.tensor_copy(out=xcb[b2][:, 0:n], in_=xc[b3][:, 0:n]), R=[xcr[b3]], W=[xcbr[b2]])
                        pa, par = K.ps[4], K.psr[4]
                        pi2, pir2 = K.ps[5], K.psr[5]
                        K.mm(pa[:, 0:n], [(wrg[:, 0, cc, :], xcb[b2][:, 0:n])], R=[wrgr, xcbr[b2]], W=[par])
                        K.mm(pi2[:, 0:n], [(wrg[:, 1, cc, :], xcb[b2][:, 0:n])], R=[wrgr, xcbr[b2]], W=[pir2])
                        K.op(dve, lambda e: e.tensor_tensor(out=tmp[b2][:, 0:n], in0=ygf[b2][:, 0:n], in1=ygf[b2][:, 0:n], op=ALU.mult),
                             R=[ygfr[b2]], W=[tmpr[b2]])
                        K.op(dve, lambda e: e.tensor_scalar(out=tmp[b2][:, 0:n], in0=tmp[b2][:, 0:n], scalar1=0.044715, scalar2=1.0,
                                                            op0=ALU.mult, op1=ALU.add), R=[tmpr[b2]], W=[tmpr[b2]])
                        K.op(dve, lambda e: e.tensor_tensor(out=tmp[b2][:, 0:n], in0=tmp[b2][:, 0:n], in1=ygf[b2][:, 0:n], op=ALU.mult),
                             R=[tmpr[b2], ygfr[b2]], W=[tmpr[b2]])
                        K.op(act, lambda e: e.activation(out=tmp[b2][:, 0:n], in_=tmp[b2][:, 0:n], func=AF.Exp,
                                                         scale=-1.5957691216057308), R=[tmpr[b2]], W=[tmpr[b2]])
                        K.op(act, lambda e: e.activation(out=tmp[b2][:, 0:n], in_=tmp[b2][:, 0:n], func=AF.Ln, bias=1.0, scale=1.0),
                             R=[tmpr[b2]], W=[tmpr[b2]])
                        K.op(act, lambda e: e.activation(out=tmp[b2][:, 0:n], in_=tmp[b2][:, 0:n], func=AF.Exp, scale=-1.0),
                             R=[tmpr[b2]], W=[tmpr[b2]])
                        K.op(dve, lambda e: e.tensor_tensor(out=glb[b3][:, 0:n], in0=ygf[b2][:, 0:n], in1=tmp[b2][:, 0:n], op=ALU.mult),
                             R=[tmpr[b2], ygfr[b2]], W=[glbr[b3]])

                    def T3(k):
                        ti, cc = items[k]
                        c0, n = TILES[ti]
                        b2 = k % 2
                        pa, par = K.ps[4], K.psr[4]
                        pi2, pir2 = K.ps[5], K.psr[5]
                        g_, g_r, i_, i_r, t_, t_r = gr[b2], grr[b2], gi[b2], gir[b2], ts_[b2], tsr[b2]
                        K.op(act, lambda e: e.activation(out=g_[:, 0:n], in_=pa[:, 0:n], func=AF.Exp,
                                                         bias=nbg[:, cc:cc + 1], scale=-1.0), R=[par, scr], W=[g_r])
                        K.op(act, lambda e: e.activation(out=i_[:, 0:n], in_=pi2[:, 0:n], func=AF.Exp,
                                                         bias=nbg[:, 4 + cc:5 + cc], scale=-1.0), R=[pir2, scr], W=[i_r])
                        K.op(act, lambda e: e.activation(out=g_[:, 0:n], in_=g_[:, 0:n], func=AF.Ln, bias=1.0, scale=1.0), R=[g_r], W=[g_r])
                        K.op(act, lambda e: e.activation(out=i_[:, 0:n], in_=i_[:, 0:n], func=AF.Ln, bias=1.0, scale=1.0), R=[i_r], W=[i_r])
                        K.op(act, lambda e: e.activation(out=g_[:, 0:n], in_=g_[:, 0:n], func=AF.Exp, scale=-1.0), R=[g_r], W=[g_r])
                        K.op(act, lambda e: e.activation(out=i_[:, 0:n], in_=i_[:, 0:n], func=AF.Exp, scale=-1.0), R=[i_r], W=[i_r])
                        K.op(act, lambda e: e.activation(out=t_[:, 0:n], in_=g_[:, 0:n], func=AF.Exp, scale=sc2[:, cc:cc + 1]),
                             R=[g_r, scr], W=[t_r])
                        K.op(act, lambda e: e.activation(out=g_[:, 0:n], in_=g_[:, 0:n], func=AF.Exp, scale=sc[:, cc:cc + 1]),
                             R=[g_r, scr], W=[g_r])
                        K.op(act, lambda e: e.activation(out=t_[:, 0:n], in_=t_[:, 0:n], func=AF.Ln, scale=-1.0, bias=1.0), R=[t_r], W=[t_r])
                        K.op(act, lambda e: e.activation(out=t_[:, 0:n], in_=t_[:, 0:n], func=AF.Exp, scale=0.5), R=[t_r], W=[t_r])

                    def T4(k):
                        ti, cc = items[k]
                        c0, n = TILES[ti]
                        npr = min(n, TP - c0)
                        b2, b3 = k % 2, k % 3
                        g_, g_r, i_, i_r, t_, t_r = gr[b2], grr[b2], gi[b2], gir[b2], ts_[b2], tsr[b2]
                        hc, hcr_ = hcur[b2], hcurr[b2]
                        K.op(dve, lambda e: e.tensor_tensor(out=i_[:, 0:n], in0=i_[:, 0:n], in1=xc[b3][:, 0:n], op=ALU.mult),
                             R=[i_r, xcr[b3]], W=[i_r])
                        K.op(dve, lambda e: e.tensor_tensor(out=i_[:, 0:n], in0=i_[:, 0:n], in1=t_[:, 0:n], op=ALU.mult),
                             R=[i_r, t_r], W=[i_r])
                        init = 0.0 if ti == 0 else hcar[:, cc:cc + 1]
                        K.op(dve, lambda e: e.tensor_tensor_scan(out=hc[:, 0:npr], data0=g_[:, 0:npr], data1=i_[:, 0:npr],
                                                                 initial=init, op0=ALU.mult, op1=ALU.add),
                             R=[g_r, i_r, hcr[cc]], W=[hcr_])
                        if ti < 4:
                            K.op(dve, lambda e: e.tensor_copy(out=hcar[:, cc:cc + 1], in_=hc[:, n - 1:n]), R=[hcr_], W=[hcr[cc]])
                        else:
                            K.op(act, lambda e: e.activation(out=php[:, cc:cc + 1], in_=hc[:, npr - 1:npr], func=AF.Copy),
                                 R=[hcr_], W=[hsor])
                            for t in range(NST):
                                s0 = npr + 16 * t
                                prev = h0s[:, cc, :] if t == 0 else hc[:, s0 - 16:s0]
                                K.op(dve, lambda e: e.tensor_tensor(out=hc[:, s0:s0 + 16], in0=g_[:, s0:s0 + 16], in1=prev,
                                                                    op=ALU.mult), R=[g_r, h0r, hcr_], W=[hcr_])
                                K.op(dve, lambda e: e.tensor_tensor(out=hc[:, s0:s0 + 16], in0=hc[:, s0:s0 + 16],
                                                                    in1=i_[:, s0:s0 + 16], op=ALU.add), R=[i_r, hcr_], W=[hcr_])
                            K.op(act, lambda e: e.activation(out=hs_o[:, cc, :], in_=hc[:, npr + 48:npr + 64], func=AF.Copy),
                                 R=[hcr_], W=[hsor])
                        K.op(dve, lambda e: e.tensor_tensor(out=sqh[b2][:, 0:n], in0=hc[:, 0:n], in1=hc[:, 0:n], op=ALU.mult), R=[hcr_], W=[sqhr[b2]])
                        pt, pr = K.ps[6], K.psr[6]
                        K.mm_multi([(pt[:, 0:n], ones_b, sqh[b2][:, 0:n], cc == 0, cc == 3)], R=[sqhr[b2], cbr], W=[pr])
                        K.op(dve, lambda e: e.tensor_tensor(out=mixin[:, 4 + cc, c0:c0 + n], in0=hc[:, 0:n], in1=glb[b3][:, 0:n],
                                                            op=ALU.mult), R=[hcr_, glbr[b3]], W=[mixr[4 + cc][ti]])
                        if cc == 3:
                            K.op(act, lambda e: e.activation(out=rsl[:, 0:n], in_=pt[:, 0:n], func=AF.Ln, scale=1.0 / DLRU,
                                                             bias=epsc[:, 0:1]), R=[pr, cbr], W=[rslr])
                            K.op(act, lambda e: e.activation(out=rsl[:, 0:n], in_=rsl[:, 0:n], func=AF.Exp, scale=-0.5),
                                 R=[rslr], W=[rslr])

                    def T5(k):
                        ti, cc = items[k]
                        if cc != 3:
                            return
                        c0, n = TILES[ti]
                        for c2 in range(4):
                            K.op(dve, lambda e: e.scalar_tensor_tensor(out=mixin[:, 4 + c2, c0:c0 + n], in0=mixin[:, 4 + c2, c0:c0 + n],
                                                                       scalar=V("g_lru_out", l, c2), in1=rsl[:, 0:n],
                                                                       op0=ALU.mult, op1=ALU.mult),
                                 R=[mixr[4 + c2][ti], vecr, rslr], W=[mixr[4 + c2][ti]])

                    for m in range(NIT + 4):
                        if 0 <= m - 4 < NIT:
                            T5(m - 4)
                        if 0 <= m - 3 < NIT:
                            T4(m - 3)
                        if 0 <= m - 2 < NIT:
                            T3(m - 2)
                        if 0 <= m - 1 < NIT:
                            T2(m - 1)
                        if m < NIT:
                            T1(m)
                    K.dma(sp, oh_d[l], hs_o[:], R=[hsor])
                    K.dma(sp, ph_d[l], php[:], R=[hsor])
                    r8rel()
                    r8rel()
                K.barrier()
                if dbg_d is not None and l == 0:
                    K.dma(sp, dbg_d, mixin[:], R=mixr)
                if stage < 5:
                    return
                wo = [r8(), r8()]
                with ExitStack() as os_:
                    S2 = lambda nm, shp, dt=F32: os_.enter_context(nc.sbuf_tensor(UN(nm), list(shp), dt))
                    if post_norm:
                        sq = S2("w_sq", [128, NKC, 512], BF16)
                        sqr = Res()
                        rs = [S2("w_rs0", [128, 512]), S2("w_rs1", [128, 512])]
                        rsr = [Res(), Res()]
                    for ti, (c0, n) in enumerate(TILES):
                        for m in range(NKC):
                            wt, wr = wo[m // 4]
                            mo = m % 4
                            pt, pr = K.psn()
                            K.mm(pt[:, 0:n], [(wt[:, kc, mo * 128:(mo + 1) * 128], mixin[:, kc, c0:c0 + n]) for kc in range(NKC)],
                                 R=[mixr[kc][ti] for kc in range(NKC)] + [wr], W=[pr])
                            K.op(dve, lambda e: e.tensor_tensor(out=x[:, m, c0:c0 + n], in0=pt[:, 0:n], in1=x[:, m, c0:c0 + n],
                                                                op=ALU.add), R=[pr, xr_[m][ti]], W=[xr_[m][ti]])
                        if post_norm and ti >= 1:
                            rmsnorm("g_ff2", l, xn, xnr, sq, sqr, rs, rsr, tiles=[ti - 1])
                    r8rel()
                    r8rel()
                    if post_norm:
                        rmsnorm("g_ff2", l, xn, xnr, sq, sqr, rs, rsr, tiles=[len(TILES) - 1])
            K.barrier()

        ones_f4 = SB("ones_f4", [H, 128])
        K.op(dve, lambda e: e.memset(ones_f4[:], 1.0), W=[cbr])

        MERGE = (stage >= 99) and not skip_ffn
        if MERGE:
            def post_norm_fn(gname, l_):
                return lambda sq, sqr, rs, rsr, ti: rmsnorm(gname, l_, xn, xnr, sq, sqr, rs, rsr, tiles=[ti])

            def post_final(sq, sqr, rs, rsr, ti):
                rmsnorm("g_final", 0, x, xr_, sq, sqr, rs, rsr, tiles=[ti])
                c0, n = TILES[ti]
                for kc in range(NKC):
                    K.dma(sp, yT_d[:, kc, c0:c0 + n], x[:, kc, c0:c0 + n], R=[xr_[kc][ti]])

            for l in range(DEPTH):
                ffn(l, 0, pre_norm=(l == 0), post=post_norm_fn("g_mix", l))
                mixer(l, pre_norm=False, post_norm=True)
                ffn(l, 1, pre_norm=False, post=(post_norm_fn("g_ff1", l + 1) if l + 1 < DEPTH else post_final))
            K.finish()
        else:
            for l in range(DEPTH):
                if stage >= 1 and not skip_ffn:
                    ffn(l, 0)
                if stage >= 2:
                    mixer(l)
                if stage >= 6 and not skip_ffn:
                    ffn(l, 1)
                if stage < 7:
                    break
            with ExitStack() as fs:
                S = lambda nm, shp, dt=F32: fs.enter_context(nc.sbuf_tensor(UN(nm), list(shp), dt))
                sq = S("o_sq", [128, NKC, 512], BF16)
                sqr = Res()
                rs = [S("o_rs0", [128, 512]), S("o_rs1", [128, 512])]
                rsr = [Res(), Res()]
                rmsnorm("g_final", 0, x, xr_, sq, sqr, rs, rsr)
                for kc in range(NKC):
                    K.dma(sp, yT_d[:, kc, :], x[:, kc, :], R=xr_[kc])
                K.finish()
    return nc


def _consts():
    cf = np.zeros((128, 512), np.float32)
    cf[:, 0:128] = np.eye(128, dtype=np.float32)
    s = np.arange(128)
    cf[:, 128:256] = (s[:, None] <= s[None, :]).astype(np.float32)
    i = np.arange(64)
    same = (i[:, None] % 16) == (i[None, :] % 16)
    caus = (i[:, None] // 16) <= (i[None, :] // 16)
    cf[0:64, 256:320] = (same & caus).astype(np.float32)
    cf[0:64, 320:336] = ((i[:, None] % 16) == np.arange(16)[None, :]).astype(np.float32)
    return cf


def _prep_shared(W):
    f = lambda a: np.ascontiguousarray(a, dtype=np.float32)
    out = {}
    wgu = np.empty((DEPTH, 2, NJP, 128, NKC, 512), np.float32)
    wdn = np.empty((DEPTH, 2, NJP, 128, 2, 1024), np.float32)
    for fi, (gn, un, dn) in enumerate((("w_ff1_gate", "w_ff1_up", "w_ff1_down"), ("w_ff2_gate", "w_ff2_up", "w_ff2_down"))):
        g = W[gn].reshape(DEPTH, NKC, 128, NJP, 256).transpose(0, 3, 2, 1, 4)
        u = W[un].reshape(DEPTH, NKC, 128, NJP, 256).transpose(0, 3, 2, 1, 4)
        wgu[:, fi, :, :, :, 0:256] = g
        wgu[:, fi, :, :, :, 256:512] = u
        wdn[:, fi] = W[dn].reshape(DEPTH, NJP, 2, 128, 1024).transpose(0, 1, 3, 2, 4)
    out["wgu"] = wgu
    out["wdn"] = wdn
    win = W["w_in"].reshape(DEPTH, NKC, 128, 3080)
    wq = np.empty((DEPTH, H, 128, NKC, 512), np.float32)
    for h in range(H):
        for qi in range(4):
            wq[:, h, :, :, qi * 128:(qi + 1) * 128] = win[:, :, :, qi * 512 + h * 128: qi * 512 + (h + 1) * 128].transpose(0, 2, 1, 3)
    out["win"] = wq
    out["wgt"] = f(win[:, :, :, 2048:2056].transpose(0, 2, 1, 3))
    wl = np.empty((DEPTH, 2, 128, NKC, 512), np.float32)
    for cc in range(4):
        o = (cc % 2) * 256
        wl[:, cc // 2, :, :, o:o + 128] = win[:, :, :, 2056 + cc * 128:2056 + (cc + 1) * 128].transpose(0, 2, 1, 3)
        wl[:, cc // 2, :, :, o + 128:o + 256] = win[:, :, :, 2568 + cc * 128:2568 + (cc + 1) * 128].transpose(0, 2, 1, 3)
    out["wlr"] = wl
    wrg = np.zeros((DEPTH, 128, 2, 4, 128), np.float32)
    for gi, nm in enumerate(("w_rg_a", "w_rg_x")):
        for nb in range(8):
            cc, half = nb // 2, nb % 2
            wrg[:, half * 64:(half + 1) * 64, gi, cc, half * 64:(half + 1) * 64] = W[nm][:, nb]
    out["wrg"] = wrg
    out["wout"] = f(W["w_out"].reshape(DEPTH, NKC, 128, 2, 512).transpose(0, 3, 2, 1, 4))
    vec = np.zeros((128, NV), np.float32)
    for l in range(DEPTH):
        for nm in ("g_ff1", "g_mix", "g_ff2"):
            vec[:, VOFF[(nm, l)]:VOFF[(nm, l)] + 8] = W[nm][l].reshape(8, 128).T
        o = VOFF[("conv_w", l)]
        vec[:, o:o + 16] = W["conv_w"][l].reshape(4, 4, 128).transpose(2, 0, 1).reshape(128, 16)
        for nm in ("conv_b", "b_rg_a", "b_rg_x", "lru_lambda", "g_lru_out"):
            vec[:, VOFF[(nm, l)]:VOFF[(nm, l)] + 4] = W[nm][l].reshape(4, 128).T
        vec[0:4, VOFF[("b_i", l)]] = W["b_gates"][l, 0:4]
        vec[0:4, VOFF[("b_f", l)]] = W["b_gates"][l, 4:8]
    vec[:, VOFF[("g_final", 0)]:VOFF[("g_final", 0)] + 8] = W["g_final"].reshape(8, 128).T
    out["vec"] = vec
    out["gml"] = f(np.broadcast_to(W["g_mlstm_out"][None], (128, DEPTH, 512)))
    out["cf"] = _consts()
    return out


def _prep_core(c, A):
    sl = slice(NSQ * c, NSQ * (c + 1))
    X = np.concatenate([A["meta_tokens"], A["x_prompt"][c],
                        A["x_sample"][sl].transpose(1, 0, 2).reshape(NS, D)], axis=0)
    m = {}
    m["xT"] = np.ascontiguousarray(X.T.reshape(NKC, 128, T).transpose(1, 0, 2))
    C = A["state_mlstm_C"][:, sl]
    m["sCT"] = np.ascontiguousarray(C.transpose(0, 2, 4, 1, 3))
    m["sC"] = np.ascontiguousarray(C.transpose(0, 2, 3, 1, 4))
    n = A["state_mlstm_n"][:, sl]
    m["snT"] = np.ascontiguousarray(n.transpose(0, 2, 3, 1))
    m["sn"] = np.ascontiguousarray(n.transpose(0, 2, 1, 3))
    m["sm"] = np.ascontiguousarray(A["state_mlstm_m"][:, sl].transpose(0, 2, 1))
    m["sh"] = np.ascontiguousarray(A["state_lru_h"][:, sl].reshape(DEPTH, NSQ, 4, 128).transpose(0, 3, 2, 1))
    m["scv"] = np.ascontiguousarray(A["state_conv"][:, sl].reshape(DEPTH, NSQ, 3, 4, 128).transpose(0, 4, 3, 2, 1))
    return m


_NC_CACHE = {}


def kernel(**inputs):
    A = {k: np.asarray(v, dtype=np.float32) for k, v in inputs.items()}
    shared = _prep_shared(A)
    in_maps = []
    for c in range(NCORES):
        m = dict(shared)
        m.update(_prep_core(c, A))
        in_maps.append(m)
    if "nc" not in _NC_CACHE:
        _NC_CACHE["nc"] = build_program()
    nc = _NC_CACHE["nc"]
    res = run_bass_kernel_spmd(nc, in_maps, core_ids=list(range(NCORES)))
    R = res.results
    B = NCORES
    y_prompt = np.empty((B, SEQ, D), np.float32)
    y_sample = np.empty((B * NSQ, NST, D), np.float32)
    pC = np.empty((DEPTH, B, H, DH, DH), np.float32)
    pn = np.empty((DEPTH, B, H, DH), np.float32)
    pm = np.empty((DEPTH, B, H), np.float32)
    ph = np.empty((DEPTH, B, DLRU), np.float32)
    pcv = np.empty((DEPTH, B, 3, DLRU), np.float32)
    sC = np.empty((DEPTH, B * NSQ, H, DH, DH), np.float32)
    sn = np.empty((DEPTH, B * NSQ, H, DH), np.float32)
    sm = np.empty((DEPTH, B * NSQ, H), np.float32)
    sh = np.empty((DEPTH, B * NSQ, DLRU), np.float32)
    scv = np.empty((DEPTH, B * NSQ, 3, DLRU), np.float32)
    for c in range(B):
        r = R[c]
        sl = slice(NSQ * c, NSQ * (c + 1))
        Y = r["yT"].transpose(1, 0, 2).reshape(D, T).T
        y_prompt[c] = Y[NMETA:TP]
        y_sample[sl] = Y[TP:].reshape(NST, NSQ, D).transpose(1, 0, 2)
        pct = r["pCT"]
        pC[:, c] = pct[:, :, :, 0:128].transpose(0, 1, 3, 2)
        pn[:, c] = pct[:, :, :, 128]
        pm[:, c] = r["pm"][:, :, 0]
        ph[:, c] = r["ph"].transpose(0, 2, 1).reshape(DEPTH, DLRU)
        pcv[:, c] = r["pcv"].transpose(0, 3, 2, 1).reshape(DEPTH, 3, DLRU)
        sC[:, sl] = r["oC"].transpose(0, 3, 1, 2, 4)
        sn[:, sl] = r["on"].transpose(0, 2, 1, 3)
        sm[:, sl] = r["om"].transpose(0, 2, 1)
        sh[:, sl] = r["oh"].transpose(0, 3, 2, 1).reshape(DEPTH, NSQ, DLRU)
        scv[:, sl] = r["ocv"].transpose(0, 4, 3, 2, 1).reshape(DEPTH, NSQ, 3, DLRU)
    return (y_prompt, y_sample, pC, pn, pm, ph, pcv, sC, sn, sm, sh, scv)
```

```python
import os
import numpy as np
import ml_dtypes
KDBG = int(os.environ.get('KDBG', '9'))
KNOS = int(os.environ.get('KNOS', '0'))
KSKIP = int(os.environ.get('KSKIP', '0'))
KCH = int(os.environ.get('KCH', '99'))
KSELF = int(os.environ.get('KSELF', '1'))
from contextlib import ExitStack
import concourse.bass as bass
import concourse.mybir as mybir
from concourse.bass_utils import run_bass_kernel_spmd

F32 = mybir.dt.float32
BF16 = mybir.dt.bfloat16
AF = mybir.ActivationFunctionType
ALU = mybir.AluOpType

NCORES = 8
D = 1024
NKC = 8
SEQ = 2048
NMETA = 16
TP = NMETA + SEQ
NSQ = 16
NST = 4
NS = NSQ * NST
T = TP + NS
DFF = 2816
NJ = DFF // 128
NJP = NJ // 2
H = 4
DH = 128
DLRU = 512
DEPTH = 2
EPS = 1e-6
TILES = [(0, 512), (512, 512), (1024, 512), (1536, 512), (2048, 80)]
CHUNKS = [(128 * c, 128) for c in range(16)] + [(2048, 16), (TP, NS)]
NCH = len(CHUNKS)
GROUPS = [[0, 1, 2], [3, 4, 5], [6, 7, 8], [9, 10]]
KSCALE = DH ** -0.5

def _vec_layout():
    off = {}
    n = 0
    for l in range(DEPTH):
        for nm, w in (("g_ff1", 8), ("g_mix", 8), ("g_ff2", 8), ("conv_w", 16), ("conv_b", 4),
                      ("b_rg_a", 4), ("b_rg_x", 4), ("lru_lambda", 4), ("g_lru_out", 4),
                      ("b_i", 1), ("b_f", 1)):
            off[(nm, l)] = n
            n += w
    off[("g_final", 0)] = n
    n += 8
    return off, n

VOFF, NV = _vec_layout()


class Res:
    __slots__ = ("w", "r")

    def __init__(self):
        self.w = None
        self.r = []


def RL(*dims):
    if len(dims) == 0:
        return Res()
    return [RL(*dims[1:]) for _ in range(dims[0])]


def flat(x):
    if isinstance(x, Res):
        return [x]
    out = []
    for y in x:
        out.extend(flat(y))
    return out


class Eng:
    def __init__(self, e, sem, name):
        self.e = e
        self.sem = sem
        self.n = 0
        self.seen = {}
        self.name = name


class Ctx:
    def __init__(self, nc, es):
        self.nc = nc
        self.es = es
        mk = lambda nm: es.enter_context(nc.semaphore(nm))
        self.pe = Eng(nc.tensor, mk("c_pe"), "pe")
        self.act = Eng(nc.scalar, mk("c_act"), "act")
        self.dve = Eng(nc.vector, mk("c_dve"), "dve")
        self.pool = Eng(nc.gpsimd, mk("c_pool"), "pool")
        self.sp = Eng(nc.sync, None, "sp")
        self.compute = [self.pe, self.act, self.dve]
        self.dsems = {}
        for q, nq in ((self.sp, 16), (self.pool, 16)):
            self.dsems[q.name] = [[mk(f"d_{q.name}{i}"), 0] for i in range(nq)]
        self.dptr = {"sp": 0, "pool": 0}
        self.semid = {}
        self.ps = []
        self.psr = []
        for i in range(8):
            self.ps.append(es.enter_context(nc.psum_tensor(f"ps{i}", [128, 512], F32)))
            self.psr.append(Res())
        self.psi = 0

    def sid(self, sem):
        k = id(sem)
        if k not in self.semid:
            self.semid[k] = sem
        return k

    def psn(self):
        i = self.psi
        self.psi = (self.psi + 1) % 8
        return self.ps[i], self.psr[i]

    def _waits(self, eng, R, W):
        deps = {}
        for r in R:
            if r.w is not None:
                k = self.sid(r.w[0])
                deps[k] = max(deps.get(k, 0), r.w[1])
        for w in W:
            if w.w is not None:
                k = self.sid(w.w[0])
                deps[k] = max(deps.get(k, 0), w.w[1])
            for t in w.r:
                k = self.sid(t[0])
                deps[k] = max(deps.get(k, 0), t[1])
        for k, v in deps.items():
            if not KSELF and eng.sem is not None and k == id(eng.sem):
                continue
            if eng.seen.get(k, 0) < v:
                eng.e.wait_ge(self.semid[k], v)
                eng.seen[k] = v

    def _done(self, tok, R, W):
        for r in R:
            r.r.append(tok)
        for w in W:
            w.w = tok
            w.r = []

    def op(self, eng, fn, R=(), W=()):
        R = flat(R)
        W = flat(W)
        self._waits(eng, R, W)
        ins = fn(eng.e)
        eng.n += 1
        ins.then_inc(eng.sem, 1)
        self._done((eng.sem, eng.n), R, W)

    def mm(self, out, parts, R, W):
        R = flat(R)
        W = flat(W)
        eng = self.pe
        self._waits(eng, R, W)
        n = len(parts)
        ins = None
        for i, (l, r) in enumerate(parts):
            ins = eng.e.matmul(out, lhsT=l, rhs=r, start=(i == 0), stop=(i == n - 1))
        eng.n += 1
        ins.then_inc(eng.sem, 1)
        self._done((eng.sem, eng.n), R, W)

    def mm_multi(self, groups, R, W):
        R = flat(R)
        W = flat(W)
        eng = self.pe
        self._waits(eng, R, W)
        ins = None
        for (o, l, r, st, sp) in groups:
            ins = eng.e.matmul(o, lhsT=l, rhs=r, start=st, stop=sp)
        eng.n += 1
        ins.then_inc(eng.sem, 1)
        self._done((eng.sem, eng.n), R, W)

    def tr(self, out, in_, ident, R, W):
        R = flat(R)
        W = flat(W)
        eng = self.pe
        self._waits(eng, R, W)
        ins = eng.e.transpose(out, in_, ident)
        eng.n += 1
        ins.then_inc(eng.sem, 1)
        self._done((eng.sem, eng.n), R, W)

    def tr_multi(self, items, R, W):
        R = flat(R)
        W = flat(W)
        eng = self.pe
        self._waits(eng, R, W)
        ins = None
        for (o, i_, idn) in items:
            ins = eng.e.transpose(o, i_, idn)
        eng.n += 1
        ins.then_inc(eng.sem, 1)
        self._done((eng.sem, eng.n), R, W)

    def dma(self, q, out, in_, R=(), W=()):
        R = flat(R)
        W = flat(W)
        self._waits(q, R, W)
        pool = self.dsems[q.name]
        i = self.dptr[q.name]
        self.dptr[q.name] = (i + 1) % len(pool)
        sem, cnt = pool[i]
        k = self.sid(sem)
        if cnt > 0 and q.seen.get(k, 0) < 16 * cnt:
            q.e.wait_ge(sem, 16 * cnt)
            q.seen[k] = 16 * cnt
        q.e.dma_start(out=out, in_=in_).then_inc(sem, 16)
        pool[i][1] = cnt + 1
        self._done((sem, 16 * (cnt + 1)), R, W)

    def barrier(self):
        toks = [(e.sem, e.n) for e in (self.pe, self.act, self.dve, self.pool) if e.n > 0]
        for q in ("sp", "pool"):
            for sem, cnt in self.dsems[q]:
                if cnt > 0:
                    toks.append((sem, 16 * cnt))
        for eng in (self.pe, self.act, self.dve, self.sp, self.pool):
            for sem, v in toks:
                k = self.sid(sem)
                if eng.seen.get(k, 0) < v:
                    eng.e.wait_ge(sem, v)
                    eng.seen[k] = v

    def finish(self):
        self.barrier()


def build_program(stage=99, skip_ffn=False):
    _uc = [0]

    def UN(nm):
        _uc[0] += 1
        return f"sb{_uc[0]}_{nm}"

    nc = bass.Bass("TRN2", target_bir_lowering=False)
    I = lambda nm, shp: nc.dram_tensor(nm, list(shp), F32, kind="ExternalInput").ap()
    O = lambda nm, shp: nc.dram_tensor(nm, list(shp), F32, kind="ExternalOutput").ap()
    xT_d = I("xT", [128, NKC, T])
    vec_d = I("vec", [128, NV])
    gml_d = I("gml", [128, DEPTH, 512])
    cf_d = I("cf", [128, 512])
    wgu_d = I("wgu", [DEPTH, 2, NJP, 128, NKC, 512])
    wdn_d = I("wdn", [DEPTH, 2, NJP, 128, 2, 1024])
    win_d = I("win", [DEPTH, H, 128, NKC, 512])
    wgt_d = I("wgt", [DEPTH, 128, NKC, 8])
    wlr_d = I("wlr", [DEPTH, 2, 128, NKC, 512])
    wrg_d = I("wrg", [DEPTH, 128, 2, 4, 128])
    wout_d = I("wout", [DEPTH, 2, 128, NKC, 512])
    sCT_d = I("sCT", [DEPTH, H, 128, NSQ, 128])
    sC_d = I("sC", [DEPTH, H, 128, NSQ, 128])
    snT_d = I("snT", [DEPTH, H, 128, NSQ])
    sn_d = I("sn", [DEPTH, H, NSQ, 128])
    sm_d = I("sm", [DEPTH, H, NSQ])
    sh_d = I("sh", [DEPTH, 128, 4, NSQ])
    scv_d = I("scv", [DEPTH, 128, 4, 3, NSQ])
    yT_d = O("yT", [128, NKC, T])
    pCT_d = O("pCT", [DEPTH, H, 128, 129])
    pm_d = O("pm", [DEPTH, H, 1])
    ph_d = O("ph", [DEPTH, 128, 4])
    pcv_d = O("pcv", [DEPTH, 128, 4, 3])
    oC_d = O("oC", [DEPTH, H, 128, NSQ, 128])
    on_d = O("on", [DEPTH, H, NSQ, 128])
    om_d = O("om", [DEPTH, H, NSQ])
    oh_d = O("oh", [DEPTH, 128, 4, NSQ])
    ocv_d = O("ocv", [DEPTH, 128, 4, 3, NSQ])
    dbg_d = nc.dram_tensor("dbg", [128, NKC, T], BF16, kind="ExternalOutput").ap() if os.environ.get('KDBGOUT') else None

    with ExitStack() as es:
        K = Ctx(nc, es)
        pe, act, dve, pool, sp = K.pe, K.act, K.dve, K.pool, K.sp
        SB = lambda nm, shp, dt=F32: es.enter_context(nc.sbuf_tensor(UN(nm), list(shp), dt))

        x = SB("x", [128, NKC, T])
        xr_ = RL(NKC, 5)
        xn = SB("xn", [128, NKC, T], BF16)
        xnr = RL(NKC, 5)
        vec = SB("vec", [128, NV])
        vecr = Res()
        cf = SB("cf", [128, 512])
        cfr = Res()
        cb = SB("cb", [128, 512], BF16)
        cbr = Res()
        NR8 = 2
        R8 = [SB(f"r8_{i}", [128, NKC, 512], BF16) for i in range(NR8)]
        R8r = [Res() for _ in range(NR8)]
        w_items = []
        for l_ in range(DEPTH):
            if stage >= 1 and not skip_ffn:
                w_items += [wgu_d[l_, 0, jp] for jp in range(NJP)]
            if stage >= 3:
                w_items += [win_d[l_, h_] for h_ in range(H)]
            if stage >= 4:
                w_items += [wlr_d[l_, i_] for i_ in range(2)]
            if stage >= 5:
                w_items += [wout_d[l_, i_] for i_ in range(2)]
            if stage >= 6 and not skip_ffn:
                w_items += [wgu_d[l_, 1, jp] for jp in range(NJP)]
            if stage < 7:
                break
        wst = {"issued": 0, "consumed": 0, "released": 0}

        def _r8issue(upto):
            while wst["issued"] < min(len(w_items), upto) and wst["issued"] - NR8 < wst["released"]:
                i = wst["issued"]
                K.dma(pool, R8[i % NR8][:], w_items[i], W=[R8r[i % NR8]])
                wst["issued"] += 1

        def r8():
            k = wst["consumed"]
            wst["consumed"] += 1
            _r8issue(k + NR8)
            assert wst["issued"] > k
            return R8[k % NR8], R8r[k % NR8]

        def r8rel():
            wst["released"] += 1
            _r8issue(wst["consumed"] + NR8 - 1)

        ident_f = cf[:, 0:128]
        ident_b = cb[:, 0:128]
        causal_b = cb[:, 128:256]
        smask_b = cb[0:64, 256:320]
        ones_b = cb[:, 320:448]
        ind_f = cf[0:64, 320:336]

        def V(nm, l, j=0, n=1, rows=128):
            o = VOFF[(nm, l)] + j
            return vec[0:rows, o:o + n]

        K.dma(sp, vec[:], vec_d, W=[vecr])
        K.dma(sp, cf[:], cf_d, W=[cfr])
        for kc in range(NKC):
            K.dma(sp, x[:, kc, :], xT_d[:, kc, :], W=xr_[kc])
        K.op(dve, lambda e: e.tensor_copy(out=cb[:, 0:320], in_=cf[:, 0:320]), R=[cfr], W=[cbr])
        K.op(dve, lambda e: e.memset(cb[:, 320:448], 1.0), W=[cbr])

        def rmsnorm(gname, l, out_t, out_r, sq, sqr, rs, rsr, tiles=None):
            for ti, (c0, n) in enumerate(TILES):
                if tiles is not None and ti not in tiles:
                    continue
                K.op(act, lambda e: e.activation(out=sq[:, :, 0:n], in_=x[:, :, c0:c0 + n], func=AF.Square),
                     R=[xr_[kc][ti] for kc in range(NKC)], W=[sqr])
                pt, pr = K.psn()
                K.mm(pt[:, 0:n], [(ones_b, sq[:, kc, 0:n]) for kc in range(NKC)], R=[sqr, cbr], W=[pr])
                b = ti % 2
                K.op(act, lambda e: e.activation(out=rs[b][:, 0:n], in_=pt[:, 0:n], func=AF.Ln,
                                                 scale=1.0 / D, bias=epsc[:, 0:1]), R=[pr, cbr], W=[rsr[b]])
                K.op(act, lambda e: e.activation(out=rs[b][:, 0:n], in_=rs[b][:, 0:n], func=AF.Exp, scale=-0.5),
                     R=[rsr[b]], W=[rsr[b]])
                for kc in range(NKC):
                    K.op(dve, lambda e: e.scalar_tensor_tensor(
                        out=out_t[:, kc, c0:c0 + n], in0=x[:, kc, c0:c0 + n], scalar=V(gname, l, kc),
                        in1=rs[b][:, 0:n], op0=ALU.mult, op1=ALU.mult),
                        R=[xr_[kc][ti], rsr[b], vecr], W=[out_r[kc][ti]])

        epsc = SB("epsc", [128, 1])
        K.op(dve, lambda e: e.memset(epsc[:], EPS), W=[cbr])

        def ffn(l, f, pre_norm=True, post=None):
            gname = "g_ff1" if f == 0 else "g_ff2"
            with ExitStack() as fs:
                S = lambda nm, shp, dt=F32: fs.enter_context(nc.sbuf_tensor(UN(nm), list(shp), dt))
                sq = S("f_sq", [128, NKC, 512], BF16)
                sqr = Res()
                rs = [S("f_rs0", [128, 512]), S("f_rs1", [128, 512])]
                rsr = [Res(), Res()]
                hg = S("f_h", [128, 6, T], BF16)
                hgr = RL(6, 5)
                sg = [S("f_sg0", [128, 512]), S("f_sg1", [128, 512])]
                sgr = [Res(), Res()]
                R4 = [S(f"f_r4_{i}", [128, 2, 1024], BF16) for i in range(3)]
                R4r = [Res() for _ in range(3)]
                r4p = [0]

                def r4():
                    i = r4p[0]
                    r4p[0] = (i + 1) % 3
                    return R4[i], R4r[i]
                if pre_norm:
                    rmsnorm(gname, l, xn, xnr, sq, sqr, rs, rsr)
                cnt = 0
                for g, pairs in enumerate(GROUPS):
                    wds = []
                    for pi, jp in enumerate(pairs):
                        wt, wr = r8()
                        dt_, dr = r4()
                        K.dma(pool, dt_[:], wdn_d[l, f, jp], W=[dr])
                        wds.append((dt_, dr))
                        for jj in range(2):
                            jl = 2 * pi + jj
                            for ti, (c0, n) in enumerate(TILES):
                                pg, pgr = K.psn()
                                pu, pur = K.psn()
                                rr = [xnr[kc][ti] for kc in range(NKC)] + [wr]
                                K.mm(pg[:, 0:n], [(wt[:, kc, jj * 128:(jj + 1) * 128], xn[:, kc, c0:c0 + n])
                                                  for kc in range(NKC)], R=rr, W=[pgr])
                                K.mm(pu[:, 0:n], [(wt[:, kc, 256 + jj * 128:256 + (jj + 1) * 128], xn[:, kc, c0:c0 + n])
                                                  for kc in range(NKC)], R=rr, W=[pur])
                                b = cnt % 2
                                cnt += 1
                                K.op(act, lambda e: e.activation(out=sg[b][:, 0:n], in_=pg[:, 0:n], func=AF.Silu),
                                     R=[pgr], W=[sgr[b]])
                                K.op(dve, lambda e: e.tensor_tensor(out=hg[:, jl, c0:c0 + n], in0=sg[b][:, 0:n],
                                                                    in1=pu[:, 0:n], op=ALU.mult),
                                     R=[sgr[b], pur], W=[hgr[jl][ti]])
                        r8rel()
                    nj = 2 * len(pairs)
                    for ti, (c0, n) in enumerate(TILES):
                        for m in range(NKC):
                            pt, pr = K.psn()
                            K.mm(pt[:, 0:n], [(wds[jl // 2][0][:, jl % 2, m * 128:(m + 1) * 128], hg[:, jl, c0:c0 + n])
                                              for jl in range(nj)],
                                 R=[hgr[jl][ti] for jl in range(nj)] + [w[1] for w in wds], W=[pr])
                            K.op(dve, lambda e: e.scalar_tensor_tensor(
                                out=x[:, m, c0:c0 + n], in0=pt[:, 0:n], scalar=0.5, in1=x[:, m, c0:c0 + n],
                                op0=ALU.mult, op1=ALU.add), R=[pr, xr_[m][ti]], W=[xr_[m][ti]])
                        if post is not None and g == len(GROUPS) - 1 and ti >= 1:
                            post(sq, sqr, rs, rsr, ti - 1)
                if post is not None:
                    post(sq, sqr, rs, rsr, len(TILES) - 1)
            K.barrier()

        def mixer(l, pre_norm=True, post_norm=False):
            with ExitStack() as ms:
                S = lambda nm, shp, dt=F32: ms.enter_context(nc.sbuf_tensor(UN(nm), list(shp), dt))
                mixin = S("m_mixin", [128, NKC, T], BF16)
                mixr = RL(NKC, 5)
                colr = Res()
                cols = S("m_cols", [128, NCH, 12])
                dcp = S("m_dcp", [128, H, 17])
                dcs = S("m_dcs", [128, H, NSQ])
                dcsT = S("m_dcsT", [NSQ, H])
                smallr = Res()
                with ExitStack() as rs_:
                    S2 = lambda nm, shp, dt=F32: rs_.enter_context(nc.sbuf_tensor(UN(nm), list(shp), dt))
                    with ExitStack() as ns_:
                        S3 = lambda nm, shp, dt=F32: ns_.enter_context(nc.sbuf_tensor(UN(nm), list(shp), dt))
                        sq = S3("r_sq", [128, NKC, 512], BF16)
                        sqr = Res()
                        rs = [S3("r_rs0", [128, 512]), S3("r_rs1", [128, 512])]
                        rsr = [Res(), Res()]
                        if pre_norm:
                            rmsnorm("g_mix", l, xn, xnr, sq, sqr, rs, rsr)
                    if pre_norm:
                        K.barrier()
                    wg = S2("r_wg", [128, NKC, 8], BF16)
                    wgr = Res()
                    K.dma(pool, wg[:], wgt_d[l], W=[wgr])
                    R1 = S2("r_R1", [H, T])
                    R2 = S2("r_R2", [H, T])
                    R3 = S2("r_R3", [H, T])
                    r1, r2, r3 = Res(), Res(), Res()
                    nbf = S2("r_nbf", [H, 1])
                    sm = S2("r_sm", [H, 24])
                    mref = S2("r_mref", [H, 18])
                    m0 = S2("r_m0", [H, NSQ])
                    mnx = S2("r_mnx", [H, NSQ])
                    dcr = S2("r_dcr", [H, 17 + NSQ])
                    dce = S2("r_dce", [H, H, 17 + NSQ])
                    K.dma(sp, m0[:], sm_d[l], W=[smallr])
                    K.op(act, lambda e: e.mul(out=nbf[:], in_=V("b_f", l, rows=H), mul=-1.0), R=[vecr], W=[smallr])
                    for ti, (c0, n) in enumerate(TILES):
                        pi_, pir = K.psn()
                        pf_, pfr = K.psn()
                        rr = [xnr[kc][ti] for kc in range(NKC)] + [wgr]
                        K.mm(pi_[0:H, 0:n], [(wg[:, kc, 0:4], xn[:, kc, c0:c0 + n]) for kc in range(NKC)], R=rr, W=[pir])
                        K.mm(pf_[0:H, 0:n], [(wg[:, kc, 4:8], xn[:, kc, c0:c0 + n]) for kc in range(NKC)], R=rr, W=[pfr])
                        K.op(act, lambda e: e.activation(out=R1[:, c0:c0 + n], in_=pi_[0:H, 0:n], func=AF.Identity,
                                                         bias=V("b_i", l, rows=H), scale=1.0), R=[pir, vecr], W=[r1])
                        K.op(act, lambda e: e.activation(out=R2[:, c0:c0 + n], in_=pf_[0:H, 0:n], func=AF.Exp,
                                                         bias=nbf[:], scale=-1.0), R=[pfr, smallr], W=[r2])
                    K.op(act, lambda e: e.activation(out=R2[:], in_=R2[:], func=AF.Ln, bias=1.0, scale=1.0), R=[r2], W=[r2])
                    K.op(dve, lambda e: e.tensor_tensor_scan(out=R3[:, 0:TP], data0=R2[:, 0:TP], data1=R2[:, 0:TP],
                                                             initial=0.0, op0=ALU.add, op1=ALU.max), R=[r2], W=[r3])
                    K.op(dve, lambda e: e.tensor_copy(out=R3[:, TP:TP + 16], in_=R2[:, TP:TP + 16]), R=[r2], W=[r3])
                    for t in range(1, NST):
                        K.op(dve, lambda e: e.tensor_tensor(out=R3[:, TP + 16 * t:TP + 16 * t + 16],
                                                            in0=R3[:, TP + 16 * (t - 1):TP + 16 * t],
                                                            in1=R2[:, TP + 16 * t:TP + 16 * t + 16], op=ALU.add),
                             R=[r2, r3], W=[r3])
                    K.op(dve, lambda e: e.tensor_tensor(out=R1[:], in0=R1[:], in1=R3[:], op=ALU.add), R=[r1, r3], W=[r1])
                    K.op(dve, lambda e: e.tensor_tensor_scan(out=R2[:, 0:TP], data0=R1[:, 0:TP], data1=R1[:, 0:TP],
                                                             initial=0.0, op0=ALU.max, op1=ALU.max), R=[r1, r2], W=[r2])
                    K.op(dve, lambda e: e.tensor_tensor(out=R2[:, TP:TP + 16], in0=R1[:, TP:TP + 16], in1=m0[:],
                                                        op=ALU.max), R=[r1, smallr, r2], W=[r2])
                    for t in range(1, NST):
                        K.op(dve, lambda e: e.tensor_tensor(out=R2[:, TP + 16 * t:TP + 16 * t + 16],
                                                            in0=R2[:, TP + 16 * (t - 1):TP + 16 * t],
                                                            in1=R1[:, TP + 16 * t:TP + 16 * t + 16], op=ALU.max),
                             R=[r1, r2], W=[r2])
                    K.op(dve, lambda e: e.memset(mref[:, 0:1], 0.0), W=[smallr])
                    K.op(dve, lambda e: e.tensor_copy(
                        out=mref[:, 1:17], in_=R2[:, 0:2048].rearrange("p (c t) -> p c t", t=128)[:, :, 127]),
                        R=[r2, smallr], W=[smallr])
                    K.op(dve, lambda e: e.tensor_copy(out=mref[:, 17:18], in_=R2[:, TP - 1:TP]), R=[r2, smallr], W=[smallr])
                    K.op(dve, lambda e: e.tensor_copy(out=mnx[:], in_=R2[:, TP + 48:TP + 64]), R=[r2, smallr], W=[smallr])
                    K.op(dve, lambda e: e.tensor_tensor(out=sm[:, 0:1], in0=R2[:, TP - 1:TP], in1=R3[:, TP - 1:TP],
                                                        op=ALU.subtract), R=[r2, r3, smallr], W=[smallr])
                    K.op(dve, lambda e: e.tensor_tensor(out=sm[:, 1:17], in0=R2[:, TP + 48:TP + 64],
                                                        in1=R3[:, TP + 48:TP + 64], op=ALU.subtract),
                         R=[r2, r3, smallr], W=[smallr])
                    K.dma(sp, pm_d[l], sm[:, 0:1], R=[smallr])
                    K.dma(sp, om_d[l], sm[:, 1:17], R=[smallr])
                    K.op(dve, lambda e: e.tensor_tensor(out=dcr[:, 0:17], in0=mref[:, 0:17], in1=mref[:, 1:18],
                                                        op=ALU.subtract), R=[smallr], W=[smallr])
                    K.op(dve, lambda e: e.tensor_tensor(out=dcr[:, 17:33], in0=m0[:], in1=mnx[:], op=ALU.subtract),
                         R=[smallr], W=[smallr])
                    K.op(act, lambda e: e.activation(out=dcr[:], in_=dcr[:], func=AF.Exp), R=[smallr], W=[smallr])
                    pv = lambda Rt: Rt[:, 0:2048].rearrange("p (c t) -> p c t", t=128)
                    sv = lambda Rt: Rt[:, TP:T].rearrange("p (t b) -> p t b", b=NSQ)
                    bc = lambda ap_, n: ap_.unsqueeze(2).to_broadcast([H, ap_.shape[1], n])
                    bs = lambda ap_: ap_.unsqueeze(1).to_broadcast([H, NST, NSQ])
                    K.op(dve, lambda e: e.tensor_tensor(out=pv(R2), in0=pv(R1), in1=bc(mref[:, 0:16], 128), op=ALU.subtract),
                         R=[r1, smallr, r2], W=[r2])
                    K.op(dve, lambda e: e.tensor_scalar(out=R2[:, 2048:TP], in0=R1[:, 2048:TP], scalar1=mref[:, 16:17],
                                                        scalar2=None, op0=ALU.subtract), R=[r1, smallr, r2], W=[r2])
                    K.op(dve, lambda e: e.tensor_tensor(out=sv(R2), in0=sv(R1), in1=bs(m0[:]), op=ALU.subtract),
                         R=[r1, smallr, r2], W=[r2])
                    K.op(act, lambda e: e.activation(out=R2[:], in_=R2[:], func=AF.Exp), R=[r2], W=[r2])
                    K.op(dve, lambda e: e.tensor_tensor(out=pv(R3), in0=pv(R3), in1=bc(mref[:, 0:16], 128), op=ALU.subtract),
                         R=[r3, smallr], W=[r3])
                    K.op(dve, lambda e: e.tensor_scalar(out=R3[:, 2048:TP], in0=R3[:, 2048:TP], scalar1=mref[:, 16:17],
                                                        scalar2=None, op0=ALU.subtract), R=[r3, smallr], W=[r3])
                    K.op(dve, lambda e: e.tensor_tensor(out=sv(R3), in0=sv(R3), in1=bs(m0[:]), op=ALU.subtract),
                         R=[r3, smallr], W=[r3])
                    K.op(act, lambda e: e.activation(out=R3[:], in_=R3[:], func=AF.Exp, scale=2.0), R=[r3], W=[r3])
                    K.op(dve, lambda e: e.tensor_tensor(out=pv(R1), in0=pv(R1), in1=bc(mref[:, 1:17], 128), op=ALU.subtract),
                         R=[r1, smallr], W=[r1])
                    K.op(dve, lambda e: e.tensor_scalar(out=R1[:, 2048:TP], in0=R1[:, 2048:TP], scalar1=mref[:, 17:18],
                                                        scalar2=None, op0=ALU.subtract), R=[r1, smallr], W=[r1])
                    K.op(dve, lambda e: e.tensor_tensor(out=sv(R1), in0=sv(R1), in1=bs(mnx[:]), op=ALU.subtract),
                         R=[r1, smallr], W=[r1])
                    K.op(act, lambda e: e.activation(out=R1[:], in_=R1[:], func=AF.Exp), R=[r1], W=[r1])
                    for ci, (c0, n) in enumerate(CHUNKS):
                        pt, pr = K.psn()
                        K.tr_multi([(pt[0:n, 4 * qi:4 * qi + 4], Rt[:, c0:c0 + n], ident_f[0:H, 0:H])
                                    for qi, Rt in enumerate((R2, R1, R3))], R=[r1, r2, r3, cfr], W=[pr])
                        K.op(act, lambda e: e.activation(out=cols[0:n, ci, :], in_=pt[0:n, 0:12], func=AF.Copy),
                             R=[pr], W=[colr])
                    for hh in range(H):
                        K.op(dve, lambda e: e.tensor_scalar(out=dce[:, hh, :], in0=dcr[:], scalar1=ident_f[0:H, hh:hh + 1],
                                                            scalar2=None, op0=ALU.mult), R=[smallr, cfr], W=[smallr])
                    pt, pr = K.psn()
                    K.mm(pt[:, 0:H * 33], [(ones_f4[:], dce[:].rearrange("p h j -> p (h j)"))], R=[smallr, cbr], W=[pr])
                    ptv = pt[:, 0:H * 33].rearrange("p (h j) -> p h j", j=33)
                    K.op(act, lambda e: e.activation(out=dcp[:], in_=ptv[:, :, 0:17], func=AF.Copy), R=[pr], W=[colr])
                    K.op(act, lambda e: e.activation(out=dcs[:], in_=ptv[:, :, 17:33], func=AF.Copy), R=[pr], W=[colr])
                    pt2, pr2 = K.psn()
                    K.tr(pt2[0:NSQ, 0:H], dcr[:, 17:33], ident_f[0:H, 0:H], R=[smallr, cfr], W=[pr2])
                    K.op(act, lambda e: e.activation(out=dcsT[:], in_=pt2[0:NSQ, 0:H], func=AF.Copy), R=[pr2], W=[colr])
                K.barrier()
                if stage < 3:
                    return
                with ExitStack() as hs:
                    S2 = lambda nm, shp, dt=F32: hs.enter_context(nc.sbuf_tensor(UN(nm), list(shp), dt))
                    NB = 5
                    gml = S2("h_gml", [128, 512])
                    gmlr = Res()
                    K.dma(sp, gml[:], gml_d[:, l, :], W=[gmlr])
                    qk = [S2(f"h_qk{i}", [128, 2, 512], BF16) for i in range(3)]
                    qkr = [Res(), Res(), Res()]
                    ktok = [S2(f"h_kt{i}", [128, 128], BF16) for i in range(NB)]
                    v1 = [S2(f"h_v1{i}", [128, 129], BF16) for i in range(NB)]
                    vwx = [S2(f"h_vw{i}", [128, 129], BF16) for i in range(NB)]
                    sgo = [S2(f"h_so{i}", [128, 128]) for i in range(NB)]
                    eo = [S2(f"h_eo{i}", [128, 128]) for i in range(2)]
                    eor = [Res(), Res()]
                    stw = [S2(f"h_sw{i}", [128, 128], BF16) for i in range(NB)]
                    tokr = [RL(5) for _ in range(NB)]
                    hgt = [S2(f"h_hg{i}", [128, 128]) for i in range(2)]
                    hmt = [S2(f"h_hm{i}", [128, 128], BF16) for i in range(2)]
                    junk = [S2(f"h_jk{i}", [128, 128], BF16) for i in range(2)]
                    pcol = [S2(f"h_pc{i}", [128, 8]) for i in range(3)]
                    pcr = [Res() for _ in range(3)]
                    postr = [RL(4), RL(4)]
                    CTn = S2("h_CTn", [128, 129])
                    CTb = S2("h_CTb", [128, 129], BF16)
                    ctr, ctbr = Res(), Res()
                    C0 = S2("h_C0", [128, NSQ, 128])
                    c0r = RL(NSQ)
                    CT0 = S2("h_CT0", [128, NSQ, 130], BF16)
                    ct0r = Res()
                    n0T = S2("h_n0T", [128, NSQ])
                    n0 = S2("h_n0", [NSQ, 128])
                    n0r = Res()
                    qd = S2("h_qd", [128, 16 * 65], BF16)
                    qdr = Res()
                    vwb = [S2(f"h_vwb{i}", [64, 128], BF16) for i in range(NSQ)]
                    vwbr = [Res() for _ in range(NSQ)]
                    ewb = S2("h_ewb", [64, NSQ], BF16)
                    ewbr = Res()
                    for i in range(NB):
                        K.op(dve, lambda e: e.memset(v1[i][:, 128:129], 1.0), W=[tokr[i][1]])
                    K.op(dve, lambda e: e.memset(qd[:], 0.0), W=[qdr])
                    qd_rows = qd[:, 0:1024].rearrange("p (b t) -> p b t", t=64)
                    qd_diag = qd[:, 0:1040].rearrange("p (b u) -> p b u", u=65)[:, :, 0:64:16]
                    rot = {"a": 0, "n": 0}

                    def bank_a():
                        i = rot["a"]
                        rot["a"] = (i + 1) % 2
                        return K.ps[i], K.psr[i]

                    def bank_n():
                        i = 3 + rot["n"]
                        rot["n"] = (rot["n"] + 1) % 3
                        return K.ps[i], K.psr[i]

                    def load_states(h):
                        K.dma(pool, CT0[:, :, 0:128], sCT_d[l, h], W=[ct0r])
                        K.dma(sp, C0[:], sC_d[l, h], W=c0r)
                        K.dma(sp, n0T[:], snT_d[l, h], W=[n0r])
                        K.dma(sp, n0[:], sn_d[l, h], W=[n0r])
                        K.op(dve, lambda e: e.tensor_copy(out=CT0[:, :, 128], in_=n0T[:]), R=[n0r, ct0r], W=[ct0r])

                    its = [(h, ci) for h in range(H) for ci in range(NCH)]
                    ctx = {}
                    wts = {}

                    def stageA(i):
                        h, ci = its[i]
                        c0, n = CHUNKS[ci]
                        issample = (ci == NCH - 1)
                        if ci == 0:
                            wts[h] = r8()
                        wt, wr = wts[h]
                        ti = min(c0 // 512, 4)
                        tc0, tn = TILES[ti]
                        qb = ti % 3

                        def qk_group(tj, part):
                            jc0, jn = TILES[tj]
                            if jn > 256:
                                lo_, hi_ = (0, 256) if part % 2 == 0 else (256, jn)
                            else:
                                if part % 2 == 1:
                                    return
                                lo_, hi_ = 0, jn
                            isk = part >= 2
                            wc = 128 if isk else 0
                            jb = tj % 3
                            pq, pqr = bank_a()
                            K.mm(pq[:, 0:hi_ - lo_], [(wt[:, kc, wc:wc + 128], xn[:, kc, jc0 + lo_:jc0 + hi_]) for kc in range(NKC)],
                                 R=[xnr[kc][tj] for kc in range(NKC)] + [wr], W=[pqr])
                            if isk:
                                K.op(dve, lambda e: e.tensor_scalar(out=qk[jb][:, 1, lo_:hi_], in0=pq[:, 0:hi_ - lo_], scalar1=KSCALE,
                                                                    scalar2=None, op0=ALU.mult), R=[pqr], W=[qkr[jb]])
                            else:
                                K.op(act, lambda e: e.activation(out=qk[jb][:, 0, lo_:hi_], in_=pq[:, 0:hi_ - lo_], func=AF.Copy),
                                     R=[pqr], W=[qkr[jb]])

                        if ci == 0:
                            for part in range(4):
                                qk_group(0, part)
                        if ci < 16:
                            qk_group(ti + 1, ci % 4)
                        lo = c0 - tc0
                        qT = qk[qb][:, 0, lo:lo + n]
                        kT = qk[qb][:, 1, lo:lo + n]
                        b = i % NB
                        tr_ = tokr[b]
                        ea = cols[0:n, ci, h:h + 1]
                        ew = cols[0:n, ci, 4 + h:5 + h]
                        fl = cols[0:n, ci, 8 + h:9 + h]
                        ctx[i] = dict(h=h, ci=ci, c0=c0, n=n, issample=issample, ti=ti, qb=qb, qT=qT, kT=kT, b=b, tr_=tr_,
                                      ea=ea, ew=ew, fl=fl)
                        pt, pr = bank_a()
                        K.mm(pt[0:n, 0:384], [(xn[:, kc, c0:c0 + n], wt[:, kc, 128:512]) for kc in range(NKC)],
                             R=[xnr[kc][ti] for kc in range(NKC)] + [wr], W=[pr])
                        K.op(act, lambda e: e.mul(out=ktok[b][0:n, :], in_=pt[0:n, 0:128], mul=KSCALE), R=[pr], W=[tr_[0]])
                        K.op(act, lambda e: e.activation(out=v1[b][0:n, 0:128], in_=pt[0:n, 128:256], func=AF.Copy), R=[pr], W=[tr_[1]])
                        eb = i % 2
                        K.op(act, lambda e: e.activation(out=eo[eb][0:n, :], in_=pt[0:n, 256:384], func=AF.Exp, scale=-1.0), R=[pr], W=[eor[eb]])
                        K.op(act, lambda e: e.activation(out=eo[eb][0:n, :], in_=eo[eb][0:n, :], func=AF.Ln, bias=1.0, scale=1.0),
                             R=[eor[eb]], W=[eor[eb]])
                        K.op(act, lambda e: e.activation(out=sgo[b][0:n, :], in_=eo[eb][0:n, :], func=AF.Exp, scale=-1.0),
                             R=[eor[eb]], W=[tr_[3]])
                        K.op(dve, lambda e: e.tensor_scalar(out=vwx[b][0:n, 0:128], in0=v1[b][0:n, 0:128], scalar1=ew, scalar2=None,
                                                            op0=ALU.mult), R=[tr_[1], colr], W=[tr_[2]])
                        K.op(act, lambda e: e.activation(out=vwx[b][0:n, 128:129], in_=ew, func=AF.Copy), R=[colr], W=[tr_[2]])
                        ps_, psr_ = K.ps[2], K.psr[2]
                        K.mm(ps_[0:n, 0:n], [(kT, qT)], R=[qkr[qb]], W=[psr_])
                        msk = smask_b if issample else causal_b[0:n, 0:n]
                        if n < 128:
                            K.op(dve, lambda e: e.memset(stw[b][:, 0:n], 0.0), W=[tr_[4]])
                        K.op(dve, lambda e: e.scalar_tensor_tensor(out=stw[b][0:n, 0:n], in0=ps_[0:n, 0:n], scalar=ea,
                                                                   in1=msk, op0=ALU.mult, op1=ALU.mult),
                             R=[psr_, colr, cbr], W=[tr_[4]])
                        if issample:
                            K.op(dve, lambda e: e.tensor_copy(out=qd_diag, in_=qT.rearrange("p (t b) -> p b t", b=NSQ)),
                                 R=[qkr[qb], qdr], W=[qdr])
                        if ci == NCH - 1:
                            r8rel()

                    def prompt_state(c):
                        h, ci, n, b, tr_ = c["h"], c["ci"], c["n"], c["b"], c["tr_"]
                        pu_, pur = K.ps[6], K.psr[6]
                        K.mm(pu_[:, 0:129], [(ktok[b][0:n, :], vwx[b][0:n, :])], R=[tr_[0], tr_[2]], W=[pur])
                        if ci == 0:
                            K.op(dve, lambda e: e.tensor_copy(out=CTn[:], in_=pu_[:, 0:129]), R=[pur, ctr], W=[ctr])
                        else:
                            K.op(dve, lambda e: e.scalar_tensor_tensor(out=CTn[:], in0=CTn[:], scalar=dcp[:, h, ci:ci + 1],
                                                                       in1=pu_[:, 0:129], op0=ALU.mult, op1=ALU.add),
                                 R=[pur, ctr, colr], W=[ctr])
                        if ci == NCH - 2:
                            K.dma(sp, pCT_d[l, h], CTn[:], R=[ctr])
                        else:
                            K.op(act, lambda e: e.activation(out=CTb[:], in_=CTn[:], func=AF.Copy), R=[ctr, ctbr], W=[ctbr])

                    def sample_state(c):
                        h, n, b, tr_, ew = c["h"], c["n"], c["b"], c["tr_"], c["ew"]
                        K.op(dve, lambda e: e.tensor_scalar(out=ewb[:], in0=ind_f, scalar1=ew, scalar2=None,
                                                            op0=ALU.mult), R=[cfr, colr], W=[ewbr])
                        pu_, pur = K.ps[6], K.psr[6]
                        K.mm(pu_[0:NSQ, 0:128], [(ewb[:], ktok[b][0:n, :])], R=[ewbr, tr_[0]], W=[pur])
                        K.op(dve, lambda e: e.scalar_tensor_tensor(out=n0[:], in0=n0[:], scalar=dcsT[:, h:h + 1],
                                                                   in1=pu_[0:NSQ, 0:128], op0=ALU.mult, op1=ALU.add),
                             R=[n0r, colr, pur], W=[n0r])
                        K.dma(sp, on_d[l, h], n0[:], R=[n0r])
                        for bb in range(NSQ):
                            K.op(dve, lambda e: e.tensor_scalar(out=vwb[bb][:], in0=vwx[b][0:n, 0:128],
                                                                scalar1=ind_f[:, bb:bb + 1], scalar2=None, op0=ALU.mult),
                                 R=[tr_[2], cfr], W=[vwbr[bb]])
                        bk = lambda bb: (K.ps[6], K.psr[6]) if bb % 2 == 0 else (K.ps[7], K.psr[7])

                        def mmb(bb):
                            pc_, pcr_ = bk(bb)
                            K.mm(pc_[:, 0:128], [(vwb[bb][:], ktok[b][0:n, :])], R=[vwbr[bb], tr_[0]], W=[pcr_])

                        mmb(0)
                        mmb(1)
                        for bb in range(NSQ):
                            pc_, pcr_ = bk(bb)
                            K.op(dve, lambda e: e.scalar_tensor_tensor(out=C0[:, bb, :], in0=C0[:, bb, :],
                                                                       scalar=dcs[:, h, bb:bb + 1], in1=pc_[:, 0:128],
                                                                       op0=ALU.mult, op1=ALU.add),
                                 R=[c0r[bb], colr, pcr_], W=[c0r[bb]])
                            if bb + 2 < NSQ:
                                mmb(bb + 2)
                        K.dma(sp, oC_d[l, h], C0[:], R=c0r)

                    def stageB(i):
                        c = ctx[i]
                        h, ci, n, b, tr_, qT, qb = c["h"], c["ci"], c["n"], c["b"], c["tr_"], c["qT"], c["qb"]
                        pn_, pnr = bank_n()
                        c["pn"] = (pn_, pnr)
                        if c["issample"]:
                            grp = [(pn_[0:n, 0:129], stw[b][:, 0:n], v1[b][:, :], True, False)]
                            for bb in range(NSQ):
                                grp.append((pn_[0:n, 0:129], qd_rows[:, bb, :], CT0[:, bb, 0:129], False, bb == NSQ - 1))
                            K.mm_multi(grp, R=[tr_[4], tr_[1], qdr, ct0r], W=[pnr])
                        elif ci == 0:
                            K.mm(pn_[0:n, 0:129], [(stw[b][:, 0:n], v1[b][:, :])], R=[tr_[4], tr_[1]], W=[pnr])
                        else:
                            K.mm(pn_[0:n, 0:129], [(stw[b][:, 0:n], v1[b][:, :]), (qT, CTb[:])],
                                 R=[tr_[4], tr_[1], qkr[qb], ctbr], W=[pnr])
                        pb3 = i % 3
                        c["pb3"] = pb3
                        jb = i % 2
                        K.op(act, lambda e: e.activation(out=junk[jb][0:n, :], in_=pn_[0:n, 0:128], func=AF.Square,
                                                         accum_out=pcol[pb3][0:n, 2:3]),
                             R=[pnr, postr[jb][2]], W=[pcr[pb3], postr[jb][2]])
                        K.op(act, lambda e: e.activation(out=pcol[pb3][0:n, 0:1], in_=pn_[0:n, 128:129], func=AF.Square),
                             R=[pnr, pcr[pb3]], W=[pcr[pb3]])
                        if not c["issample"]:
                            prompt_state(c)

                    def stageC1(i):
                        c = ctx[i]
                        n, fl = c["n"], c["fl"]
                        pn_, pnr = c["pn"]
                        pc = pcol[c["pb3"]]
                        r_ = pcr[c["pb3"]]
                        K.op(dve, lambda e: e.tensor_tensor(out=pc[0:n, 0:1], in0=pc[0:n, 0:1], in1=fl, op=ALU.max),
                             R=[colr, r_], W=[r_])
                        K.op(dve, lambda e: e.scalar_tensor_tensor(out=pc[0:n, 3:4], in0=pc[0:n, 0:1], scalar=DH * EPS, in1=pc[0:n, 2:3],
                                                                   op0=ALU.mult, op1=ALU.add), R=[r_], W=[r_])
                        K.op(act, lambda e: e.activation(out=pc[0:n, 4:5], in_=pc[0:n, 3:4], func=AF.Ln, scale=1.0 / DH), R=[r_], W=[r_])
                        K.op(act, lambda e: e.activation(out=pc[0:n, 5:6], in_=pc[0:n, 4:5], func=AF.Exp, scale=-0.5), R=[r_], W=[r_])

                    def stageC2a(i):
                        c = ctx[i]
                        h, ci, c0, n, b, tr_, ti = c["h"], c["ci"], c["c0"], c["n"], c["b"], c["tr_"], c["ti"]
                        pn_, pnr = c["pn"]
                        pc = pcol[c["pb3"]]
                        r_ = pcr[c["pb3"]]
                        pb = i % 2
                        po_ = postr[pb]
                        K.op(dve, lambda e: e.scalar_tensor_tensor(out=hgt[pb][0:n, :], in0=pn_[0:n, 0:128], scalar=pc[0:n, 5:6],
                                                                   in1=gml[0:n, h * 128:(h + 1) * 128],
                                                                   op0=ALU.mult, op1=ALU.mult),
                             R=[pnr, r_, gmlr, po_[0]], W=[po_[0]])
                        K.op(dve, lambda e: e.tensor_tensor(out=hmt[pb][0:n, :], in0=hgt[pb][0:n, :], in1=sgo[b][0:n, :],
                                                            op=ALU.mult), R=[po_[0], tr_[3], po_[1]], W=[po_[1]])

                    def stageC2b(i):
                        c = ctx.pop(i)
                        h, c0, n, ti = c["h"], c["c0"], c["n"], c["ti"]
                        pb = i % 2
                        po_ = postr[pb]
                        ph_, phr = K.ps[7], K.psr[7]
                        phb = ph_[:].bitcast(BF16)
                        K.tr(phb[:, 0:n], hmt[pb][0:n, :], ident_b[0:n, 0:n], R=[po_[1], cbr], W=[phr])
                        K.op(act, lambda e: e.activation(out=mixin[:, h, c0:c0 + n], in_=phb[:, 0:n], func=AF.Copy),
                             R=[phr], W=[mixr[h][ti]])
                        if c["issample"]:
                            sample_state(c)
                            if h + 1 < H:
                                load_states(h + 1)

                    load_states(0)
                    NI = len(its)
                    for i in range(NI + 5):
                        if 0 <= i - 5 < NI:
                            stageC2b(i - 5)
                        if 0 <= i - 2 < NI:
                            stageB(i - 2)
                        if 0 <= i - 3 < NI:
                            stageC1(i - 3)
                        if 0 <= i - 4 < NI:
                            stageC2a(i - 4)
                        if i < NI:
                            stageA(i)
                K.barrier()
                if stage < 4:
                    return
                with ExitStack() as ls:
                    S2 = lambda nm, shp, dt=F32: ls.enter_context(nc.sbuf_tensor(UN(nm), list(shp), dt))
                    wrg = S2("l_wrg", [128, 2, 4, 128], BF16)
                    wrgr = Res()
                    K.dma(pool, wrg[:], wrg_d[l], W=[wrgr])
                    wl = [r8(), r8()]
                    sc = S2("l_sc", [128, 4])
                    scr = Res()
                    K.op(act, lambda e: e.activation(out=sc[:], in_=V("lru_lambda", l, 0, 4), func=AF.Exp, scale=-1.0),
                         R=[vecr], W=[scr])
                    K.op(act, lambda e: e.activation(out=sc[:], in_=sc[:], func=AF.Ln, bias=1.0, scale=1.0), R=[scr], W=[scr])
                    K.op(act, lambda e: e.mul(out=sc[:], in_=sc[:], mul=-8.0), R=[scr], W=[scr])
                    sc2 = S2("l_sc2", [128, 4])
                    K.op(act, lambda e: e.mul(out=sc2[:], in_=sc[:], mul=2.0), R=[scr], W=[scr])
                    nbg = S2("l_nbg", [128, 8])
                    K.op(act, lambda e: e.mul(out=nbg[:, 0:4], in_=V("b_rg_a", l, 0, 4), mul=-1.0), R=[vecr, scr], W=[scr])
                    K.op(act, lambda e: e.mul(out=nbg[:, 4:8], in_=V("b_rg_x", l, 0, 4), mul=-1.0), R=[vecr, scr], W=[scr])
                    mk = lambda nm, n_, shp, dt=F32: ([S2(f"{nm}{i}", shp, dt) for i in range(n_)], [Res() for _ in range(n_)])
                    xet, xetr = mk("l_xet", 2, [128, 3 + 512])
                    ygf, ygfr = mk("l_ygf", 2, [128, 512])
                    xc, xcr = mk("l_xc", 3, [128, 512])
                    xcb, xcbr = mk("l_xcb", 2, [128, 512], BF16)
                    tmp, tmpr = mk("l_tmp", 2, [128, 512])
                    glb, glbr = mk("l_glb", 3, [128, 512], BF16)
                    gr, grr = mk("l_gr", 2, [128, 512])
                    gi, gir = mk("l_gi", 2, [128, 512])
                    ts_, tsr = mk("l_ts", 2, [128, 512])
                    hcur, hcurr = mk("l_hc", 2, [128, 512])
                    sqh, sqhr = mk("l_sqh", 2, [128, 512], BF16)
                    car = S2("l_car", [128, 4, 3])
                    carr = RL(4)
                    xes = S2("l_xes", [128, 4, 7, NSQ])
                    xesr = RL(4)
                    hcar = S2("l_hcar", [128, 4])
                    hcr = RL(4)
                    h0s = S2("l_h0s", [128, 4, NSQ])
                    h0r = Res()
                    hs_o = S2("l_hso", [128, 4, NSQ])
                    php = S2("l_php", [128, 4])
                    hsor = Res()
                    rsl = S2("l_rsl", [128, 512])
                    rslr = Res()
                    K.dma(sp, xes[:, :, 0:3, :], scv_d[l], W=xesr)
                    K.dma(sp, h0s[:], sh_d[l], W=[h0r])
                    K.op(dve, lambda e: e.memset(car[:], 0.0), W=carr)
                    items = [(ti, cc) for ti in range(5) for cc in range(4)]
                    NIT = len(items)
                    rotx = {"x": 0, "y": 0}

                    def T1(k):
                        ti, cc = items[k]
                        c0, n = TILES[ti]
                        npr = min(n, TP - c0)
                        b2 = k % 2
                        wt, wr = wl[cc // 2]
                        o_ = (cc % 2) * 256
                        px, pxr = K.ps[rotx["x"]], K.psr[rotx["x"]]
                        rotx["x"] = (rotx["x"] + 1) % 2
                        py, pyr = K.ps[2 + rotx["y"]], K.psr[2 + rotx["y"]]
                        rotx["y"] = (rotx["y"] + 1) % 2
                        rr = [xnr[kc][ti] for kc in range(NKC)] + [wr]
                        K.mm(px[:, 0:n], [(wt[:, kc, o_:o_ + 128], xn[:, kc, c0:c0 + n]) for kc in range(NKC)], R=rr, W=[pxr])
                        K.mm(py[:, 0:n], [(wt[:, kc, o_ + 128:o_ + 256], xn[:, kc, c0:c0 + n]) for kc in range(NKC)], R=rr, W=[pyr])
                        K.op(act, lambda e: e.activation(out=xet[b2][:, 3:3 + npr], in_=px[:, 0:npr], func=AF.Copy),
                             R=[pxr], W=[xetr[b2]])
                        K.op(act, lambda e: e.activation(out=xet[b2][:, 0:3], in_=car[:, cc, :], func=AF.Copy),
                             R=[carr[cc]], W=[xetr[b2]])
                        K.op(act, lambda e: e.activation(out=ygf[b2][:, 0:n], in_=py[:, 0:n], func=AF.Copy), R=[pyr], W=[ygfr[b2]])
                        if ti == 4:
                            K.op(act, lambda e: e.activation(
                                out=xes[:, cc, 3:7, :], in_=px[:, npr:n].rearrange("p (t b) -> p t b", b=NSQ), func=AF.Copy),
                                R=[pxr], W=[xesr[cc]])
                            K.dma(sp, pcv_d[l, :, cc, :], xet[b2][:, npr:npr + 3], R=[xetr[b2]])
                            K.dma(sp, ocv_d[l, :, cc], xes[:, cc, 4:7, :], R=[xesr[cc]])
                        else:
                            K.op(act, lambda e: e.activation(out=car[:, cc, :], in_=xet[b2][:, n:n + 3], func=AF.Copy),
                                 R=[xetr[b2]], W=[carr[cc]])

                    def T2(k):
                        ti, cc = items[k]
                        c0, n = TILES[ti]
                        npr = min(n, TP - c0)
                        b2, b3 = k % 2, k % 3
                        cw = lambda j: V("conv_w", l, j * 4 + cc)
                        K.op(dve, lambda e: e.tensor_scalar(out=xc[b3][:, 0:npr], in0=xet[b2][:, 3:3 + npr], scalar1=cw(3),
                                                            scalar2=V("conv_b", l, cc), op0=ALU.mult, op1=ALU.add),
                             R=[xetr[b2], vecr], W=[xcr[b3]])
                        for j in range(1, 4):
                            K.op(dve, lambda e: e.scalar_tensor_tensor(out=xc[b3][:, 0:npr], in0=xet[b2][:, 3 - j:3 - j + npr],
                                                                       scalar=cw(3 - j), in1=xc[b3][:, 0:npr],
                                                                       op0=ALU.mult, op1=ALU.add),
                                 R=[xetr[b2], vecr, xcr[b3]], W=[xcr[b3]])
                        if ti == 4:
                            xcs = xc[b3][:, npr:n].rearrange("p (t b) -> p t b", b=NSQ)
                            K.op(dve, lambda e: e.tensor_scalar(out=xcs, in0=xes[:, cc, 3:7, :], scalar1=cw(3),
                                                                scalar2=V("conv_b", l, cc), op0=ALU.mult, op1=ALU.add),
                                 R=[xesr[cc], vecr, xcr[b3]], W=[xcr[b3]])
                            for j in range(1, 4):
                                K.op(dve, lambda e: e.scalar_tensor_tensor(out=xcs, in0=xes[:, cc, 3 - j:7 - j, :],
                                                                           scalar=cw(3 - j), in1=xcs, op0=ALU.mult, op1=ALU.add),
                                     R=[xesr[cc], vecr, xcr[b3]], W=[xcr[b3]])
                        K.op(dve, lambda e: e.tensor_copy(out=xcb[b2][:, 0:n], in_=xc[b3][:, 0:n]), R=[xcr[b3]], W=[xcbr[b2]])
                        pa, par = K.ps[4], K.psr[4]
                        pi2, pir2 = K.ps[5], K.psr[5]
                        K.mm(pa[:, 0:n], [(wrg[:, 0, cc, :], xcb[b2][:, 0:n])], R=[wrgr, xcbr[b2]], W=[par])
                        K.mm(pi2[:, 0:n], [(wrg[:, 1, cc, :], xcb[b2][:, 0:n])], R=[wrgr, xcbr[b2]], W=[pir2])
                        K.op(dve, lambda e: e.tensor_tensor(out=tmp[b2][:, 0:n], in0=ygf[b2][:, 0:n], in1=ygf[b2][:, 0:n], op=ALU.mult),
                             R=[ygfr[b2]], W=[tmpr[b2]])
                        K.op(dve, lambda e: e.tensor_scalar(out=tmp[b2][:, 0:n], in0=tmp[b2][:, 0:n], scalar1=0.044715, scalar2=1.0,
                                                            op0=ALU.mult, op1=ALU.add), R=[tmpr[b2]], W=[tmpr[b2]])
                        K.op(dve, lambda e: e.tensor_tensor(out=tmp[b2][:, 0:n], in0=tmp[b2][:, 0:n], in1=ygf[b2][:, 0:n], op=ALU.mult),
                             R=[tmpr[b2], ygfr[b2]], W=[tmpr[b2]])
                        K.op(act, lambda e: e.activation(out=tmp[b2][:, 0:n], in_=tmp[b2][:, 0:n], func=AF.Exp,
                                                         scale=-1.5957691216057308), R=[tmpr[b2]], W=[tmpr[b2]])
                        K.op(act, lambda e: e.activation(out=tmp[b2][:, 0:n], in_=tmp[b2][:, 0:n], func=AF.Ln, bias=1.0, scale=1.0),
                             R=[tmpr[b2]], W=[tmpr[b2]])
                        K.op(act, lambda e: e.activation(out=tmp[b2][:, 0:n], in_=tmp[b2][:, 0:n], func=AF.Exp, scale=-1.0),
                             R=[tmpr[b2]], W=[tmpr[b2]])
                        K.op(dve, lambda e: e.tensor_tensor(out=glb[b3][:, 0:n], in0=ygf[b2][:, 0:n], in1=tmp[b2][:, 0:n], op=ALU.mult),
                             R=[tmpr[b2], ygfr[b2]], W=[glbr[b3]])

                    def T3(k):
                        ti, cc = items[k]
                        c0, n = TILES[ti]
                        b2 = k % 2
                        pa, par = K.ps[4], K.psr[4]
                        pi2, pir2 = K.ps[5], K.psr[5]
                        g_, g_r, i_, i_r, t_, t_r = gr[b2], grr[b2], gi[b2], gir[b2], ts_[b2], tsr[b2]
                        K.op(act, lambda e: e.activation(out=g_[:, 0:n], in_=pa[:, 0:n], func=AF.Exp,
                                                         bias=nbg[:, cc:cc + 1], scale=-1.0), R=[par, scr], W=[g_r])
                        K.op(act, lambda e: e.activation(out=i_[:, 0:n], in_=pi2[:, 0:n], func=AF.Exp,
                                                         bias=nbg[:, 4 + cc:5 + cc], scale=-1.0), R=[pir2, scr], W=[i_r])
                        K.op(act, lambda e: e.activation(out=g_[:, 0:n], in_=g_[:, 0:n], func=AF.Ln, bias=1.0, scale=1.0), R=[g_r], W=[g_r])
                        K.op(act, lambda e: e.activation(out=i_[:, 0:n], in_=i_[:, 0:n], func=AF.Ln, bias=1.0, scale=1.0), R=[i_r], W=[i_r])
                        K.op(act, lambda e: e.activation(out=g_[:, 0:n], in_=g_[:, 0:n], func=AF.Exp, scale=-1.0), R=[g_r], W=[g_r])
                        K.op(act, lambda e: e.activation(out=i_[:, 0:n], in_=i_[:, 0:n], func=AF.Exp, scale=-1.0), R=[i_r], W=[i_r])
                        K.op(act, lambda e: e.activation(out=t_[:, 0:n], in_=g_[:, 0:n], func=AF.Exp, scale=sc2[:, cc:cc + 1]),
                             R=[g_r, scr], W=[t_r])
                        K.op(act, lambda e: e.activation(out=g_[:, 0:n], in_=g_[:, 0:n], func=AF.Exp, scale=sc[:, cc:cc + 1]),
                             R=[g_r, scr], W=[g_r])
                        K.op(act, lambda e: e.activation(out=t_[:, 0:n], in_=t_[:, 0:n], func=AF.Ln, scale=-1.0, bias=1.0), R=[t_r], W=[t_r])
                        K.op(act, lambda e: e.activation(out=t_[:, 0:n], in_=t_[:, 0:n], func=AF.Exp, scale=0.5), R=[t_r], W=[t_r])

                    def T4(k):
                        ti, cc = items[k]
                        c0, n = TILES[ti]
                        npr = min(n, TP - c0)
                        b2, b3 = k % 2, k % 3
                        g_, g_r, i_, i_r, t_, t_r = gr[b2], grr[b2], gi[b2], gir[b2], ts_[b2], tsr[b2]
                        hc, hcr_ = hcur[b2], hcurr[b2]
                        K.op(dve, lambda e: e.tensor_tensor(out=i_[:, 0:n], in0=i_[:, 0:n], in1=xc[b3][:, 0:n], op=ALU.mult),
                             R=[i_r, xcr[b3]], W=[i_r])
                        K.op(dve, lambda e: e.tensor_tensor(out=i_[:, 0:n], in0=i_[:, 0:n], in1=t_[:, 0:n], op=ALU.mult),
                             R=[i_r, t_r], W=[i_r])
                        init = 0.0 if ti == 0 else hcar[:, cc:cc + 1]
                        K.op(dve, lambda e: e.tensor_tensor_scan(out=hc[:, 0:npr], data0=g_[:, 0:npr], data1=i_[:, 0:npr],
                                                                 initial=init, op0=ALU.mult, op1=ALU.add),
                             R=[g_r, i_r, hcr[cc]], W=[hcr_])
                        if ti < 4:
                            K.op(act, lambda e: e.activation(out=hcar[:, cc:cc + 1], in_=hc[:, n - 1:n], func=AF.Copy),
                                 R=[hcr_], W=[hcr[cc]])
                        else:
                            K.op(act, lambda e: e.activation(out=php[:, cc:cc + 1], in_=hc[:, npr - 1:npr], func=AF.Copy),
                                 R=[hcr_], W=[hsor])
                            for t in range(NST):
                                s0 = npr + 16 * t
                                prev = h0s[:, cc, :] if t == 0 else hc[:, s0 - 16:s0]
                                K.op(dve, lambda e: e.tensor_tensor(out=hc[:, s0:s0 + 16], in0=g_[:, s0:s0 + 16], in1=prev,
                                                                    op=ALU.mult), R=[g_r, h0r, hcr_], W=[hcr_])
                                K.op(dve, lambda e: e.tensor_tensor(out=hc[:, s0:s0 + 16], in0=hc[:, s0:s0 + 16],
                                                                    in1=i_[:, s0:s0 + 16], op=ALU.add), R=[i_r, hcr_], W=[hcr_])
                            K.op(act, lambda e: e.activation(out=hs_o[:, cc, :], in_=hc[:, npr + 48:npr + 64], func=AF.Copy),
                                 R=[hcr_], W=[hsor])
                        K.op(dve, lambda e: e.tensor_tensor(out=sqh[b2][:, 0:n], in0=hc[:, 0:n], in1=hc[:, 0:n], op=ALU.mult), R=[hcr_], W=[sqhr[b2]])
                        pt, pr = K.ps[6], K.psr[6]
                        K.mm_multi([(pt[:, 0:n], ones_b, sqh[b2][:, 0:n], cc == 0, cc == 3)], R=[sqhr[b2], cbr], W=[pr])
                        K.op(dve, lambda e: e.tensor_tensor(out=mixin[:, 4 + cc, c0:c0 + n], in0=hc[:, 0:n], in1=glb[b3][:, 0:n],
                                                            op=ALU.mult), R=[hcr_, glbr[b3]], W=[mixr[4 + cc][ti]])
                        if cc == 3:
                            K.op(act, lambda e: e.activation(out=rsl[:, 0:n], in_=pt[:, 0:n], func=AF.Ln, scale=1.0 / DLRU,
                                                             bias=epsc[:, 0:1]), R=[pr, cbr], W=[rslr])
                            K.op(act, lambda e: e.activation(out=rsl[:, 0:n], in_=rsl[:, 0:n], func=AF.Exp, scale=-0.5),
                                 R=[rslr], W=[rslr])

                    def T5(k):
                        ti, cc = items[k]
                        if cc != 3:
                            return
                        c0, n = TILES[ti]
                        for c2 in range(4):
                            K.op(dve, lambda e: e.scalar_tensor_tensor(out=mixin[:, 4 + c2, c0:c0 + n], in0=mixin[:, 4 + c2, c0:c0 + n],
                                                                       scalar=V("g_lru_out", l, c2), in1=rsl[:, 0:n],
                                                                       op0=ALU.mult, op1=ALU.mult),
                                 R=[mixr[4 + c2][ti], vecr, rslr], W=[mixr[4 + c2][ti]])

                    for m in range(NIT + 4):
                        if 0 <= m - 4 < NIT:
                            T5(m - 4)
                        if 0 <= m - 3 < NIT:
                            T4(m - 3)
                        if 0 <= m - 2 < NIT:
                            T3(m - 2)
                        if 0 <= m - 1 < NIT:
                            T2(m - 1)
                        if m < NIT:
                            T1(m)
                    K.dma(sp, oh_d[l], hs_o[:], R=[hsor])
                    K.dma(sp, ph_d[l], php[:], R=[hsor])
                    r8rel()
                    r8rel()
                K.barrier()
                if dbg_d is not None and l == 0:
                    K.dma(sp, dbg_d, mixin[:], R=mixr)
                if stage < 5:
                    return
                wo = [r8(), r8()]
                with ExitStack() as os_:
                    S2 = lambda nm, shp, dt=F32: os_.enter_context(nc.sbuf_tensor(UN(nm), list(shp), dt))
                    if post_norm:
                        sq = S2("w_sq", [128, NKC, 512], BF16)
                        sqr = Res()
                        rs = [S2("w_rs0", [128, 512]), S2("w_rs1", [128, 512])]
                        rsr = [Res(), Res()]
                    for ti, (c0, n) in enumerate(TILES):
                        for m in range(NKC):
                            wt, wr = wo[m // 4]
                            mo = m % 4
                            pt, pr = K.psn()
                            K.mm(pt[:, 0:n], [(wt[:, kc, mo * 128:(mo + 1) * 128], mixin[:, kc, c0:c0 + n]) for kc in range(NKC)],
                                 R=[mixr[kc][ti] for kc in range(NKC)] + [wr], W=[pr])
                            K.op(dve, lambda e: e.tensor_tensor(out=x[:, m, c0:c0 + n], in0=pt[:, 0:n], in1=x[:, m, c0:c0 + n],
                                                                op=ALU.add), R=[pr, xr_[m][ti]], W=[xr_[m][ti]])
                        if post_norm and ti >= 1:
                            rmsnorm("g_ff2", l, xn, xnr, sq, sqr, rs, rsr, tiles=[ti - 1])
                    r8rel()
                    r8rel()
                    if post_norm:
                        rmsnorm("g_ff2", l, xn, xnr, sq, sqr, rs, rsr, tiles=[len(TILES) - 1])
            K.barrier()

        ones_f4 = SB("ones_f4", [H, 128])
        K.op(dve, lambda e: e.memset(ones_f4[:], 1.0), W=[cbr])

        MERGE = (stage >= 99) and not skip_ffn
        if MERGE:
            def post_norm_fn(gname, l_):
                return lambda sq, sqr, rs, rsr, ti: rmsnorm(gname, l_, xn, xnr, sq, sqr, rs, rsr, tiles=[ti])

            def post_final(sq, sqr, rs, rsr, ti):
                rmsnorm("g_final", 0, x, xr_, sq, sqr, rs, rsr, tiles=[ti])
                c0, n = TILES[ti]
                for kc in range(NKC):
                    K.dma(sp, yT_d[:, kc, c0:c0 + n], x[:, kc, c0:c0 + n], R=[xr_[kc][ti]])

            for l in range(DEPTH):
                ffn(l, 0, pre_norm=(l == 0), post=post_norm_fn("g_mix", l))
                mixer(l, pre_norm=False, post_norm=True)
                ffn(l, 1, pre_norm=False, post=(post_norm_fn("g_ff1", l + 1) if l + 1 < DEPTH else post_final))
            K.finish()
        else:
            for l in range(DEPTH):
                if stage >= 1 and not skip_ffn:
                    ffn(l, 0)
                if stage >= 2:
                    mixer(l)
                if stage >= 6 and not skip_ffn:
                    ffn(l, 1)
                if stage < 7:
                    break
            with ExitStack() as fs:
                S = lambda nm, shp, dt=F32: fs.enter_context(nc.sbuf_tensor(UN(nm), list(shp), dt))
                sq = S("o_sq", [128, NKC, 512], BF16)
                sqr = Res()
                rs = [S("o_rs0", [128, 512]), S("o_rs1", [128, 512])]
                rsr = [Res(), Res()]
                rmsnorm("g_final", 0, x, xr_, sq, sqr, rs, rsr)
                for kc in range(NKC):
                    K.dma(sp, yT_d[:, kc, :], x[:, kc, :], R=xr_[kc])
                K.finish()
    return nc


def _consts():
    cf = np.zeros((128, 512), np.float32)
    cf[:, 0:128] = np.eye(128, dtype=np.float32)
    s = np.arange(128)
    cf[:, 128:256] = (s[:, None] <= s[None, :]).astype(np.float32)
    i = np.arange(64)
    same = (i[:, None] % 16) == (i[None, :] % 16)
    caus = (i[:, None] // 16) <= (i[None, :] // 16)
    cf[0:64, 256:320] = (same & caus).astype(np.float32)
    cf[0:64, 320:336] = ((i[:, None] % 16) == np.arange(16)[None, :]).astype(np.float32)
    return cf


def _prep_shared(W):
    f = lambda a: np.ascontiguousarray(a, dtype=np.float32)
    out = {}
    wgu = np.empty((DEPTH, 2, NJP, 128, NKC, 512), np.float32)
    wdn = np.empty((DEPTH, 2, NJP, 128, 2, 1024), np.float32)
    for fi, (gn, un, dn) in enumerate((("w_ff1_gate", "w_ff1_up", "w_ff1_down"), ("w_ff2_gate", "w_ff2_up", "w_ff2_down"))):
        g = W[gn].reshape(DEPTH, NKC, 128, NJP, 256).transpose(0, 3, 2, 1, 4)
        u = W[un].reshape(DEPTH, NKC, 128, NJP, 256).transpose(0, 3, 2, 1, 4)
        wgu[:, fi, :, :, :, 0:256] = g
        wgu[:, fi, :, :, :, 256:512] = u
        wdn[:, fi] = W[dn].reshape(DEPTH, NJP, 2, 128, 1024).transpose(0, 1, 3, 2, 4)
    out["wgu"] = wgu
    out["wdn"] = wdn
    win = W["w_in"].reshape(DEPTH, NKC, 128, 3080)
    wq = np.empty((DEPTH, H, 128, NKC, 512), np.float32)
    for h in range(H):
        for qi in range(4):
            wq[:, h, :, :, qi * 128:(qi + 1) * 128] = win[:, :, :, qi * 512 + h * 128: qi * 512 + (h + 1) * 128].transpose(0, 2, 1, 3)
    out["win"] = wq
    out["wgt"] = f(win[:, :, :, 2048:2056].transpose(0, 2, 1, 3))
    wl = np.empty((DEPTH, 2, 128, NKC, 512), np.float32)
    for cc in range(4):
        o = (cc % 2) * 256
        wl[:, cc // 2, :, :, o:o + 128] = win[:, :, :, 2056 + cc * 128:2056 + (cc + 1) * 128].transpose(0, 2, 1, 3)
        wl[:, cc // 2, :, :, o + 128:o + 256] = win[:, :, :, 2568 + cc * 128:2568 + (cc + 1) * 128].transpose(0, 2, 1, 3)
    out["wlr"] = wl
    wrg = np.zeros((DEPTH, 128, 2, 4, 128), np.float32)
    for gi, nm in enumerate(("w_rg_a", "w_rg_x")):
        for nb in range(8):
            cc, half = nb // 2, nb % 2
            wrg[:, half * 64:(half + 1) * 64, gi, cc, half * 64:(half + 1) * 64] = W[nm][:, nb]
    out["wrg"] = wrg
    out["wout"] = f(W["w_out"].reshape(DEPTH, NKC, 128, 2, 512).transpose(0, 3, 2, 1, 4))
    vec = np.zeros((128, NV), np.float32)
    for l in range(DEPTH):
        for nm in ("g_ff1", "g_mix", "g_ff2"):
            vec[:, VOFF[(nm, l)]:VOFF[(nm, l)] + 8] = W[nm][l].reshape(8, 128).T
        o = VOFF[("conv_w", l)]
        vec[:, o:o + 16] = W["conv_w"][l].reshape(4, 4, 128).transpose(2, 0, 1).reshape(128, 16)
        for nm in ("conv_b", "b_rg_a", "b_rg_x", "lru_lambda", "g_lru_out"):
            vec[:, VOFF[(nm, l)]:VOFF[(nm, l)] + 4] = W[nm][l].reshape(4, 128).T
        vec[0:4, VOFF[("b_i", l)]] = W["b_gates"][l, 0:4]
        vec[0:4, VOFF[("b_f", l)]] = W["b_gates"][l, 4:8]
    vec[:, VOFF[("g_final", 0)]:VOFF[("g_final", 0)] + 8] = W["g_final"].reshape(8, 128).T
    out["vec"] = vec
    out["gml"] = f(np.broadcast_to(W["g_mlstm_out"][None], (128, DEPTH, 512)))
    out["cf"] = _consts()
    return out


def _prep_core(c, A):
    sl = slice(NSQ * c, NSQ * (c + 1))
    X = np.concatenate([A["meta_tokens"], A["x_prompt"][c],
                        A["x_sample"][sl].transpose(1, 0, 2).reshape(NS, D)], axis=0)
    m = {}
    m["xT"] = np.ascontiguousarray(X.T.reshape(NKC, 128, T).transpose(1, 0, 2))
    C = A["state_mlstm_C"][:, sl]
    m["sCT"] = np.ascontiguousarray(C.transpose(0, 2, 4, 1, 3))
    m["sC"] = np.ascontiguousarray(C.transpose(0, 2, 3, 1, 4))
    n = A["state_mlstm_n"][:, sl]
    m["snT"] = np.ascontiguousarray(n.transpose(0, 2, 3, 1))
    m["sn"] = np.ascontiguousarray(n.transpose(0, 2, 1, 3))
    m["sm"] = np.ascontiguousarray(A["state_mlstm_m"][:, sl].transpose(0, 2, 1))
    m["sh"] = np.ascontiguousarray(A["state_lru_h"][:, sl].reshape(DEPTH, NSQ, 4, 128).transpose(0, 3, 2, 1))
    m["scv"] = np.ascontiguousarray(A["state_conv"][:, sl].reshape(DEPTH, NSQ, 3, 4, 128).transpose(0, 4, 3, 2, 1))
    return m


_NC_CACHE = {}


def kernel(**inputs):
    A = {k: np.asarray(v, dtype=np.float32) for k, v in inputs.items()}
    shared = _prep_shared(A)
    in_maps = []
    for c in range(NCORES):
        m = dict(shared)
        m.update(_prep_core(c, A))
        in_maps.append(m)
    if "nc" not in _NC_CACHE:
        _NC_CACHE["nc"] = build_program()
    nc = _NC_CACHE["nc"]
    res = run_bass_kernel_spmd(nc, in_maps, core_ids=list(range(NCORES)))
    R = res.results
    B = NCORES
    y_prompt = np.empty((B, SEQ, D), np.float32)
    y_sample = np.empty((B * NSQ, NST, D), np.float32)
    pC = np.empty((DEPTH, B, H, DH, DH), np.float32)
    pn = np.empty((DEPTH, B, H, DH), np.float32)
    pm = np.empty((DEPTH, B, H), np.float32)
    ph = np.empty((DEPTH, B, DLRU), np.float32)
    pcv = np.empty((DEPTH, B, 3, DLRU), np.float32)
    sC = np.empty((DEPTH, B * NSQ, H, DH, DH), np.float32)
    sn = np.empty((DEPTH, B * NSQ, H, DH), np.float32)
    sm = np.empty((DEPTH, B * NSQ, H), np.float32)
    sh = np.empty((DEPTH, B * NSQ, DLRU), np.float32)
    scv = np.empty((DEPTH, B * NSQ, 3, DLRU), np.float32)
    for c in range(B):
        r = R[c]
        sl = slice(NSQ * c, NSQ * (c + 1))
        Y = r["yT"].transpose(1, 0, 2).reshape(D, T).T
        y_prompt[c] = Y[NMETA:TP]
        y_sample[sl] = Y[TP:].reshape(NST, NSQ, D).transpose(1, 0, 2)
        pct = r["pCT"]
        pC[:, c] = pct[:, :, :, 0:128].transpose(0, 1, 3, 2)
        pn[:, c] = pct[:, :, :, 128]
        pm[:, c] = r["pm"][:, :, 0]
        ph[:, c] = r["ph"].transpose(0, 2, 1).reshape(DEPTH, DLRU)
        pcv[:, c] = r["pcv"].transpose(0, 3, 2, 1).reshape(DEPTH, 3, DLRU)
        sC[:, sl] = r["oC"].transpose(0, 3, 1, 2, 4)
        sn[:, sl] = r["on"].transpose(0, 2, 1, 3)
        sm[:, sl] = r["om"].transpose(0, 2, 1)
        sh[:, sl] = r["oh"].transpose(0, 3, 2, 1).reshape(DEPTH, NSQ, DLRU)
        scv[:, sl] = r["ocv"].transpose(0, 4, 3, 2, 1).reshape(DEPTH, NSQ, 3, DLRU)
    return (y_prompt, y_sample, pC, pn, pm, ph, pcv, sC, sn, sm, sh, scv)
```

```python
import os
import numpy as np
import ml_dtypes
KDBG = int(os.environ.get('KDBG', '9'))
KNOS = int(os.environ.get('KNOS', '0'))
KSKIP = int(os.environ.get('KSKIP', '0'))
KCH = int(os.environ.get('KCH', '99'))
KSELF = int(os.environ.get('KSELF', '1'))
from contextlib import ExitStack
import concourse.bass as bass
import concourse.mybir as mybir
from concourse.bass_utils import run_bass_kernel_spmd

F32 = mybir.dt.float32
BF16 = mybir.dt.bfloat16
AF = mybir.ActivationFunctionType
ALU = mybir.AluOpType

NCORES = 8
D = 1024
NKC = 8
SEQ = 2048
NMETA = 16
TP = NMETA + SEQ
NSQ = 16
NST = 4
NS = NSQ * NST
T = TP + NS
DFF = 2816
NJ = DFF // 128
NJP = NJ // 2
H = 4
DH = 128
DLRU = 512
DEPTH = 2
EPS = 1e-6
TILES = [(0, 512), (512, 512), (1024, 512), (1536, 512), (2048, 80)]
CHUNKS = [(128 * c, 128) for c in range(16)] + [(2048, 16), (TP, NS)]
NCH = len(CHUNKS)
GROUPS = [[0, 1, 2], [3, 4, 5], [6, 7, 8], [9, 10]]
KSCALE = DH ** -0.5

def _vec_layout():
    off = {}
    n = 0
    for l in range(DEPTH):
        for nm, w in (("g_ff1", 8), ("g_mix", 8), ("g_ff2", 8), ("conv_w", 16), ("conv_b", 4),
                      ("b_rg_a", 4), ("b_rg_x", 4), ("lru_lambda", 4), ("g_lru_out", 4),
                      ("b_i", 1), ("b_f", 1)):
            off[(nm, l)] = n
            n += w
    off[("g_final", 0)] = n
    n += 8
    return off, n

VOFF, NV = _vec_layout()


class Res:
    __slots__ = ("w", "r")

    def __init__(self):
        self.w = None
        self.r = []


def RL(*dims):
    if len(dims) == 0:
        return Res()
    return [RL(*dims[1:]) for _ in range(dims[0])]


def flat(x):
    if isinstance(x, Res):
        return [x]
    out = []
    for y in x:
        out.extend(flat(y))
    return out


class Eng:
    def __init__(self, e, sem, name):
        self.e = e
        self.sem = sem
        self.n = 0
        self.seen = {}
        self.name = name


class Ctx:
    def __init__(self, nc, es):
        self.nc = nc
        self.es = es
        mk = lambda nm: es.enter_context(nc.semaphore(nm))
        self.pe = Eng(nc.tensor, mk("c_pe"), "pe")
        self.act = Eng(nc.scalar, mk("c_act"), "act")
        self.dve = Eng(nc.vector, mk("c_dve"), "dve")
        self.pool = Eng(nc.gpsimd, mk("c_pool"), "pool")
        self.sp = Eng(nc.sync, None, "sp")
        self.compute = [self.pe, self.act, self.dve]
        self.dsems = {}
        for q, nq in ((self.sp, 16), (self.pool, 16)):
            self.dsems[q.name] = [[mk(f"d_{q.name}{i}"), 0] for i in range(nq)]
        self.dptr = {"sp": 0, "pool": 0}
        self.semid = {}
        self.ps = []
        self.psr = []
        for i in range(8):
            self.ps.append(es.enter_context(nc.psum_tensor(f"ps{i}", [128, 512], F32)))
            self.psr.append(Res())
        self.psi = 0

    def sid(self, sem):
        k = id(sem)
        if k not in self.semid:
            self.semid[k] = sem
        return k

    def psn(self):
        i = self.psi
        self.psi = (self.psi + 1) % 8
        return self.ps[i], self.psr[i]

    def _waits(self, eng, R, W):
        deps = {}
        for r in R:
            if r.w is not None:
                k = self.sid(r.w[0])
                deps[k] = max(deps.get(k, 0), r.w[1])
        for w in W:
            if w.w is not None:
                k = self.sid(w.w[0])
                deps[k] = max(deps.get(k, 0), w.w[1])
            for t in w.r:
                k = self.sid(t[0])
                deps[k] = max(deps.get(k, 0), t[1])
        for k, v in deps.items():
            if not KSELF and eng.sem is not None and k == id(eng.sem):
                continue
            if eng.seen.get(k, 0) < v:
                eng.e.wait_ge(self.semid[k], v)
                eng.seen[k] = v

    def _done(self, tok, R, W):
        for r in R:
            r.r.append(tok)
        for w in W:
            w.w = tok
            w.r = []

    def op(self, eng, fn, R=(), W=()):
        R = flat(R)
        W = flat(W)
        self._waits(eng, R, W)
        ins = fn(eng.e)
        eng.n += 1
        ins.then_inc(eng.sem, 1)
        self._done((eng.sem, eng.n), R, W)

    def mm(self, out, parts, R, W):
        R = flat(R)
        W = flat(W)
        eng = self.pe
        self._waits(eng, R, W)
        n = len(parts)
        ins = None
        for i, (l, r) in enumerate(parts):
            ins = eng.e.matmul(out, lhsT=l, rhs=r, start=(i == 0), stop=(i == n - 1))
        eng.n += 1
        ins.then_inc(eng.sem, 1)
        self._done((eng.sem, eng.n), R, W)

    def mm_multi(self, groups, R, W):
        R = flat(R)
        W = flat(W)
        eng = self.pe
        self._waits(eng, R, W)
        ins = None
        for (o, l, r, st, sp) in groups:
            ins = eng.e.matmul(o, lhsT=l, rhs=r, start=st, stop=sp)
        eng.n += 1
        ins.then_inc(eng.sem, 1)
        self._done((eng.sem, eng.n), R, W)

    def tr(self, out, in_, ident, R, W):
        R = flat(R)
        W = flat(W)
        eng = self.pe
        self._waits(eng, R, W)
        ins = eng.e.transpose(out, in_, ident)
        eng.n += 1
        ins.then_inc(eng.sem, 1)
        self._done((eng.sem, eng.n), R, W)

    def tr_multi(self, items, R, W):
        R = flat(R)
        W = flat(W)
        eng = self.pe
        self._waits(eng, R, W)
        ins = None
        for (o, i_, idn) in items:
            ins = eng.e.transpose(o, i_, idn)
        eng.n += 1
        ins.then_inc(eng.sem, 1)
        self._done((eng.sem, eng.n), R, W)

    def dma(self, q, out, in_, R=(), W=()):
        R = flat(R)
        W = flat(W)
        self._waits(q, R, W)
        pool = self.dsems[q.name]
        i = self.dptr[q.name]
        self.dptr[q.name] = (i + 1) % len(pool)
        sem, cnt = pool[i]
        k = self.sid(sem)
        if cnt > 0 and q.seen.get(k, 0) < 16 * cnt:
            q.e.wait_ge(sem, 16 * cnt)
            q.seen[k] = 16 * cnt
        q.e.dma_start(out=out, in_=in_).then_inc(sem, 16)
        pool[i][1] = cnt + 1
        self._done((sem, 16 * (cnt + 1)), R, W)

    def barrier(self):
        toks = [(e.sem, e.n) for e in (self.pe, self.act, self.dve, self.pool) if e.n > 0]
        for q in ("sp", "pool"):
            for sem, cnt in self.dsems[q]:
                if cnt > 0:
                    toks.append((sem, 16 * cnt))
        for eng in (self.pe, self.act, self.dve, self.sp, self.pool):
            for sem, v in toks:
                k = self.sid(sem)
                if eng.seen.get(k, 0) < v:
                    eng.e.wait_ge(sem, v)
                    eng.seen[k] = v

    def finish(self):
        self.barrier()


def build_program(stage=99, skip_ffn=False):
    _uc = [0]

    def UN(nm):
        _uc[0] += 1
        return f"sb{_uc[0]}_{nm}"

    nc = bass.Bass("TRN2", target_bir_lowering=False)
    I = lambda nm, shp: nc.dram_tensor(nm, list(shp), F32, kind="ExternalInput").ap()
    O = lambda nm, shp: nc.dram_tensor(nm, list(shp), F32, kind="ExternalOutput").ap()
    xT_d = I("xT", [128, NKC, T])
    vec_d = I("vec", [128, NV])
    gml_d = I("gml", [128, DEPTH, 512])
    cf_d = I("cf", [128, 512])
    wgu_d = I("wgu", [DEPTH, 2, NJP, 128, NKC, 512])
    wdn_d = I("wdn", [DEPTH, 2, NJP, 128, 2, 1024])
    win_d = I("win", [DEPTH, H, 128, NKC, 512])
    wgt_d = I("wgt", [DEPTH, 128, NKC, 8])
    wlr_d = I("wlr", [DEPTH, 2, 128, NKC, 512])
    wrg_d = I("wrg", [DEPTH, 128, 2, 4, 128])
    wout_d = I("wout", [DEPTH, 2, 128, NKC, 512])
    sCT_d = I("sCT", [DEPTH, H, 128, NSQ, 128])
    sC_d = I("sC", [DEPTH, H, 128, NSQ, 128])
    snT_d = I("snT", [DEPTH, H, 128, NSQ])
    sn_d = I("sn", [DEPTH, H, NSQ, 128])
    sm_d = I("sm", [DEPTH, H, NSQ])
    sh_d = I("sh", [DEPTH, 128, 4, NSQ])
    scv_d = I("scv", [DEPTH, 128, 4, 3, NSQ])
    yT_d = O("yT", [128, NKC, T])
    pCT_d = O("pCT", [DEPTH, H, 128, 129])
    pm_d = O("pm", [DEPTH, H, 1])
    ph_d = O("ph", [DEPTH, 128, 4])
    pcv_d = O("pcv", [DEPTH, 128, 4, 3])
    oC_d = O("oC", [DEPTH, H, 128, NSQ, 128])
    on_d = O("on", [DEPTH, H, NSQ, 128])
    om_d = O("om", [DEPTH, H, NSQ])
    oh_d = O("oh", [DEPTH, 128, 4, NSQ])
    ocv_d = O("ocv", [DEPTH, 128, 4, 3, NSQ])
    dbg_d = nc.dram_tensor("dbg", [128, NKC, T], BF16, kind="ExternalOutput").ap() if os.environ.get('KDBGOUT') else None

    with ExitStack() as es:
        K = Ctx(nc, es)
        pe, act, dve, pool, sp = K.pe, K.act, K.dve, K.pool, K.sp
        SB = lambda nm, shp, dt=F32: es.enter_context(nc.sbuf_tensor(UN(nm), list(shp), dt))

        x = SB("x", [128, NKC, T])
        xr_ = RL(NKC, 5)
        xn = SB("xn", [128, NKC, T], BF16)
        xnr = RL(NKC, 5)
        vec = SB("vec", [128, NV])
        vecr = Res()
        cf = SB("cf", [128, 512])
        cfr = Res()
        cb = SB("cb", [128, 512], BF16)
        cbr = Res()
        NR8 = 2
        R8 = [SB(f"r8_{i}", [128, NKC, 512], BF16) for i in range(NR8)]
        R8r = [Res() for _ in range(NR8)]
        w_items = []
        for l_ in range(DEPTH):
            if stage >= 1 and not skip_ffn:
                w_items += [wgu_d[l_, 0, jp] for jp in range(NJP)]
            if stage >= 3:
                w_items += [win_d[l_, h_] for h_ in range(H)]
            if stage >= 4:
                w_items += [wlr_d[l_, i_] for i_ in range(2)]
            if stage >= 5:
                w_items += [wout_d[l_, i_] for i_ in range(2)]
            if stage >= 6 and not skip_ffn:
                w_items += [wgu_d[l_, 1, jp] for jp in range(NJP)]
            if stage < 7:
                break
        wst = {"issued": 0, "consumed": 0, "released": 0}

        def _r8issue(upto):
            while wst["issued"] < min(len(w_items), upto) and wst["issued"] - NR8 < wst["released"]:
                i = wst["issued"]
                K.dma(pool, R8[i % NR8][:], w_items[i], W=[R8r[i % NR8]])
                wst["issued"] += 1

        def r8():
            k = wst["consumed"]
            wst["consumed"] += 1
            _r8issue(k + NR8)
            assert wst["issued"] > k
            return R8[k % NR8], R8r[k % NR8]

        def r8rel():
            wst["released"] += 1
            _r8issue(wst["consumed"] + NR8 - 1)

        ident_f = cf[:, 0:128]
        ident_b = cb[:, 0:128]
        causal_b = cb[:, 128:256]
        smask_b = cb[0:64, 256:320]
        ones_b = cb[:, 320:448]
        ind_f = cf[0:64, 320:336]

        def V(nm, l, j=0, n=1, rows=128):
            o = VOFF[(nm, l)] + j
            return vec[0:rows, o:o + n]

        K.dma(sp, vec[:], vec_d, W=[vecr])
        K.dma(sp, cf[:], cf_d, W=[cfr])
        for ti, (c0, n) in enumerate(TILES):
            for kc in range(NKC):
                K.dma(sp, x[:, kc, c0:c0 + n], xT_d[:, kc, c0:c0 + n], W=[xr_[kc][ti]])
        K.op(dve, lambda e: e.tensor_copy(out=cb[:, 0:320], in_=cf[:, 0:320]), R=[cfr], W=[cbr])
        K.op(dve, lambda e: e.memset(cb[:, 320:448], 1.0), W=[cbr])

        def rmsnorm(gname, l, out_t, out_r, sq, sqr, rs, rsr, tiles=None):
            for ti, (c0, n) in enumerate(TILES):
                if tiles is not None and ti not in tiles:
                    continue
                K.op(act, lambda e: e.activation(out=sq[:, :, 0:n], in_=x[:, :, c0:c0 + n], func=AF.Square),
                     R=[xr_[kc][ti] for kc in range(NKC)], W=[sqr])
                pt, pr = K.psn()
                K.mm(pt[:, 0:n], [(ones_b, sq[:, kc, 0:n]) for kc in range(NKC)], R=[sqr, cbr], W=[pr])
                b = ti % 2
                K.op(act, lambda e: e.activation(out=rs[b][:, 0:n], in_=pt[:, 0:n], func=AF.Ln,
                                                 scale=1.0 / D, bias=epsc[:, 0:1]), R=[pr, cbr], W=[rsr[b]])
                K.op(act, lambda e: e.activation(out=rs[b][:, 0:n], in_=rs[b][:, 0:n], func=AF.Exp, scale=-0.5),
                     R=[rsr[b]], W=[rsr[b]])
                for kc in range(NKC):
                    K.op(dve, lambda e: e.scalar_tensor_tensor(
                        out=out_t[:, kc, c0:c0 + n], in0=x[:, kc, c0:c0 + n], scalar=V(gname, l, kc),
                        in1=rs[b][:, 0:n], op0=ALU.mult, op1=ALU.mult),
                        R=[xr_[kc][ti], rsr[b], vecr], W=[out_r[kc][ti]])

        epsc = SB("epsc", [128, 1])
        K.op(dve, lambda e: e.memset(epsc[:], EPS), W=[cbr])

        def ffn(l, f, pre_norm=True, post=None):
            gname = "g_ff1" if f == 0 else "g_ff2"
            with ExitStack() as fs:
                S = lambda nm, shp, dt=F32: fs.enter_context(nc.sbuf_tensor(UN(nm), list(shp), dt))
                sq = S("f_sq", [128, NKC, 512], BF16)
                sqr = Res()
                rs = [S("f_rs0", [128, 512]), S("f_rs1", [128, 512])]
                rsr = [Res(), Res()]
                hg = S("f_h", [128, 6, T], BF16)
                hgr = RL(6, 5)
                sg = [S("f_sg0", [128, 512]), S("f_sg1", [128, 512])]
                sgr = [Res(), Res()]
                R4 = [S(f"f_r4_{i}", [128, 2, 1024], BF16) for i in range(3)]
                R4r = [Res() for _ in range(3)]
                r4p = [0]

                def r4():
                    i = r4p[0]
                    r4p[0] = (i + 1) % 3
                    return R4[i], R4r[i]
                if pre_norm:
                    rmsnorm(gname, l, xn, xnr, sq, sqr, rs, rsr)
                cnt = 0
                for g, pairs in enumerate(GROUPS):
                    wds = []
                    for pi, jp in enumerate(pairs):
                        wt, wr = r8()
                        dt_, dr = r4()
                        K.dma(pool, dt_[:], wdn_d[l, f, jp], W=[dr])
                        wds.append((dt_, dr))
                        for jj in range(2):
                            jl = 2 * pi + jj
                            for ti, (c0, n) in enumerate(TILES):
                                pg, pgr = K.psn()
                                pu, pur = K.psn()
                                rr = [xnr[kc][ti] for kc in range(NKC)] + [wr]
                                K.mm(pg[:, 0:n], [(wt[:, kc, jj * 128:(jj + 1) * 128], xn[:, kc, c0:c0 + n])
                                                  for kc in range(NKC)], R=rr, W=[pgr])
                                K.mm(pu[:, 0:n], [(wt[:, kc, 256 + jj * 128:256 + (jj + 1) * 128], xn[:, kc, c0:c0 + n])
                                                  for kc in range(NKC)], R=rr, W=[pur])
                                b = cnt % 2
                                cnt += 1
                                K.op(act, lambda e: e.activation(out=sg[b][:, 0:n], in_=pg[:, 0:n], func=AF.Silu),
                                     R=[pgr], W=[sgr[b]])
                                K.op(dve, lambda e: e.tensor_tensor(out=hg[:, jl, c0:c0 + n], in0=sg[b][:, 0:n],
                                                                    in1=pu[:, 0:n], op=ALU.mult),
                                     R=[sgr[b], pur], W=[hgr[jl][ti]])
                        r8rel()
                    nj = 2 * len(pairs)
                    for ti, (c0, n) in enumerate(TILES):
                        for m in range(NKC):
                            pt, pr = K.psn()
                            K.mm(pt[:, 0:n], [(wds[jl // 2][0][:, jl % 2, m * 128:(m + 1) * 128], hg[:, jl, c0:c0 + n])
                                              for jl in range(nj)],
                                 R=[hgr[jl][ti] for jl in range(nj)] + [w[1] for w in wds], W=[pr])
                            K.op(dve, lambda e: e.scalar_tensor_tensor(
                                out=x[:, m, c0:c0 + n], in0=pt[:, 0:n], scalar=0.5, in1=x[:, m, c0:c0 + n],
                                op0=ALU.mult, op1=ALU.add), R=[pr, xr_[m][ti]], W=[xr_[m][ti]])
                        if post is not None and g == len(GROUPS) - 1 and ti >= 1:
                            post(sq, sqr, rs, rsr, ti - 1)
                if post is not None:
                    post(sq, sqr, rs, rsr, len(TILES) - 1)
            K.barrier()

        def mixer(l, pre_norm=True, post_norm=False):
            with ExitStack() as ms:
                S = lambda nm, shp, dt=F32: ms.enter_context(nc.sbuf_tensor(UN(nm), list(shp), dt))
                mixin = S("m_mixin", [128, NKC, T], BF16)
                mixr = RL(NKC, 5)
                colr = Res()
                cols = S("m_cols", [128, NCH, 12])
                dcp = S("m_dcp", [128, H, 17])
                dcs = S("m_dcs", [128, H, NSQ])
                dcsT = S("m_dcsT", [NSQ, H])
                smallr = Res()
                with ExitStack() as rs_:
                    S2 = lambda nm, shp, dt=F32: rs_.enter_context(nc.sbuf_tensor(UN(nm), list(shp), dt))
                    with ExitStack() as ns_:
                        S3 = lambda nm, shp, dt=F32: ns_.enter_context(nc.sbuf_tensor(UN(nm), list(shp), dt))
                        sq = S3("r_sq", [128, NKC, 512], BF16)
                        sqr = Res()
                        rs = [S3("r_rs0", [128, 512]), S3("r_rs1", [128, 512])]
                        rsr = [Res(), Res()]
                        if pre_norm:
                            rmsnorm("g_mix", l, xn, xnr, sq, sqr, rs, rsr)
                    if pre_norm:
                        K.barrier()
                    wg = S2("r_wg", [128, NKC, 8], BF16)
                    wgr = Res()
                    K.dma(pool, wg[:], wgt_d[l], W=[wgr])
                    R1 = S2("r_R1", [H, T])
                    R2 = S2("r_R2", [H, T])
                    R3 = S2("r_R3", [H, T])
                    r1, r2, r3 = Res(), Res(), Res()
                    nbf = S2("r_nbf", [H, 1])
                    sm = S2("r_sm", [H, 24])
                    mref = S2("r_mref", [H, 18])
                    m0 = S2("r_m0", [H, NSQ])
                    mnx = S2("r_mnx", [H, NSQ])
                    dcr = S2("r_dcr", [H, 17 + NSQ])
                    dce = S2("r_dce", [H, H, 17 + NSQ])
                    K.dma(sp, m0[:], sm_d[l], W=[smallr])
                    K.op(act, lambda e: e.mul(out=nbf[:], in_=V("b_f", l, rows=H), mul=-1.0), R=[vecr], W=[smallr])
                    for ti, (c0, n) in enumerate(TILES):
                        pi_, pir = K.psn()
                        pf_, pfr = K.psn()
                        rr = [xnr[kc][ti] for kc in range(NKC)] + [wgr]
                        K.mm(pi_[0:H, 0:n], [(wg[:, kc, 0:4], xn[:, kc, c0:c0 + n]) for kc in range(NKC)], R=rr, W=[pir])
                        K.mm(pf_[0:H, 0:n], [(wg[:, kc, 4:8], xn[:, kc, c0:c0 + n]) for kc in range(NKC)], R=rr, W=[pfr])
                        K.op(act, lambda e: e.activation(out=R1[:, c0:c0 + n], in_=pi_[0:H, 0:n], func=AF.Identity,
                                                         bias=V("b_i", l, rows=H), scale=1.0), R=[pir, vecr], W=[r1])
                        K.op(act, lambda e: e.activation(out=R2[:, c0:c0 + n], in_=pf_[0:H, 0:n], func=AF.Exp,
                                                         bias=nbf[:], scale=-1.0), R=[pfr, smallr], W=[r2])
                    K.op(act, lambda e: e.activation(out=R2[:], in_=R2[:], func=AF.Ln, bias=1.0, scale=1.0), R=[r2], W=[r2])
                    K.op(dve, lambda e: e.tensor_tensor_scan(out=R3[:, 0:TP], data0=R2[:, 0:TP], data1=R2[:, 0:TP],
                                                             initial=0.0, op0=ALU.add, op1=ALU.max), R=[r2], W=[r3])
                    K.op(dve, lambda e: e.tensor_copy(out=R3[:, TP:TP + 16], in_=R2[:, TP:TP + 16]), R=[r2], W=[r3])
                    for t in range(1, NST):
                        K.op(dve, lambda e: e.tensor_tensor(out=R3[:, TP + 16 * t:TP + 16 * t + 16],
                                                            in0=R3[:, TP + 16 * (t - 1):TP + 16 * t],
                                                            in1=R2[:, TP + 16 * t:TP + 16 * t + 16], op=ALU.add),
                             R=[r2, r3], W=[r3])
                    K.op(dve, lambda e: e.tensor_tensor(out=R1[:], in0=R1[:], in1=R3[:], op=ALU.add), R=[r1, r3], W=[r1])
                    K.op(dve, lambda e: e.tensor_tensor_scan(out=R2[:, 0:TP], data0=R1[:, 0:TP], data1=R1[:, 0:TP],
                                                             initial=0.0, op0=ALU.max, op1=ALU.max), R=[r1, r2], W=[r2])
                    K.op(dve, lambda e: e.tensor_tensor(out=R2[:, TP:TP + 16], in0=R1[:, TP:TP + 16], in1=m0[:],
                                                        op=ALU.max), R=[r1, smallr, r2], W=[r2])
                    for t in range(1, NST):
                        K.op(dve, lambda e: e.tensor_tensor(out=R2[:, TP + 16 * t:TP + 16 * t + 16],
                                                            in0=R2[:, TP + 16 * (t - 1):TP + 16 * t],
                                                            in1=R1[:, TP + 16 * t:TP + 16 * t + 16], op=ALU.max),
                             R=[r1, r2], W=[r2])
                    K.op(dve, lambda e: e.memset(mref[:, 0:1], 0.0), W=[smallr])
                    K.op(dve, lambda e: e.tensor_copy(
                        out=mref[:, 1:17], in_=R2[:, 0:2048].rearrange("p (c t) -> p c t", t=128)[:, :, 127]),
                        R=[r2, smallr], W=[smallr])
                    K.op(dve, lambda e: e.tensor_copy(out=mref[:, 17:18], in_=R2[:, TP - 1:TP]), R=[r2, smallr], W=[smallr])
                    K.op(dve, lambda e: e.tensor_copy(out=mnx[:], in_=R2[:, TP + 48:TP + 64]), R=[r2, smallr], W=[smallr])
                    K.op(dve, lambda e: e.tensor_tensor(out=sm[:, 0:1], in0=R2[:, TP - 1:TP], in1=R3[:, TP - 1:TP],
                                                        op=ALU.subtract), R=[r2, r3, smallr], W=[smallr])
                    K.op(dve, lambda e: e.tensor_tensor(out=sm[:, 1:17], in0=R2[:, TP + 48:TP + 64],
                                                        in1=R3[:, TP + 48:TP + 64], op=ALU.subtract),
                         R=[r2, r3, smallr], W=[smallr])
                    K.dma(sp, pm_d[l], sm[:, 0:1], R=[smallr])
                    K.dma(sp, om_d[l], sm[:, 1:17], R=[smallr])
                    K.op(dve, lambda e: e.tensor_tensor(out=dcr[:, 0:17], in0=mref[:, 0:17], in1=mref[:, 1:18],
                                                        op=ALU.subtract), R=[smallr], W=[smallr])
                    K.op(dve, lambda e: e.tensor_tensor(out=dcr[:, 17:33], in0=m0[:], in1=mnx[:], op=ALU.subtract),
                         R=[smallr], W=[smallr])
                    K.op(act, lambda e: e.activation(out=dcr[:], in_=dcr[:], func=AF.Exp), R=[smallr], W=[smallr])
                    pv = lambda Rt: Rt[:, 0:2048].rearrange("p (c t) -> p c t", t=128)
                    sv = lambda Rt: Rt[:, TP:T].rearrange("p (t b) -> p t b", b=NSQ)
                    bc = lambda ap_, n: ap_.unsqueeze(2).to_broadcast([H, ap_.shape[1], n])
                    bs = lambda ap_: ap_.unsqueeze(1).to_broadcast([H, NST, NSQ])
                    K.op(dve, lambda e: e.tensor_tensor(out=pv(R2), in0=pv(R1), in1=bc(mref[:, 0:16], 128), op=ALU.subtract),
                         R=[r1, smallr, r2], W=[r2])
                    K.op(dve, lambda e: e.tensor_scalar(out=R2[:, 2048:TP], in0=R1[:, 2048:TP], scalar1=mref[:, 16:17],
                                                        scalar2=None, op0=ALU.subtract), R=[r1, smallr, r2], W=[r2])
                    K.op(dve, lambda e: e.tensor_tensor(out=sv(R2), in0=sv(R1), in1=bs(m0[:]), op=ALU.subtract),
                         R=[r1, smallr, r2], W=[r2])
                    K.op(act, lambda e: e.activation(out=R2[:], in_=R2[:], func=AF.Exp), R=[r2], W=[r2])
                    K.op(dve, lambda e: e.tensor_tensor(out=pv(R3), in0=pv(R3), in1=bc(mref[:, 0:16], 128), op=ALU.subtract),
                         R=[r3, smallr], W=[r3])
                    K.op(dve, lambda e: e.tensor_scalar(out=R3[:, 2048:TP], in0=R3[:, 2048:TP], scalar1=mref[:, 16:17],
                                                        scalar2=None, op0=ALU.subtract), R=[r3, smallr], W=[r3])
                    K.op(dve, lambda e: e.tensor_tensor(out=sv(R3), in0=sv(R3), in1=bs(m0[:]), op=ALU.subtract),
                         R=[r3, smallr], W=[r3])
                    K.op(act, lambda e: e.activation(out=R3[:], in_=R3[:], func=AF.Exp, scale=2.0), R=[r3], W=[r3])
                    K.op(dve, lambda e: e.tensor_tensor(out=pv(R1), in0=pv(R1), in1=bc(mref[:, 1:17], 128), op=ALU.subtract),
                         R=[r1, smallr], W=[r1])
                    K.op(dve, lambda e: e.tensor_scalar(out=R1[:, 2048:TP], in0=R1[:, 2048:TP], scalar1=mref[:, 17:18],
                                                        scalar2=None, op0=ALU.subtract), R=[r1, smallr], W=[r1])
                    K.op(dve, lambda e: e.tensor_tensor(out=sv(R1), in0=sv(R1), in1=bs(mnx[:]), op=ALU.subtract),
                         R=[r1, smallr], W=[r1])
                    K.op(act, lambda e: e.activation(out=R1[:], in_=R1[:], func=AF.Exp), R=[r1], W=[r1])
                    for ci, (c0, n) in enumerate(CHUNKS):
                        pt, pr = K.psn()
                        K.tr_multi([(pt[0:n, 4 * qi:4 * qi + 4], Rt[:, c0:c0 + n], ident_f[0:H, 0:H])
                                    for qi, Rt in enumerate((R2, R1, R3))], R=[r1, r2, r3, cfr], W=[pr])
                        K.op(act, lambda e: e.activation(out=cols[0:n, ci, :], in_=pt[0:n, 0:12], func=AF.Copy),
                             R=[pr], W=[colr])
                    for hh in range(H):
                        K.op(dve, lambda e: e.tensor_scalar(out=dce[:, hh, :], in0=dcr[:], scalar1=ident_f[0:H, hh:hh + 1],
                                                            scalar2=None, op0=ALU.mult), R=[smallr, cfr], W=[smallr])
                    pt, pr = K.psn()
                    K.mm(pt[:, 0:H * 33], [(ones_f4[:], dce[:].rearrange("p h j -> p (h j)"))], R=[smallr, cbr], W=[pr])
                    ptv = pt[:, 0:H * 33].rearrange("p (h j) -> p h j", j=33)
                    K.op(act, lambda e: e.activation(out=dcp[:], in_=ptv[:, :, 0:17], func=AF.Copy), R=[pr], W=[colr])
                    K.op(act, lambda e: e.activation(out=dcs[:], in_=ptv[:, :, 17:33], func=AF.Copy), R=[pr], W=[colr])
                    pt2, pr2 = K.psn()
                    K.tr(pt2[0:NSQ, 0:H], dcr[:, 17:33], ident_f[0:H, 0:H], R=[smallr, cfr], W=[pr2])
                    K.op(act, lambda e: e.activation(out=dcsT[:], in_=pt2[0:NSQ, 0:H], func=AF.Copy), R=[pr2], W=[colr])
                K.barrier()
                if stage < 3:
                    return
                with ExitStack() as hs:
                    S2 = lambda nm, shp, dt=F32: hs.enter_context(nc.sbuf_tensor(UN(nm), list(shp), dt))
                    NB = 5
                    gml = S2("h_gml", [128, 512])
                    gmlr = Res()
                    K.dma(sp, gml[:], gml_d[:, l, :], W=[gmlr])
                    qk = [S2(f"h_qk{i}", [128, 2, 512], BF16) for i in range(3)]
                    qkr = [Res(), Res(), Res()]
                    ktok = [S2(f"h_kt{i}", [128, 128], BF16) for i in range(NB)]
                    v1 = [S2(f"h_v1{i}", [128, 129], BF16) for i in range(NB)]
                    vwx = [S2(f"h_vw{i}", [128, 129], BF16) for i in range(NB)]
                    sgo = [S2(f"h_so{i}", [128, 128]) for i in range(NB)]
                    eo = [S2(f"h_eo{i}", [128, 128]) for i in range(2)]
                    eor = [Res(), Res()]
                    stw = [S2(f"h_sw{i}", [128, 128], BF16) for i in range(NB)]
                    tokr = [RL(5) for _ in range(NB)]
                    hgt = [S2(f"h_hg{i}", [128, 128]) for i in range(2)]
                    hmt = [S2(f"h_hm{i}", [128, 128], BF16) for i in range(2)]
                    junk = [S2(f"h_jk{i}", [128, 128], BF16) for i in range(2)]
                    pcol = [S2(f"h_pc{i}", [128, 8]) for i in range(3)]
                    pcr = [Res() for _ in range(3)]
                    postr = [RL(4), RL(4)]
                    CTn = S2("h_CTn", [128, 129])
                    CTb = S2("h_CTb", [128, 129], BF16)
                    ctr, ctbr = Res(), Res()
                    C0 = S2("h_C0", [128, NSQ, 128])
                    c0r = RL(NSQ)
                    CT0 = S2("h_CT0", [128, NSQ, 130], BF16)
                    ct0r = Res()
                    n0T = S2("h_n0T", [128, NSQ])
                    n0 = S2("h_n0", [NSQ, 128])
                    n0r = Res()
                    qd = S2("h_qd", [128, 16 * 65], BF16)
                    qdr = Res()
                    vwb = [S2(f"h_vwb{i}", [64, 128], BF16) for i in range(NSQ)]
                    vwbr = [Res() for _ in range(NSQ)]
                    ewb = S2("h_ewb", [64, NSQ], BF16)
                    ewbr = Res()
                    for i in range(NB):
                        K.op(dve, lambda e: e.memset(v1[i][:, 128:129], 1.0), W=[tokr[i][1]])
                    K.op(dve, lambda e: e.memset(qd[:], 0.0), W=[qdr])
                    qd_rows = qd[:, 0:1024].rearrange("p (b t) -> p b t", t=64)
                    qd_diag = qd[:, 0:1040].rearrange("p (b u) -> p b u", u=65)[:, :, 0:64:16]
                    rot = {"a": 0, "n": 0}

                    def bank_a():
                        i = rot["a"]
                        rot["a"] = (i + 1) % 2
                        return K.ps[i], K.psr[i]

                    def bank_n():
                        i = 3 + rot["n"]
                        rot["n"] = (rot["n"] + 1) % 3
                        return K.ps[i], K.psr[i]

                    def load_states(h):
                        K.dma(pool, CT0[:, :, 0:128], sCT_d[l, h], W=[ct0r])
                        K.dma(sp, C0[:], sC_d[l, h], W=c0r)
                        K.dma(sp, n0T[:], snT_d[l, h], W=[n0r])
                        K.dma(sp, n0[:], sn_d[l, h], W=[n0r])
                        K.op(dve, lambda e: e.tensor_copy(out=CT0[:, :, 128], in_=n0T[:]), R=[n0r, ct0r], W=[ct0r])

                    its = [(h, ci) for h in range(H) for ci in range(NCH)]
                    ctx = {}
                    wts = {}

                    def stageA(i):
                        h, ci = its[i]
                        c0, n = CHUNKS[ci]
                        issample = (ci == NCH - 1)
                        if ci == 0:
                            wts[h] = r8()
                        wt, wr = wts[h]
                        ti = min(c0 // 512, 4)
                        tc0, tn = TILES[ti]
                        qb = ti % 3

                        def qk_group(tj, part):
                            jc0, jn = TILES[tj]
                            if jn > 256:
                                lo_, hi_ = (0, 256) if part % 2 == 0 else (256, jn)
                            else:
                                if part % 2 == 1:
                                    return
                                lo_, hi_ = 0, jn
                            isk = part >= 2
                            wc = 128 if isk else 0
                            jb = tj % 3
                            pq, pqr = bank_a()
                            K.mm(pq[:, 0:hi_ - lo_], [(wt[:, kc, wc:wc + 128], xn[:, kc, jc0 + lo_:jc0 + hi_]) for kc in range(NKC)],
                                 R=[xnr[kc][tj] for kc in range(NKC)] + [wr], W=[pqr])
                            if isk:
                                K.op(dve, lambda e: e.tensor_scalar(out=qk[jb][:, 1, lo_:hi_], in0=pq[:, 0:hi_ - lo_], scalar1=KSCALE,
                                                                    scalar2=None, op0=ALU.mult), R=[pqr], W=[qkr[jb]])
                            else:
                                K.op(act, lambda e: e.activation(out=qk[jb][:, 0, lo_:hi_], in_=pq[:, 0:hi_ - lo_], func=AF.Copy),
                                     R=[pqr], W=[qkr[jb]])

                        if ci == 0:
                            for part in range(4):
                                qk_group(0, part)
                        if ci < 16:
                            qk_group(ti + 1, ci % 4)
                        lo = c0 - tc0
                        qT = qk[qb][:, 0, lo:lo + n]
                        kT = qk[qb][:, 1, lo:lo + n]
                        b = i % NB
                        tr_ = tokr[b]
                        ea = cols[0:n, ci, h:h + 1]
                        ew = cols[0:n, ci, 4 + h:5 + h]
                        fl = cols[0:n, ci, 8 + h:9 + h]
                        ctx[i] = dict(h=h, ci=ci, c0=c0, n=n, issample=issample, ti=ti, qb=qb, qT=qT, kT=kT, b=b, tr_=tr_,
                                      ea=ea, ew=ew, fl=fl)
                        pt, pr = bank_a()
                        K.mm(pt[0:n, 0:384], [(xn[:, kc, c0:c0 + n], wt[:, kc, 128:512]) for kc in range(NKC)],
                             R=[xnr[kc][ti] for kc in range(NKC)] + [wr], W=[pr])
                        K.op(act, lambda e: e.mul(out=ktok[b][0:n, :], in_=pt[0:n, 0:128], mul=KSCALE), R=[pr], W=[tr_[0]])
                        K.op(act, lambda e: e.activation(out=v1[b][0:n, 0:128], in_=pt[0:n, 128:256], func=AF.Copy), R=[pr], W=[tr_[1]])
                        eb = i % 2
                        K.op(act, lambda e: e.activation(out=eo[eb][0:n, :], in_=pt[0:n, 256:384], func=AF.Exp, scale=-1.0), R=[pr], W=[eor[eb]])
                        K.op(act, lambda e: e.activation(out=eo[eb][0:n, :], in_=eo[eb][0:n, :], func=AF.Ln, bias=1.0, scale=1.0),
                             R=[eor[eb]], W=[eor[eb]])
                        K.op(act, lambda e: e.activation(out=sgo[b][0:n, :], in_=eo[eb][0:n, :], func=AF.Exp, scale=-1.0),
                             R=[eor[eb]], W=[tr_[3]])
                        K.op(dve, lambda e: e.tensor_scalar(out=vwx[b][0:n, 0:128], in0=v1[b][0:n, 0:128], scalar1=ew, scalar2=None,
                                                            op0=ALU.mult), R=[tr_[1], colr], W=[tr_[2]])
                        K.op(act, lambda e: e.activation(out=vwx[b][0:n, 128:129], in_=ew, func=AF.Copy), R=[colr], W=[tr_[2]])
                        ps_, psr_ = K.ps[2], K.psr[2]
                        K.mm(ps_[0:n, 0:n], [(kT, qT)], R=[qkr[qb]], W=[psr_])
                        msk = smask_b if issample else causal_b[0:n, 0:n]
                        if n < 128:
                            K.op(dve, lambda e: e.memset(stw[b][:, 0:n], 0.0), W=[tr_[4]])
                        K.op(dve, lambda e: e.scalar_tensor_tensor(out=stw[b][0:n, 0:n], in0=ps_[0:n, 0:n], scalar=ea,
                                                                   in1=msk, op0=ALU.mult, op1=ALU.mult),
                             R=[psr_, colr, cbr], W=[tr_[4]])
                        if issample:
                            K.op(dve, lambda e: e.tensor_copy(out=qd_diag, in_=qT.rearrange("p (t b) -> p b t", b=NSQ)),
                                 R=[qkr[qb], qdr], W=[qdr])
                        if ci == NCH - 1:
                            r8rel()

                    def prompt_state(c):
                        h, ci, n, b, tr_ = c["h"], c["ci"], c["n"], c["b"], c["tr_"]
                        pu_, pur = K.ps[6], K.psr[6]
                        K.mm(pu_[:, 0:129], [(ktok[b][0:n, :], vwx[b][0:n, :])], R=[tr_[0], tr_[2]], W=[pur])
                        if ci == 0:
                            K.op(dve, lambda e: e.tensor_copy(out=CTn[:], in_=pu_[:, 0:129]), R=[pur, ctr], W=[ctr])
                        else:
                            K.op(dve, lambda e: e.scalar_tensor_tensor(out=CTn[:], in0=CTn[:], scalar=dcp[:, h, ci:ci + 1],
                                                                       in1=pu_[:, 0:129], op0=ALU.mult, op1=ALU.add),
                                 R=[pur, ctr, colr], W=[ctr])
                        if ci == NCH - 2:
                            K.dma(sp, pCT_d[l, h], CTn[:], R=[ctr])
                        else:
                            K.op(act, lambda e: e.activation(out=CTb[:], in_=CTn[:], func=AF.Copy), R=[ctr, ctbr], W=[ctbr])

                    def sample_state(c):
                        h, n, b, tr_, ew = c["h"], c["n"], c["b"], c["tr_"], c["ew"]
                        K.op(dve, lambda e: e.tensor_scalar(out=ewb[:], in0=ind_f, scalar1=ew, scalar2=None,
                                                            op0=ALU.mult), R=[cfr, colr], W=[ewbr])
                        pu_, pur = K.ps[6], K.psr[6]
                        K.mm(pu_[0:NSQ, 0:128], [(ewb[:], ktok[b][0:n, :])], R=[ewbr, tr_[0]], W=[pur])
                        K.op(dve, lambda e: e.scalar_tensor_tensor(out=n0[:], in0=n0[:], scalar=dcsT[:, h:h + 1],
                                                                   in1=pu_[0:NSQ, 0:128], op0=ALU.mult, op1=ALU.add),
                             R=[n0r, colr, pur], W=[n0r])
                        K.dma(sp, on_d[l, h], n0[:], R=[n0r])
                        for bb in range(NSQ):
                            K.op(dve, lambda e: e.tensor_scalar(out=vwb[bb][:], in0=vwx[b][0:n, 0:128],
                                                                scalar1=ind_f[:, bb:bb + 1], scalar2=None, op0=ALU.mult),
                                 R=[tr_[2], cfr], W=[vwbr[bb]])
                        bk = lambda bb: (K.ps[6], K.psr[6]) if bb % 2 == 0 else (K.ps[7], K.psr[7])

                        def mmb(bb):
                            pc_, pcr_ = bk(bb)
                            K.mm(pc_[:, 0:128], [(vwb[bb][:], ktok[b][0:n, :])], R=[vwbr[bb], tr_[0]], W=[pcr_])

                        mmb(0)
                        mmb(1)
                        for bb in range(NSQ):
                            pc_, pcr_ = bk(bb)
                            K.op(dve, lambda e: e.scalar_tensor_tensor(out=C0[:, bb, :], in0=C0[:, bb, :],
                                                                       scalar=dcs[:, h, bb:bb + 1], in1=pc_[:, 0:128],
                                                                       op0=ALU.mult, op1=ALU.add),
                                 R=[c0r[bb], colr, pcr_], W=[c0r[bb]])
                            if bb + 2 < NSQ:
                                mmb(bb + 2)
                        K.dma(sp, oC_d[l, h], C0[:], R=c0r)

                    def stageB(i):
                        c = ctx[i]
                        h, ci, n, b, tr_, qT, qb = c["h"], c["ci"], c["n"], c["b"], c["tr_"], c["qT"], c["qb"]
                        pn_, pnr = bank_n()
                        c["pn"] = (pn_, pnr)
                        if c["issample"]:
                            grp = [(pn_[0:n, 0:129], stw[b][:, 0:n], v1[b][:, :], True, False)]
                            for bb in range(NSQ):
                                grp.append((pn_[0:n, 0:129], qd_rows[:, bb, :], CT0[:, bb, 0:129], False, bb == NSQ - 1))
                            K.mm_multi(grp, R=[tr_[4], tr_[1], qdr, ct0r], W=[pnr])
                        elif ci == 0:
                            K.mm(pn_[0:n, 0:129], [(stw[b][:, 0:n], v1[b][:, :])], R=[tr_[4], tr_[1]], W=[pnr])
                        else:
                            K.mm(pn_[0:n, 0:129], [(stw[b][:, 0:n], v1[b][:, :]), (qT, CTb[:])],
                                 R=[tr_[4], tr_[1], qkr[qb], ctbr], W=[pnr])
                        pb3 = i % 3
                        c["pb3"] = pb3
                        jb = i % 2
                        K.op(act, lambda e: e.activation(out=junk[jb][0:n, :], in_=pn_[0:n, 0:128], func=AF.Square,
                                                         accum_out=pcol[pb3][0:n, 2:3]),
                             R=[pnr, postr[jb][2]], W=[pcr[pb3], postr[jb][2]])
                        K.op(act, lambda e: e.activation(out=pcol[pb3][0:n, 0:1], in_=pn_[0:n, 128:129], func=AF.Square),
                             R=[pnr, pcr[pb3]], W=[pcr[pb3]])
                        if not c["issample"]:
                            prompt_state(c)

                    def stageC1(i):
                        c = ctx[i]
                        n, fl = c["n"], c["fl"]
                        pn_, pnr = c["pn"]
                        pc = pcol[c["pb3"]]
                        r_ = pcr[c["pb3"]]
                        K.op(dve, lambda e: e.tensor_tensor(out=pc[0:n, 0:1], in0=pc[0:n, 0:1], in1=fl, op=ALU.max),
                             R=[colr, r_], W=[r_])
                        K.op(dve, lambda e: e.scalar_tensor_tensor(out=pc[0:n, 3:4], in0=pc[0:n, 0:1], scalar=DH * EPS, in1=pc[0:n, 2:3],
                                                                   op0=ALU.mult, op1=ALU.add), R=[r_], W=[r_])
                        K.op(act, lambda e: e.activation(out=pc[0:n, 4:5], in_=pc[0:n, 3:4], func=AF.Ln, scale=1.0 / DH), R=[r_], W=[r_])
                        K.op(act, lambda e: e.activation(out=pc[0:n, 5:6], in_=pc[0:n, 4:5], func=AF.Exp, scale=-0.5), R=[r_], W=[r_])

                    def stageC2a(i):
                        c = ctx[i]
                        h, ci, c0, n, b, tr_, ti = c["h"], c["ci"], c["c0"], c["n"], c["b"], c["tr_"], c["ti"]
                        pn_, pnr = c["pn"]
                        pc = pcol[c["pb3"]]
                        r_ = pcr[c["pb3"]]
                        pb = i % 2
                        po_ = postr[pb]
                        K.op(dve, lambda e: e.scalar_tensor_tensor(out=hgt[pb][0:n, :], in0=pn_[0:n, 0:128], scalar=pc[0:n, 5:6],
                                                                   in1=gml[0:n, h * 128:(h + 1) * 128],
                                                                   op0=ALU.mult, op1=ALU.mult),
                             R=[pnr, r_, gmlr, po_[0]], W=[po_[0]])
                        K.op(dve, lambda e: e.tensor_tensor(out=hmt[pb][0:n, :], in0=hgt[pb][0:n, :], in1=sgo[b][0:n, :],
                                                            op=ALU.mult), R=[po_[0], tr_[3], po_[1]], W=[po_[1]])

                    def stageC2b(i):
                        c = ctx.pop(i)
                        h, c0, n, ti = c["h"], c["c0"], c["n"], c["ti"]
                        pb = i % 2
                        po_ = postr[pb]
                        ph_, phr = K.ps[7], K.psr[7]
                        phb = ph_[:].bitcast(BF16)
                        K.tr(phb[:, 0:n], hmt[pb][0:n, :], ident_b[0:n, 0:n], R=[po_[1], cbr], W=[phr])
                        K.op(act, lambda e: e.activation(out=mixin[:, h, c0:c0 + n], in_=phb[:, 0:n], func=AF.Copy),
                             R=[phr], W=[mixr[h][ti]])
                        if c["issample"]:
                            sample_state(c)
                            if h + 1 < H:
                                load_states(h + 1)

                    load_states(0)
                    NI = len(its)
                    for i in range(NI + 5):
                        if 0 <= i - 5 < NI:
                            stageC2b(i - 5)
                        if 0 <= i - 2 < NI:
                            stageB(i - 2)
                        if 0 <= i - 3 < NI:
                            stageC1(i - 3)
                        if 0 <= i - 4 < NI:
                            stageC2a(i - 4)
                        if i < NI:
                            stageA(i)
                K.barrier()
                if stage < 4:
                    return
                with ExitStack() as ls:
                    S2 = lambda nm, shp, dt=F32: ls.enter_context(nc.sbuf_tensor(UN(nm), list(shp), dt))
                    wrg = S2("l_wrg", [128, 2, 4, 128], BF16)
                    wrgr = Res()
                    K.dma(pool, wrg[:], wrg_d[l], W=[wrgr])
                    wl = [r8(), r8()]
                    sc = S2("l_sc", [128, 4])
                    scr = Res()
                    K.op(act, lambda e: e.activation(out=sc[:], in_=V("lru_lambda", l, 0, 4), func=AF.Exp, scale=-1.0),
                         R=[vecr], W=[scr])
                    K.op(act, lambda e: e.activation(out=sc[:], in_=sc[:], func=AF.Ln, bias=1.0, scale=1.0), R=[scr], W=[scr])
                    K.op(act, lambda e: e.mul(out=sc[:], in_=sc[:], mul=-8.0), R=[scr], W=[scr])
                    sc2 = S2("l_sc2", [128, 4])
                    K.op(act, lambda e: e.mul(out=sc2[:], in_=sc[:], mul=2.0), R=[scr], W=[scr])
                    nbg = S2("l_nbg", [128, 8])
                    K.op(act, lambda e: e.mul(out=nbg[:, 0:4], in_=V("b_rg_a", l, 0, 4), mul=-1.0), R=[vecr, scr], W=[scr])
                    K.op(act, lambda e: e.mul(out=nbg[:, 4:8], in_=V("b_rg_x", l, 0, 4), mul=-1.0), R=[vecr, scr], W=[scr])
                    mk = lambda nm, n_, shp, dt=F32: ([S2(f"{nm}{i}", shp, dt) for i in range(n_)], [Res() for _ in range(n_)])
                    xet, xetr = mk("l_xet", 2, [128, 3 + 512])
                    ygf, ygfr = mk("l_ygf", 2, [128, 512])
                    xc, xcr = mk("l_xc", 3, [128, 512])
                    xcb, xcbr = mk("l_xcb", 2, [128, 512], BF16)
                    tmp, tmpr = mk("l_tmp", 2, [128, 512])
                    glb, glbr = mk("l_glb", 3, [128, 512], BF16)
                    gr, grr = mk("l_gr", 2, [128, 512])
                    gi, gir = mk("l_gi", 2, [128, 512])
                    ts_, tsr = mk("l_ts", 2, [128, 512])
                    hcur, hcurr = mk("l_hc", 2, [128, 512])
                    sqh, sqhr = mk("l_sqh", 2, [128, 512], BF16)
                    car = S2("l_car", [128, 4, 3])
                    carr = RL(4)
                    xes = S2("l_xes", [128, 4, 7, NSQ])
                    xesr = RL(4)
                    hcar = S2("l_hcar", [128, 4])
                    hcr = RL(4)
                    h0s = S2("l_h0s", [128, 4, NSQ])
                    h0r = Res()
                    hs_o = S2("l_hso", [128, 4, NSQ])
                    php = S2("l_php", [128, 4])
                    hsor = Res()
                    rsl = S2("l_rsl", [128, 512])
                    rslr = Res()
                    K.dma(sp, xes[:, :, 0:3, :], scv_d[l], W=xesr)
                    K.dma(sp, h0s[:], sh_d[l], W=[h0r])
                    K.op(dve, lambda e: e.memset(car[:], 0.0), W=carr)
                    items = [(ti, cc) for ti in range(5) for cc in range(4)]
                    NIT = len(items)
                    rotx = {"x": 0, "y": 0}

                    def T1(k):
                        ti, cc = items[k]
                        c0, n = TILES[ti]
                        npr = min(n, TP - c0)
                        b2 = k % 2
                        wt, wr = wl[cc // 2]
                        o_ = (cc % 2) * 256
                        px, pxr = K.ps[rotx["x"]], K.psr[rotx["x"]]
                        rotx["x"] = (rotx["x"] + 1) % 2
                        py, pyr = K.ps[2 + rotx["y"]], K.psr[2 + rotx["y"]]
                        rotx["y"] = (rotx["y"] + 1) % 2
                        rr = [xnr[kc][ti] for kc in range(NKC)] + [wr]
                        K.mm(px[:, 0:n], [(wt[:, kc, o_:o_ + 128], xn[:, kc, c0:c0 + n]) for kc in range(NKC)], R=rr, W=[pxr])
                        K.mm(py[:, 0:n], [(wt[:, kc, o_ + 128:o_ + 256], xn[:, kc, c0:c0 + n]) for kc in range(NKC)], R=rr, W=[pyr])
                        K.op(act, lambda e: e.activation(out=xet[b2][:, 3:3 + npr], in_=px[:, 0:npr], func=AF.Copy),
                             R=[pxr], W=[xetr[b2]])
                        K.op(act, lambda e: e.activation(out=xet[b2][:, 0:3], in_=car[:, cc, :], func=AF.Copy),
                             R=[carr[cc]], W=[xetr[b2]])
                        K.op(act, lambda e: e.activation(out=ygf[b2][:, 0:n], in_=py[:, 0:n], func=AF.Copy), R=[pyr], W=[ygfr[b2]])
                        if ti == 4:
                            K.op(act, lambda e: e.activation(
                                out=xes[:, cc, 3:7, :], in_=px[:, npr:n].rearrange("p (t b) -> p t b", b=NSQ), func=AF.Copy),
                                R=[pxr], W=[xesr[cc]])
                            K.dma(sp, pcv_d[l, :, cc, :], xet[b2][:, npr:npr + 3], R=[xetr[b2]])
                            K.dma(sp, ocv_d[l, :, cc], xes[:, cc, 4:7, :], R=[xesr[cc]])
                        else:
                            K.op(act, lambda e: e.activation(out=car[:, cc, :], in_=xet[b2][:, n:n + 3], func=AF.Copy),
                                 R=[xetr[b2]], W=[carr[cc]])

                    def T2(k):
                        ti, cc = items[k]
                        c0, n = TILES[ti]
                        npr = min(n, TP - c0)
                        b2, b3 = k % 2, k % 3
                        cw = lambda j: V("conv_w", l, j * 4 + cc)
                        K.op(dve, lambda e: e.tensor_scalar(out=xc[b3][:, 0:npr], in0=xet[b2][:, 3:3 + npr], scalar1=cw(3),
                                                            scalar2=V("conv_b", l, cc), op0=ALU.mult, op1=ALU.add),
                             R=[xetr[b2], vecr], W=[xcr[b3]])
                        for j in range(1, 4):
                            K.op(dve, lambda e: e.scalar_tensor_tensor(out=xc[b3][:, 0:npr], in0=xet[b2][:, 3 - j:3 - j + npr],
                                                                       scalar=cw(3 - j), in1=xc[b3][:, 0:npr],
                                                                       op0=ALU.mult, op1=ALU.add),
                                 R=[xetr[b2], vecr, xcr[b3]], W=[xcr[b3]])
                        if ti == 4:
                            xcs = xc[b3][:, npr:n].rearrange("p (t b) -> p t b", b=NSQ)
                            K.op(dve, lambda e: e.tensor_scalar(out=xcs, in0=xes[:, cc, 3:7, :], scalar1=cw(3),
                                                                scalar2=V("conv_b", l, cc), op0=ALU.mult, op1=ALU.add),
                                 R=[xesr[cc], vecr, xcr[b3]], W=[xcr[b3]])
                            for j in range(1, 4):
                                K.op(dve, lambda e: e.scalar_tensor_tensor(out=xcs, in0=xes[:, cc, 3 - j:7 - j, :],
                                                                           scalar=cw(3 - j), in1=xcs, op0=ALU.mult, op1=ALU.add),
                                     R=[xesr[cc], vecr, xcr[b3]], W=[xcr[b3]])
                        K.op(dve, lambda e: e.tensor_copy(out=xcb[b2][:, 0:n], in_=xc[b3][:, 0:n]), R=[xcr[b3]], W=[xcbr[b2]])
                        pa, par = K.ps[4], K.psr[4]
                        pi2, pir2 = K.ps[5], K.psr[5]
                        K.mm(pa[:, 0:n], [(wrg[:, 0, cc, :], xcb[b2][:, 0:n])], R=[wrgr, xcbr[b2]], W=[par])
                        K.mm(pi2[:, 0:n], [(wrg[:, 1, cc, :], xcb[b2][:, 0:n])], R=[wrgr, xcbr[b2]], W=[pir2])
                        K.op(dve, lambda e: e.tensor_tensor(out=tmp[b2][:, 0:n], in0=ygf[b2][:, 0:n], in1=ygf[b2][:, 0:n], op=ALU.mult),
                             R=[ygfr[b2]], W=[tmpr[b2]])
                        K.op(dve, lambda e: e.tensor_scalar(out=tmp[b2][:, 0:n], in0=tmp[b2][:, 0:n], scalar1=0.044715, scalar2=1.0,
                                                            op0=ALU.mult, op1=ALU.add), R=[tmpr[b2]], W=[tmpr[b2]])
                        K.op(dve, lambda e: e.tensor_tensor(out=tmp[b2][:, 0:n], in0=tmp[b2][:, 0:n], in1=ygf[b2][:, 0:n], op=ALU.mult),
                             R=[tmpr[b2], ygfr[b2]], W=[tmpr[b2]])
                        K.op(act, lambda e: e.activation(out=tmp[b2][:, 0:n], in_=tmp[b2][:, 0:n], func=AF.Exp,
                                                         scale=-1.5957691216057308), R=[tmpr[b2]], W=[tmpr[b2]])
                        K.op(act, lambda e: e.activation(out=tmp[b2][:, 0:n], in_=tmp[b2][:, 0:n], func=AF.Ln, bias=1.0, scale=1.0),
                             R=[tmpr[b2]], W=[tmpr[b2]])
                        K.op(act, lambda e: e.activation(out=tmp[b2][:, 0:n], in_=tmp[b2][:, 0:n], func=AF.Exp, scale=-1.0),
                             R=[tmpr[b2]], W=[tmpr[b2]])
                        K.op(dve, lambda e: e.tensor_tensor(out=glb[b3][:, 0:n], in0=ygf[b2][:, 0:n], in1=tmp[b2][:, 0:n], op=ALU.mult),
                             R=[tmpr[b2], ygfr[b2]], W=[glbr[b3]])

                    def T3(k):
                        ti, cc = items[k]
                        c0, n = TILES[ti]
                        b2 = k % 2
                        pa, par = K.ps[4], K.psr[4]
                        pi2, pir2 = K.ps[5], K.psr[5]
                        g_, g_r, i_, i_r, t_, t_r = gr[b2], grr[b2], gi[b2], gir[b2], ts_[b2], tsr[b2]
                        K.op(act, lambda e: e.activation(out=g_[:, 0:n], in_=pa[:, 0:n], func=AF.Exp,
                                                         bias=nbg[:, cc:cc + 1], scale=-1.0), R=[par, scr], W=[g_r])
                        K.op(act, lambda e: e.activation(out=i_[:, 0:n], in_=pi2[:, 0:n], func=AF.Exp,
                                                         bias=nbg[:, 4 + cc:5 + cc], scale=-1.0), R=[pir2, scr], W=[i_r])
                        K.op(act, lambda e: e.activation(out=g_[:, 0:n], in_=g_[:, 0:n], func=AF.Ln, bias=1.0, scale=1.0), R=[g_r], W=[g_r])
                        K.op(act, lambda e: e.activation(out=i_[:, 0:n], in_=i_[:, 0:n], func=AF.Ln, bias=1.0, scale=1.0), R=[i_r], W=[i_r])
                        K.op(act, lambda e: e.activation(out=g_[:, 0:n], in_=g_[:, 0:n], func=AF.Exp, scale=-1.0), R=[g_r], W=[g_r])
                        K.op(act, lambda e: e.activation(out=i_[:, 0:n], in_=i_[:, 0:n], func=AF.Exp, scale=-1.0), R=[i_r], W=[i_r])
                        K.op(act, lambda e: e.activation(out=t_[:, 0:n], in_=g_[:, 0:n], func=AF.Exp, scale=sc2[:, cc:cc + 1]),
                             R=[g_r, scr], W=[t_r])
                        K.op(act, lambda e: e.activation(out=g_[:, 0:n], in_=g_[:, 0:n], func=AF.Exp, scale=sc[:, cc:cc + 1]),
                             R=[g_r, scr], W=[g_r])
                        K.op(act, lambda e: e.activation(out=t_[:, 0:n], in_=t_[:, 0:n], func=AF.Ln, scale=-1.0, bias=1.0), R=[t_r], W=[t_r])
                        K.op(act, lambda e: e.activation(out=t_[:, 0:n], in_=t_[:, 0:n], func=AF.Exp, scale=0.5), R=[t_r], W=[t_r])

                    def T4(k):
                        ti, cc = items[k]
                        c0, n = TILES[ti]
                        npr = min(n, TP - c0)
                        b2, b3 = k % 2, k % 3
                        g_, g_r, i_, i_r, t_, t_r = gr[b2], grr[b2], gi[b2], gir[b2], ts_[b2], tsr[b2]
                        hc, hcr_ = hcur[b2], hcurr[b2]
                        K.op(dve, lambda e: e.tensor_tensor(out=i_[:, 0:n], in0=i_[:, 0:n], in1=xc[b3][:, 0:n], op=ALU.mult),
                             R=[i_r, xcr[b3]], W=[i_r])
                        K.op(dve, lambda e: e.tensor_tensor(out=i_[:, 0:n], in0=i_[:, 0:n], in1=t_[:, 0:n], op=ALU.mult),
                             R=[i_r, t_r], W=[i_r])
                        init = 0.0 if ti == 0 else hcar[:, cc:cc + 1]
                        K.op(dve, lambda e: e.tensor_tensor_scan(out=hc[:, 0:npr], data0=g_[:, 0:npr], data1=i_[:, 0:npr],
                                                                 initial=init, op0=ALU.mult, op1=ALU.add),
                             R=[g_r, i_r, hcr[cc]], W=[hcr_])
                        if ti < 4:
                            K.op(act, lambda e: e.activation(out=hcar[:, cc:cc + 1], in_=hc[:, n - 1:n], func=AF.Copy),
                                 R=[hcr_], W=[hcr[cc]])
                        else:
                            K.op(act, lambda e: e.activation(out=php[:, cc:cc + 1], in_=hc[:, npr - 1:npr], func=AF.Copy),
                                 R=[hcr_], W=[hsor])
                            for t in range(NST):
                                s0 = npr + 16 * t
                                prev = h0s[:, cc, :] if t == 0 else hc[:, s0 - 16:s0]
                                K.op(dve, lambda e: e.tensor_tensor(out=hc[:, s0:s0 + 16], in0=g_[:, s0:s0 + 16], in1=prev,
                                                                    op=ALU.mult), R=[g_r, h0r, hcr_], W=[hcr_])
                                K.op(dve, lambda e: e.tensor_tensor(out=hc[:, s0:s0 + 16], in0=hc[:, s0:s0 + 16],
                                                                    in1=i_[:, s0:s0 + 16], op=ALU.add), R=[i_r, hcr_], W=[hcr_])
                            K.op(act, lambda e: e.activation(out=hs_o[:, cc, :], in_=hc[:, npr + 48:npr + 64], func=AF.Copy),
                                 R=[hcr_], W=[hsor])
                        K.op(dve, lambda e: e.tensor_tensor(out=sqh[b2][:, 0:n], in0=hc[:, 0:n], in1=hc[:, 0:n], op=ALU.mult), R=[hcr_], W=[sqhr[b2]])
                        pt, pr = K.ps[6], K.psr[6]
                        K.mm_multi([(pt[:, 0:n], ones_b, sqh[b2][:, 0:n], cc == 0, cc == 3)], R=[sqhr[b2], cbr], W=[pr])
                        K.op(dve, lambda e: e.tensor_tensor(out=mixin[:, 4 + cc, c0:c0 + n], in0=hc[:, 0:n], in1=glb[b3][:, 0:n],
                                                            op=ALU.mult), R=[hcr_, glbr[b3]], W=[mixr[4 + cc][ti]])
                        if cc == 3:
                            K.op(act, lambda e: e.activation(out=rsl[:, 0:n], in_=pt[:, 0:n], func=AF.Ln, scale=1.0 / DLRU,
                                                             bias=epsc[:, 0:1]), R=[pr, cbr], W=[rslr])
                            K.op(act, lambda e: e.activation(out=rsl[:, 0:n], in_=rsl[:, 0:n], func=AF.Exp, scale=-0.5),
                                 R=[rslr], W=[rslr])

                    def T5(k):
                        ti, cc = items[k]
                        if cc != 3:
                            return
                        c0, n = TILES[ti]
                        for c2 in range(4):
                            K.op(dve, lambda e: e.scalar_tensor_tensor(out=mixin[:, 4 + c2, c0:c0 + n], in0=mixin[:, 4 + c2, c0:c0 + n],
                                                                       scalar=V("g_lru_out", l, c2), in1=rsl[:, 0:n],
                                                                       op0=ALU.mult, op1=ALU.mult),
                                 R=[mixr[4 + c2][ti], vecr, rslr], W=[mixr[4 + c2][ti]])

                    for m in range(NIT + 4):
                        if 0 <= m - 4 < NIT:
                            T5(m - 4)
                        if 0 <= m - 3 < NIT:
                            T4(m - 3)
                        if 0 <= m - 2 < NIT:
                            T3(m - 2)
                        if 0 <= m - 1 < NIT:
                            T2(m - 1)
                        if m < NIT:
                            T1(m)
                    K.dma(sp, oh_d[l], hs_o[:], R=[hsor])
                    K.dma(sp, ph_d[l], php[:], R=[hsor])
                    r8rel()
                    r8rel()
                K.barrier()
                if dbg_d is not None and l == 0:
                    K.dma(sp, dbg_d, mixin[:], R=mixr)
                if stage < 5:
                    return
                wo = [r8(), r8()]
                with ExitStack() as os_:
                    S2 = lambda nm, shp, dt=F32: os_.enter_context(nc.sbuf_tensor(UN(nm), list(shp), dt))
                    if post_norm:
                        sq = S2("w_sq", [128, NKC, 512], BF16)
                        sqr = Res()
                        rs = [S2("w_rs0", [128, 512]), S2("w_rs1", [128, 512])]
                        rsr = [Res(), Res()]
                    for ti, (c0, n) in enumerate(TILES):
                        for m in range(NKC):
                            wt, wr = wo[m // 4]
                            mo = m % 4
                            pt, pr = K.psn()
                            K.mm(pt[:, 0:n], [(wt[:, kc, mo * 128:(mo + 1) * 128], mixin[:, kc, c0:c0 + n]) for kc in range(NKC)],
                                 R=[mixr[kc][ti] for kc in range(NKC)] + [wr], W=[pr])
                            K.op(dve, lambda e: e.tensor_tensor(out=x[:, m, c0:c0 + n], in0=pt[:, 0:n], in1=x[:, m, c0:c0 + n],
                                                                op=ALU.add), R=[pr, xr_[m][ti]], W=[xr_[m][ti]])
                        if post_norm and ti >= 1:
                            rmsnorm("g_ff2", l, xn, xnr, sq, sqr, rs, rsr, tiles=[ti - 1])
                    r8rel()
                    r8rel()
                    if post_norm:
                        rmsnorm("g_ff2", l, xn, xnr, sq, sqr, rs, rsr, tiles=[len(TILES) - 1])
            K.barrier()

        ones_f4 = SB("ones_f4", [H, 128])
        K.op(dve, lambda e: e.memset(ones_f4[:], 1.0), W=[cbr])

        MERGE = (stage >= 99) and not skip_ffn
        if MERGE:
            def post_norm_fn(gname, l_):
                return lambda sq, sqr, rs, rsr, ti: rmsnorm(gname, l_, xn, xnr, sq, sqr, rs, rsr, tiles=[ti])

            def post_final(sq, sqr, rs, rsr, ti):
                rmsnorm("g_final", 0, x, xr_, sq, sqr, rs, rsr, tiles=[ti])
                c0, n = TILES[ti]
                for kc in range(NKC):
                    K.dma(sp, yT_d[:, kc, c0:c0 + n], x[:, kc, c0:c0 + n], R=[xr_[kc][ti]])

            for l in range(DEPTH):
                ffn(l, 0, pre_norm=(l == 0), post=post_norm_fn("g_mix", l))
                mixer(l, pre_norm=False, post_norm=True)
                ffn(l, 1, pre_norm=False, post=(post_norm_fn("g_ff1", l + 1) if l + 1 < DEPTH else post_final))
            K.finish()
        else:
            for l in range(DEPTH):
                if stage >= 1 and not skip_ffn:
                    ffn(l, 0)
                if stage >= 2:
                    mixer(l)
                if stage >= 6 and not skip_ffn:
                    ffn(l, 1)
                if stage < 7:
                    break
            with ExitStack() as fs:
                S = lambda nm, shp, dt=F32: fs.enter_context(nc.sbuf_tensor(UN(nm), list(shp), dt))
                sq = S("o_sq", [128, NKC, 512], BF16)
                sqr = Res()
                rs = [S("o_rs0", [128, 512]), S("o_rs1", [128, 512])]
                rsr = [Res(), Res()]
                rmsnorm("g_final", 0, x, xr_, sq, sqr, rs, rsr)
                for kc in range(NKC):
                    K.dma(sp, yT_d[:, kc, :], x[:, kc, :], R=xr_[kc])
                K.finish()
    return nc


def _consts():
    cf = np.zeros((128, 512), np.float32)
    cf[:, 0:128] = np.eye(128, dtype=np.float32)
    s = np.arange(128)
    cf[:, 128:256] = (s[:, None] <= s[None, :]).astype(np.float32)
    i = np.arange(64)
    same = (i[:, None] % 16) == (i[None, :] % 16)
    caus = (i[:, None] // 16) <= (i[None, :] // 16)
    cf[0:64, 256:320] = (same & caus).astype(np.float32)
    cf[0:64, 320:336] = ((i[:, None] % 16) == np.arange(16)[None, :]).astype(np.float32)
    return cf


def _prep_shared(W):
    f = lambda a: np.ascontiguousarray(a, dtype=np.float32)
    out = {}
    wgu = np.empty((DEPTH, 2, NJP, 128, NKC, 512), np.float32)
    wdn = np.empty((DEPTH, 2, NJP, 128, 2, 1024), np.float32)
    for fi, (gn, un, dn) in enumerate((("w_ff1_gate", "w_ff1_up", "w_ff1_down"), ("w_ff2_gate", "w_ff2_up", "w_ff2_down"))):
        g = W[gn].reshape(DEPTH, NKC, 128, NJP, 256).transpose(0, 3, 2, 1, 4)
        u = W[un].reshape(DEPTH, NKC, 128, NJP, 256).transpose(0, 3, 2, 1, 4)
        wgu[:, fi, :, :, :, 0:256] = g
        wgu[:, fi, :, :, :, 256:512] = u
        wdn[:, fi] = W[dn].reshape(DEPTH, NJP, 2, 128, 1024).transpose(0, 1, 3, 2, 4)
    out["wgu"] = wgu
    out["wdn"] = wdn
    win = W["w_in"].reshape(DEPTH, NKC, 128, 3080)
    wq = np.empty((DEPTH, H, 128, NKC, 512), np.float32)
    for h in range(H):
        for qi in range(4):
            wq[:, h, :, :, qi * 128:(qi + 1) * 128] = win[:, :, :, qi * 512 + h * 128: qi * 512 + (h + 1) * 128].transpose(0, 2, 1, 3)
    out["win"] = wq
    out["wgt"] = f(win[:, :, :, 2048:2056].transpose(0, 2, 1, 3))
    wl = np.empty((DEPTH, 2, 128, NKC, 512), np.float32)
    for cc in range(4):
        o = (cc % 2) * 256
        wl[:, cc // 2, :, :, o:o + 128] = win[:, :, :, 2056 + cc * 128:2056 + (cc + 1) * 128].transpose(0, 2, 1, 3)
        wl[:, cc // 2, :, :, o + 128:o + 256] = win[:, :, :, 2568 + cc * 128:2568 + (cc + 1) * 128].transpose(0, 2, 1, 3)
    out["wlr"] = wl
    wrg = np.zeros((DEPTH, 128, 2, 4, 128), np.float32)
    for gi, nm in enumerate(("w_rg_a", "w_rg_x")):
        for nb in range(8):
            cc, half = nb // 2, nb % 2
            wrg[:, half * 64:(half + 1) * 64, gi, cc, half * 64:(half + 1) * 64] = W[nm][:, nb]
    out["wrg"] = wrg
    out["wout"] = f(W["w_out"].reshape(DEPTH, NKC, 128, 2, 512).transpose(0, 3, 2, 1, 4))
    vec = np.zeros((128, NV), np.float32)
    for l in range(DEPTH):
        for nm in ("g_ff1", "g_mix", "g_ff2"):
            vec[:, VOFF[(nm, l)]:VOFF[(nm, l)] + 8] = W[nm][l].reshape(8, 128).T
        o = VOFF[("conv_w", l)]
        vec[:, o:o + 16] = W["conv_w"][l].reshape(4, 4, 128).transpose(2, 0, 1).reshape(128, 16)
        for nm in ("conv_b", "b_rg_a", "b_rg_x", "lru_lambda", "g_lru_out"):
            vec[:, VOFF[(nm, l)]:VOFF[(nm, l)] + 4] = W[nm][l].reshape(4, 128).T
        vec[0:4, VOFF[("b_i", l)]] = W["b_gates"][l, 0:4]
        vec[0:4, VOFF[("b_f", l)]] = W["b_gates"][l, 4:8]
    vec[:, VOFF[("g_final", 0)]:VOFF[("g_final", 0)] + 8] = W["g_final"].reshape(8, 128).T
    out["vec"] = vec
    out["gml"] = f(np.broadcast_to(W["g_mlstm_out"][None], (128, DEPTH, 512)))
    out["cf"] = _consts()
    return out


def _prep_core(c, A):
    sl = slice(NSQ * c, NSQ * (c + 1))
    X = np.concatenate([A["meta_tokens"], A["x_prompt"][c],
                        A["x_sample"][sl].transpose(1, 0, 2).reshape(NS, D)], axis=0)
    m = {}
    m["xT"] = np.ascontiguousarray(X.T.reshape(NKC, 128, T).transpose(1, 0, 2))
    C = A["state_mlstm_C"][:, sl]
    m["sCT"] = np.ascontiguousarray(C.transpose(0, 2, 4, 1, 3))
    m["sC"] = np.ascontiguousarray(C.transpose(0, 2, 3, 1, 4))
    n = A["state_mlstm_n"][:, sl]
    m["snT"] = np.ascontiguousarray(n.transpose(0, 2, 3, 1))
    m["sn"] = np.ascontiguousarray(n.transpose(0, 2, 1, 3))
    m["sm"] = np.ascontiguousarray(A["state_mlstm_m"][:, sl].transpose(0, 2, 1))
    m["sh"] = np.ascontiguousarray(A["state_lru_h"][:, sl].reshape(DEPTH, NSQ, 4, 128).transpose(0, 3, 2, 1))
    m["scv"] = np.ascontiguousarray(A["state_conv"][:, sl].reshape(DEPTH, NSQ, 3, 4, 128).transpose(0, 4, 3, 2, 1))
    return m


_NC_CACHE = {}


def kernel(**inputs):
    A = {k: np.asarray(v, dtype=np.float32) for k, v in inputs.items()}
    shared = _prep_shared(A)
    in_maps = []
    for c in range(NCORES):
        m = dict(shared)
        m.update(_prep_core(c, A))
        in_maps.append(m)
    if "nc" not in _NC_CACHE:
        _NC_CACHE["nc"] = build_program()
    nc = _NC_CACHE["nc"]
    res = run_bass_kernel_spmd(nc, in_maps, core_ids=list(range(NCORES)))
    R = res.results
    B = NCORES
    y_prompt = np.empty((B, SEQ, D), np.float32)
    y_sample = np.empty((B * NSQ, NST, D), np.float32)
    pC = np.empty((DEPTH, B, H, DH, DH), np.float32)
    pn = np.empty((DEPTH, B, H, DH), np.float32)
    pm = np.empty((DEPTH, B, H), np.float32)
    ph = np.empty((DEPTH, B, DLRU), np.float32)
    pcv = np.empty((DEPTH, B, 3, DLRU), np.float32)
    sC = np.empty((DEPTH, B * NSQ, H, DH, DH), np.float32)
    sn = np.empty((DEPTH, B * NSQ, H, DH), np.float32)
    sm = np.empty((DEPTH, B * NSQ, H), np.float32)
    sh = np.empty((DEPTH, B * NSQ, DLRU), np.float32)
    scv = np.empty((DEPTH, B * NSQ, 3, DLRU), np.float32)
    for c in range(B):
        r = R[c]
        sl = slice(NSQ * c, NSQ * (c + 1))
        Y = r["yT"].transpose(1, 0, 2).reshape(D, T).T
        y_prompt[c] = Y[NMETA:TP]
        y_sample[sl] = Y[TP:].reshape(NST, NSQ, D).transpose(1, 0, 2)
        pct = r["pCT"]
        pC[:, c] = pct[:, :, :, 0:128].transpose(0, 1, 3, 2)
        pn[:, c] = pct[:, :, :, 128]
        pm[:, c] = r["pm"][:, :, 0]
        ph[:, c] = r["ph"].transpose(0, 2, 1).reshape(DEPTH, DLRU)
        pcv[:, c] = r["pcv"].transpose(0, 3, 2, 1).reshape(DEPTH, 3, DLRU)
        sC[:, sl] = r["oC"].transpose(0, 3, 1, 2, 4)
        sn[:, sl] = r["on"].transpose(0, 2, 1, 3)
        sm[:, sl] = r["om"].transpose(0, 2, 1)
        sh[:, sl] = r["oh"].transpose(0, 3, 2, 1).reshape(DEPTH, NSQ, DLRU)
        scv[:, sl] = r["ocv"].transpose(0, 4, 3, 2, 1).reshape(DEPTH, NSQ, 3, DLRU)
    return (y_prompt, y_sample, pC, pn, pm, ph, pcv, sC, sn, sm, sh, scv)
```

```python
import os
import numpy as np
import ml_dtypes
KDBG = int(os.environ.get('KDBG', '9'))
KNOS = int(os.environ.get('KNOS', '0'))
KSKIP = int(os.environ.get('KSKIP', '0'))
KCH = int(os.environ.get('KCH', '99'))
KSELF = int(os.environ.get('KSELF', '1'))
from contextlib import ExitStack
import concourse.bass as bass
import concourse.mybir as mybir
from concourse.bass_utils import run_bass_kernel_spmd

F32 = mybir.dt.float32
BF16 = mybir.dt.bfloat16
AF = mybir.ActivationFunctionType
ALU = mybir.AluOpType

NCORES = 8
D = 1024
NKC = 8
SEQ = 2048
NMETA = 16
TP = NMETA + SEQ
NSQ = 16
NST = 4
NS = NSQ * NST
T = TP + NS
DFF = 2816
NJ = DFF // 128
NJP = NJ // 2
H = 4
DH = 128
DLRU = 512
DEPTH = 2
EPS = 1e-6
TILES = [(0, 512), (512, 512), (1024, 512), (1536, 512), (2048, 80)]
CHUNKS = [(128 * c, 128) for c in range(16)] + [(2048, 16), (TP, NS)]
NCH = len(CHUNKS)
GROUPS = [[0, 1, 2], [3, 4, 5], [6, 7, 8], [9, 10]]
KSCALE = DH ** -0.5

def _vec_layout():
    off = {}
    n = 0
    for l in range(DEPTH):
        for nm, w in (("g_ff1", 8), ("g_mix", 8), ("g_ff2", 8), ("conv_w", 16), ("conv_b", 4),
                      ("b_rg_a", 4), ("b_rg_x", 4), ("lru_lambda", 4), ("g_lru_out", 4),
                      ("b_i", 1), ("b_f", 1)):
            off[(nm, l)] = n
            n += w
    off[("g_final", 0)] = n
    n += 8
    return off, n

VOFF, NV = _vec_layout()


class Res:
    __slots__ = ("w", "r")

    def __init__(self):
        self.w = None
        self.r = []


def RL(*dims):
    if len(dims) == 0:
        return Res()
    return [RL(*dims[1:]) for _ in range(dims[0])]


def flat(x):
    if isinstance(x, Res):
        return [x]
    out = []
    for y in x:
        out.extend(flat(y))
    return out


class Eng:
    def __init__(self, e, sem, name):
        self.e = e
        self.sem = sem
        self.n = 0
        self.seen = {}
        self.name = name


class Ctx:
    def __init__(self, nc, es):
        self.nc = nc
        self.es = es
        mk = lambda nm: es.enter_context(nc.semaphore(nm))
        self.pe = Eng(nc.tensor, mk("c_pe"), "pe")
        self.act = Eng(nc.scalar, mk("c_act"), "act")
        self.dve = Eng(nc.vector, mk("c_dve"), "dve")
        self.pool = Eng(nc.gpsimd, mk("c_pool"), "pool")
        self.sp = Eng(nc.sync, None, "sp")
        self.compute = [self.pe, self.act, self.dve]
        self.dsems = {}
        for q, nq in ((self.sp, 16), (self.pool, 16)):
            self.dsems[q.name] = [[mk(f"d_{q.name}{i}"), 0] for i in range(nq)]
        self.dptr = {"sp": 0, "pool": 0}
        self.semid = {}
        self.ps = []
        self.psr = []
        for i in range(8):
            self.ps.append(es.enter_context(nc.psum_tensor(f"ps{i}", [128, 512], F32)))
            self.psr.append(Res())
        self.psi = 0

    def sid(self, sem):
        k = id(sem)
        if k not in self.semid:
            self.semid[k] = sem
        return k

    def psn(self):
        i = self.psi
        self.psi = (self.psi + 1) % 8
        return self.ps[i], self.psr[i]

    def _waits(self, eng, R, W):
        deps = {}
        for r in R:
            if r.w is not None:
                k = self.sid(r.w[0])
                deps[k] = max(deps.get(k, 0), r.w[1])
        for w in W:
            if w.w is not None:
                k = self.sid(w.w[0])
                deps[k] = max(deps.get(k, 0), w.w[1])
            for t in w.r:
                k = self.sid(t[0])
                deps[k] = max(deps.get(k, 0), t[1])
        for k, v in deps.items():
            if not KSELF and eng.sem is not None and k == id(eng.sem):
                continue
            if eng.seen.get(k, 0) < v:
                eng.e.wait_ge(self.semid[k], v)
                eng.seen[k] = v

    def _done(self, tok, R, W):
        for r in R:
            r.r.append(tok)
        for w in W:
            w.w = tok
            w.r = []

    def op(self, eng, fn, R=(), W=()):
        R = flat(R)
        W = flat(W)
        self._waits(eng, R, W)
        ins = fn(eng.e)
        eng.n += 1
        ins.then_inc(eng.sem, 1)
        self._done((eng.sem, eng.n), R, W)

    def mm(self, out, parts, R, W):
        R = flat(R)
        W = flat(W)
        eng = self.pe
        self._waits(eng, R, W)
        n = len(parts)
        ins = None
        for i, (l, r) in enumerate(parts):
            ins = eng.e.matmul(out, lhsT=l, rhs=r, start=(i == 0), stop=(i == n - 1))
        eng.n += 1
        ins.then_inc(eng.sem, 1)
        self._done((eng.sem, eng.n), R, W)

    def mm_multi(self, groups, R, W):
        R = flat(R)
        W = flat(W)
        eng = self.pe
        self._waits(eng, R, W)
        ins = None
        for (o, l, r, st, sp) in groups:
            ins = eng.e.matmul(o, lhsT=l, rhs=r, start=st, stop=sp)
        eng.n += 1
        ins.then_inc(eng.sem, 1)
        self._done((eng.sem, eng.n), R, W)

    def tr(self, out, in_, ident, R, W):
        R = flat(R)
        W = flat(W)
        eng = self.pe
        self._waits(eng, R, W)
        ins = eng.e.transpose(out, in_, ident)
        eng.n += 1
        ins.then_inc(eng.sem, 1)
        self._done((eng.sem, eng.n), R, W)

    def tr_multi(self, items, R, W):
        R = flat(R)
        W = flat(W)
        eng = self.pe
        self._waits(eng, R, W)
        ins = None
        for (o, i_, idn) in items:
            ins = eng.e.transpose(o, i_, idn)
        eng.n += 1
        ins.then_inc(eng.sem, 1)
        self._done((eng.sem, eng.n), R, W)

    def dma(self, q, out, in_, R=(), W=()):
        R = flat(R)
        W = flat(W)
        self._waits(q, R, W)
        pool = self.dsems[q.name]
        i = self.dptr[q.name]
        self.dptr[q.name] = (i + 1) % len(pool)
        sem, cnt = pool[i]
        k = self.sid(sem)
        if cnt > 0 and q.seen.get(k, 0) < 16 * cnt:
            q.e.wait_ge(sem, 16 * cnt)
            q.seen[k] = 16 * cnt
        q.e.dma_start(out=out, in_=in_).then_inc(sem, 16)
        pool[i][1] = cnt + 1
        self._done((sem, 16 * (cnt + 1)), R, W)

    def barrier(self):
        toks = [(e.sem, e.n) for e in (self.pe, self.act, self.dve, self.pool) if e.n > 0]
        for q in ("sp", "pool"):
            for sem, cnt in self.dsems[q]:
                if cnt > 0:
                    toks.append((sem, 16 * cnt))
        for eng in (self.pe, self.act, self.dve, self.sp, self.pool):
            for sem, v in toks:
                k = self.sid(sem)
                if eng.seen.get(k, 0) < v:
                    eng.e.wait_ge(sem, v)
                    eng.seen[k] = v

    def finish(self):
        self.barrier()


def build_program(stage=99, skip_ffn=False):
    _uc = [0]

    def UN(nm):
        _uc[0] += 1
        return f"sb{_uc[0]}_{nm}"

    nc = bass.Bass("TRN2", target_bir_lowering=False)
    I = lambda nm, shp: nc.dram_tensor(nm, list(shp), F32, kind="ExternalInput").ap()
    O = lambda nm, shp: nc.dram_tensor(nm, list(shp), F32, kind="ExternalOutput").ap()
    xT_d = I("xT", [128, NKC, T])
    vec_d = I("vec", [128, NV])
    gml_d = I("gml", [128, DEPTH, 512])
    cf_d = I("cf", [128, 512])
    wgu_d = I("wgu", [DEPTH, 2, NJP, 128, NKC, 512])
    wdn_d = I("wdn", [DEPTH, 2, NJP, 128, 2, 1024])
    win_d = I("win", [DEPTH, H, 128, NKC, 512])
    wgt_d = I("wgt", [DEPTH, 128, NKC, 8])
    wlr_d = I("wlr", [DEPTH, 2, 128, NKC, 512])
    wrg_d = I("wrg", [DEPTH, 128, 2, 4, 128])
    wout_d = I("wout", [DEPTH, 2, 128, NKC, 512])
    sCT_d = I("sCT", [DEPTH, H, 128, NSQ, 128])
    sC_d = I("sC", [DEPTH, H, 128, NSQ, 128])
    snT_d = I("snT", [DEPTH, H, 128, NSQ])
    sn_d = I("sn", [DEPTH, H, NSQ, 128])
    sm_d = I("sm", [DEPTH, H, NSQ])
    sh_d = I("sh", [DEPTH, 128, 4, NSQ])
    scv_d = I("scv", [DEPTH, 128, 4, 3, NSQ])
    yT_d = O("yT", [128, NKC, T])
    pCT_d = O("pCT", [DEPTH, H, 128, 129])
    pm_d = O("pm", [DEPTH, H, 1])
    ph_d = O("ph", [DEPTH, 128, 4])
    pcv_d = O("pcv", [DEPTH, 128, 4, 3])
    oC_d = O("oC", [DEPTH, H, 128, NSQ, 128])
    on_d = O("on", [DEPTH, H, NSQ, 128])
    om_d = O("om", [DEPTH, H, NSQ])
    oh_d = O("oh", [DEPTH, 128, 4, NSQ])
    ocv_d = O("ocv", [DEPTH, 128, 4, 3, NSQ])
    dbg_d = nc.dram_tensor("dbg", [128, NKC, T], BF16, kind="ExternalOutput").ap() if os.environ.get('KDBGOUT') else None

    with ExitStack() as es:
        K = Ctx(nc, es)
        pe, act, dve, pool, sp = K.pe, K.act, K.dve, K.pool, K.sp
        SB = lambda nm, shp, dt=F32: es.enter_context(nc.sbuf_tensor(UN(nm), list(shp), dt))

        x = SB("x", [128, NKC, T])
        xr_ = RL(NKC, 5)
        xn = SB("xn", [128, NKC, T], BF16)
        xnr = RL(NKC, 5)
        vec = SB("vec", [128, NV])
        vecr = Res()
        cf = SB("cf", [128, 512])
        cfr = Res()
        cb = SB("cb", [128, 512], BF16)
        cbr = Res()
        NR8 = 2
        R8 = [SB(f"r8_{i}", [128, NKC, 512], BF16) for i in range(NR8)]
        R8r = [Res() for _ in range(NR8)]
        w_items = []
        for l_ in range(DEPTH):
            if stage >= 1 and not skip_ffn:
                w_items += [wgu_d[l_, 0, jp] for jp in range(NJP)]
            if stage >= 3:
                w_items += [win_d[l_, h_] for h_ in range(H)]
            if stage >= 4:
                w_items += [wlr_d[l_, i_] for i_ in range(2)]
            if stage >= 5:
                w_items += [wout_d[l_, i_] for i_ in range(2)]
            if stage >= 6 and not skip_ffn:
                w_items += [wgu_d[l_, 1, jp] for jp in range(NJP)]
            if stage < 7:
                break
        wst = {"issued": 0, "consumed": 0, "released": 0}

        def _r8issue(upto):
            while wst["issued"] < min(len(w_items), upto) and wst["issued"] - NR8 < wst["released"]:
                i = wst["issued"]
                K.dma(pool, R8[i % NR8][:], w_items[i], W=[R8r[i % NR8]])
                wst["issued"] += 1

        def r8():
            k = wst["consumed"]
            wst["consumed"] += 1
            _r8issue(k + NR8)
            assert wst["issued"] > k
            return R8[k % NR8], R8r[k % NR8]

        def r8rel():
            wst["released"] += 1
            _r8issue(wst["consumed"] + NR8 - 1)

        ident_f = cf[:, 0:128]
        ident_b = cb[:, 0:128]
        causal_b = cb[:, 128:256]
        smask_b = cb[0:64, 256:320]
        ones_b = cb[:, 320:448]
        ind_f = cf[0:64, 320:336]

        def V(nm, l, j=0, n=1, rows=128):
            o = VOFF[(nm, l)] + j
            return vec[0:rows, o:o + n]

        K.dma(sp, vec[:], vec_d, W=[vecr])
        K.dma(sp, cf[:], cf_d, W=[cfr])
        for ti, (c0, n) in enumerate(TILES):
            for kc in range(NKC):
                K.dma(sp, x[:, kc, c0:c0 + n], xT_d[:, kc, c0:c0 + n], W=[xr_[kc][ti]])
        K.op(dve, lambda e: e.tensor_copy(out=cb[:, 0:320], in_=cf[:, 0:320]), R=[cfr], W=[cbr])
        K.op(dve, lambda e: e.memset(cb[:, 320:448], 1.0), W=[cbr])

        def rmsnorm(gname, l, out_t, out_r, sq, sqr, rs, rsr, tiles=None):
            for ti, (c0, n) in enumerate(TILES):
                if tiles is not None and ti not in tiles:
                    continue
                K.op(act, lambda e: e.activation(out=sq[:, :, 0:n], in_=x[:, :, c0:c0 + n], func=AF.Square),
                     R=[xr_[kc][ti] for kc in range(NKC)], W=[sqr])
                pt, pr = K.psn()
                K.mm(pt[:, 0:n], [(ones_b, sq[:, kc, 0:n]) for kc in range(NKC)], R=[sqr, cbr], W=[pr])
                b = ti % 2
                K.op(act, lambda e: e.activation(out=rs[b][:, 0:n], in_=pt[:, 0:n], func=AF.Ln,
                                                 scale=1.0 / D, bias=epsc[:, 0:1]), R=[pr, cbr], W=[rsr[b]])
                K.op(act, lambda e: e.activation(out=rs[b][:, 0:n], in_=rs[b][:, 0:n], func=AF.Exp, scale=-0.5),
                     R=[rsr[b]], W=[rsr[b]])
                for kc in range(NKC):
                    K.op(dve, lambda e: e.scalar_tensor_tensor(
                        out=out_t[:, kc, c0:c0 + n], in0=x[:, kc, c0:c0 + n], scalar=V(gname, l, kc),
                        in1=rs[b][:, 0:n], op0=ALU.mult, op1=ALU.mult),
                        R=[xr_[kc][ti], rsr[b], vecr], W=[out_r[kc][ti]])

        epsc = SB("epsc", [128, 1])
        K.op(dve, lambda e: e.memset(epsc[:], EPS), W=[cbr])

        def ffn(l, f, pre_norm=True, post=None):
            gname = "g_ff1" if f == 0 else "g_ff2"
            with ExitStack() as fs:
                S = lambda nm, shp, dt=F32: fs.enter_context(nc.sbuf_tensor(UN(nm), list(shp), dt))
                sq = S("f_sq", [128, NKC, 512], BF16)
                sqr = Res()
                rs = [S("f_rs0", [128, 512]), S("f_rs1", [128, 512])]
                rsr = [Res(), Res()]
                hg = S("f_h", [128, 6, T], BF16)
                hgr = RL(6, 5)
                sg = [S("f_sg0", [128, 512]), S("f_sg1", [128, 512])]
                sgr = [Res(), Res()]
                R4 = [S(f"f_r4_{i}", [128, 2, 1024], BF16) for i in range(3)]
                R4r = [Res() for _ in range(3)]
                r4p = [0]

                def r4():
                    i = r4p[0]
                    r4p[0] = (i + 1) % 3
                    return R4[i], R4r[i]
                if pre_norm:
                    rmsnorm(gname, l, xn, xnr, sq, sqr, rs, rsr)
                cnt = 0
                for g, pairs in enumerate(GROUPS):
                    wds = []
                    for pi, jp in enumerate(pairs):
                        wt, wr = r8()
                        dt_, dr = r4()
                        K.dma(pool, dt_[:], wdn_d[l, f, jp], W=[dr])
                        wds.append((dt_, dr))
                        for jj in range(2):
                            jl = 2 * pi + jj
                            for ti, (c0, n) in enumerate(TILES):
                                pg, pgr = K.psn()
                                pu, pur = K.psn()
                                rr = [xnr[kc][ti] for kc in range(NKC)] + [wr]
                                K.mm(pg[:, 0:n], [(wt[:, kc, jj * 128:(jj + 1) * 128], xn[:, kc, c0:c0 + n])
                                                  for kc in range(NKC)], R=rr, W=[pgr])
                                K.mm(pu[:, 0:n], [(wt[:, kc, 256 + jj * 128:256 + (jj + 1) * 128], xn[:, kc, c0:c0 + n])
                                                  for kc in range(NKC)], R=rr, W=[pur])
                                b = cnt % 2
                                cnt += 1
                                K.op(act, lambda e: e.activation(out=sg[b][:, 0:n], in_=pg[:, 0:n], func=AF.Silu),
                                     R=[pgr], W=[sgr[b]])
                                K.op(dve, lambda e: e.tensor_tensor(out=hg[:, jl, c0:c0 + n], in0=sg[b][:, 0:n],
                                                                    in1=pu[:, 0:n], op=ALU.mult),
                                     R=[sgr[b], pur], W=[hgr[jl][ti]])
                        r8rel()
                    nj = 2 * len(pairs)
                    for ti, (c0, n) in enumerate(TILES):
                        for m in range(NKC):
                            pt, pr = K.psn()
                            K.mm(pt[:, 0:n], [(wds[jl // 2][0][:, jl % 2, m * 128:(m + 1) * 128], hg[:, jl, c0:c0 + n])
                                              for jl in range(nj)],
                                 R=[hgr[jl][ti] for jl in range(nj)] + [w[1] for w in wds], W=[pr])
                            K.op(dve, lambda e: e.scalar_tensor_tensor(
                                out=x[:, m, c0:c0 + n], in0=pt[:, 0:n], scalar=0.5, in1=x[:, m, c0:c0 + n],
                                op0=ALU.mult, op1=ALU.add), R=[pr, xr_[m][ti]], W=[xr_[m][ti]])
                        if post is not None and g == len(GROUPS) - 1 and ti >= 1:
                            post(sq, sqr, rs, rsr, ti - 1)
                if post is not None:
                    post(sq, sqr, rs, rsr, len(TILES) - 1)
            K.barrier()

        def mixer(l, pre_norm=True, post_norm=False):
            with ExitStack() as ms:
                S = lambda nm, shp, dt=F32: ms.enter_context(nc.sbuf_tensor(UN(nm), list(shp), dt))
                mixin = S("m_mixin", [128, NKC, T], BF16)
                mixr = RL(NKC, 5)
                colr = Res()
                cols = S("m_cols", [128, NCH, 12])
                dcp = S("m_dcp", [128, H, 17])
                dcs = S("m_dcs", [128, H, NSQ])
                dcsT = S("m_dcsT", [NSQ, H])
                smallr = Res()
                with ExitStack() as rs_:
                    S2 = lambda nm, shp, dt=F32: rs_.enter_context(nc.sbuf_tensor(UN(nm), list(shp), dt))
                    with ExitStack() as ns_:
                        S3 = lambda nm, shp, dt=F32: ns_.enter_context(nc.sbuf_tensor(UN(nm), list(shp), dt))
                        sq = S3("r_sq", [128, NKC, 512], BF16)
                        sqr = Res()
                        rs = [S3("r_rs0", [128, 512]), S3("r_rs1", [128, 512])]
                        rsr = [Res(), Res()]
                        if pre_norm:
                            rmsnorm("g_mix", l, xn, xnr, sq, sqr, rs, rsr)
                    if pre_norm:
                        K.barrier()
                    wg = S2("r_wg", [128, NKC, 8], BF16)
                    wgr = Res()
                    K.dma(pool, wg[:], wgt_d[l], W=[wgr])
                    R1 = S2("r_R1", [H, T])
                    R2 = S2("r_R2", [H, T])
                    R3 = S2("r_R3", [H, T])
                    r1, r2, r3 = Res(), Res(), Res()
                    nbf = S2("r_nbf", [H, 1])
                    sm = S2("r_sm", [H, 24])
                    mref = S2("r_mref", [H, 18])
                    m0 = S2("r_m0", [H, NSQ])
                    mnx = S2("r_mnx", [H, NSQ])
                    dcr = S2("r_dcr", [H, 17 + NSQ])
                    dce = S2("r_dce", [H, H, 17 + NSQ])
                    K.dma(sp, m0[:], sm_d[l], W=[smallr])
                    K.op(act, lambda e: e.mul(out=nbf[:], in_=V("b_f", l, rows=H), mul=-1.0), R=[vecr], W=[smallr])
                    for ti, (c0, n) in enumerate(TILES):
                        pi_, pir = K.psn()
                        pf_, pfr = K.psn()
                        rr = [xnr[kc][ti] for kc in range(NKC)] + [wgr]
                        K.mm(pi_[0:H, 0:n], [(wg[:, kc, 0:4], xn[:, kc, c0:c0 + n]) for kc in range(NKC)], R=rr, W=[pir])
                        K.mm(pf_[0:H, 0:n], [(wg[:, kc, 4:8], xn[:, kc, c0:c0 + n]) for kc in range(NKC)], R=rr, W=[pfr])
                        K.op(act, lambda e: e.activation(out=R1[:, c0:c0 + n], in_=pi_[0:H, 0:n], func=AF.Identity,
                                                         bias=V("b_i", l, rows=H), scale=1.0), R=[pir, vecr], W=[r1])
                        K.op(act, lambda e: e.activation(out=R2[:, c0:c0 + n], in_=pf_[0:H, 0:n], func=AF.Exp,
                                                         bias=nbf[:], scale=-1.0), R=[pfr, smallr], W=[r2])
                    K.op(act, lambda e: e.activation(out=R2[:], in_=R2[:], func=AF.Ln, bias=1.0, scale=1.0), R=[r2], W=[r2])
                    K.op(dve, lambda e: e.tensor_tensor_scan(out=R3[:, 0:TP], data0=R2[:, 0:TP], data1=R2[:, 0:TP],
                                                             initial=0.0, op0=ALU.add, op1=ALU.max), R=[r2], W=[r3])
                    K.op(dve, lambda e: e.tensor_copy(out=R3[:, TP:TP + 16], in_=R2[:, TP:TP + 16]), R=[r2], W=[r3])
                    for t in range(1, NST):
                        K.op(dve, lambda e: e.tensor_tensor(out=R3[:, TP + 16 * t:TP + 16 * t + 16],
                                                            in0=R3[:, TP + 16 * (t - 1):TP + 16 * t],
                                                            in1=R2[:, TP + 16 * t:TP + 16 * t + 16], op=ALU.add),
                             R=[r2, r3], W=[r3])
                    K.op(dve, lambda e: e.tensor_tensor(out=R1[:], in0=R1[:], in1=R3[:], op=ALU.add), R=[r1, r3], W=[r1])
                    K.op(dve, lambda e: e.tensor_tensor_scan(out=R2[:, 0:TP], data0=R1[:, 0:TP], data1=R1[:, 0:TP],
                                                             initial=0.0, op0=ALU.max, op1=ALU.max), R=[r1, r2], W=[r2])
                    K.op(dve, lambda e: e.tensor_tensor(out=R2[:, TP:TP + 16], in0=R1[:, TP:TP + 16], in1=m0[:],
                                                        op=ALU.max), R=[r1, smallr, r2], W=[r2])
                    for t in range(1, NST):
                        K.op(dve, lambda e: e.tensor_tensor(out=R2[:, TP + 16 * t:TP + 16 * t + 16],
                                                            in0=R2[:, TP + 16 * (t - 1):TP + 16 * t],
                                                            in1=R1[:, TP + 16 * t:TP + 16 * t + 16], op=ALU.max),
                             R=[r1, r2], W=[r2])
                    K.op(dve, lambda e: e.memset(mref[:, 0:1], 0.0), W=[smallr])
                    K.op(dve, lambda e: e.tensor_copy(
                        out=mref[:, 1:17], in_=R2[:, 0:2048].rearrange("p (c t) -> p c t", t=128)[:, :, 127]),
                        R=[r2, smallr], W=[smallr])
                    K.op(dve, lambda e: e.tensor_copy(out=mref[:, 17:18], in_=R2[:, TP - 1:TP]), R=[r2, smallr], W=[smallr])
                    K.op(dve, lambda e: e.tensor_copy(out=mnx[:], in_=R2[:, TP + 48:TP + 64]), R=[r2, smallr], W=[smallr])
                    K.op(dve, lambda e: e.tensor_tensor(out=sm[:, 0:1], in0=R2[:, TP - 1:TP], in1=R3[:, TP - 1:TP],
                                                        op=ALU.subtract), R=[r2, r3, smallr], W=[smallr])
                    K.op(dve, lambda e: e.tensor_tensor(out=sm[:, 1:17], in0=R2[:, TP + 48:TP + 64],
                                                        in1=R3[:, TP + 48:TP + 64], op=ALU.subtract),
                         R=[r2, r3, smallr], W=[smallr])
                    K.dma(sp, pm_d[l], sm[:, 0:1], R=[smallr])
                    K.dma(sp, om_d[l], sm[:, 1:17], R=[smallr])
                    K.op(dve, lambda e: e.tensor_tensor(out=dcr[:, 0:17], in0=mref[:, 0:17], in1=mref[:, 1:18],
                                                        op=ALU.subtract), R=[smallr], W=[smallr])
                    K.op(dve, lambda e: e.tensor_tensor(out=dcr[:, 17:33], in0=m0[:], in1=mnx[:], op=ALU.subtract),
                         R=[smallr], W=[smallr])
                    K.op(act, lambda e: e.activation(out=dcr[:], in_=dcr[:], func=AF.Exp), R=[smallr], W=[smallr])
                    pv = lambda Rt: Rt[:, 0:2048].rearrange("p (c t) -> p c t", t=128)
                    sv = lambda Rt: Rt[:, TP:T].rearrange("p (t b) -> p t b", b=NSQ)
                    bc = lambda ap_, n: ap_.unsqueeze(2).to_broadcast([H, ap_.shape[1], n])
                    bs = lambda ap_: ap_.unsqueeze(1).to_broadcast([H, NST, NSQ])
                    K.op(dve, lambda e: e.tensor_tensor(out=pv(R2), in0=pv(R1), in1=bc(mref[:, 0:16], 128), op=ALU.subtract),
                         R=[r1, smallr, r2], W=[r2])
                    K.op(dve, lambda e: e.tensor_scalar(out=R2[:, 2048:TP], in0=R1[:, 2048:TP], scalar1=mref[:, 16:17],
                                                        scalar2=None, op0=ALU.subtract), R=[r1, smallr, r2], W=[r2])
                    K.op(dve, lambda e: e.tensor_tensor(out=sv(R2), in0=sv(R1), in1=bs(m0[:]), op=ALU.subtract),
                         R=[r1, smallr, r2], W=[r2])
                    K.op(act, lambda e: e.activation(out=R2[:], in_=R2[:], func=AF.Exp), R=[r2], W=[r2])
                    K.op(dve, lambda e: e.tensor_tensor(out=pv(R3), in0=pv(R3), in1=bc(mref[:, 0:16], 128), op=ALU.subtract),
                         R=[r3, smallr], W=[r3])
                    K.op(dve, lambda e: e.tensor_scalar(out=R3[:, 2048:TP], in0=R3[:, 2048:TP], scalar1=mref[:, 16:17],
                                                        scalar2=None, op0=ALU.subtract), R=[r3, smallr], W=[r3])
                    K.op(dve, lambda e: e.tensor_tensor(out=sv(R3), in0=sv(R3), in1=bs(m0[:]), op=ALU.subtract),
                         R=[r3, smallr], W=[r3])
                    K.op(act, lambda e: e.activation(out=R3[:], in_=R3[:], func=AF.Exp, scale=2.0), R=[r3], W=[r3])
                    K.op(dve, lambda e: e.tensor_tensor(out=pv(R1), in0=pv(R1), in1=bc(mref[:, 1:17], 128), op=ALU.subtract),
                         R=[r1, smallr], W=[r1])
                    K.op(dve, lambda e: e.tensor_scalar(out=R1[:, 2048:TP], in0=R1[:, 2048:TP], scalar1=mref[:, 17:18],
                                                        scalar2=None, op0=ALU.subtract), R=[r1, smallr], W=[r1])
                    K.op(dve, lambda e: e.tensor_tensor(out=sv(R1), in0=sv(R1), in1=bs(mnx[:]), op=ALU.subtract),
                         R=[r1, smallr], W=[r1])
                    K.op(act, lambda e: e.activation(out=R1[:], in_=R1[:], func=AF.Exp), R=[r1], W=[r1])
                    for ci, (c0, n) in enumerate(CHUNKS):
                        pt, pr = K.psn()
                        K.tr_multi([(pt[0:n, 4 * qi:4 * qi + 4], Rt[:, c0:c0 + n], ident_f[0:H, 0:H])
                                    for qi, Rt in enumerate((R2, R1, R3))], R=[r1, r2, r3, cfr], W=[pr])
                        K.op(act, lambda e: e.activation(out=cols[0:n, ci, :], in_=pt[0:n, 0:12], func=AF.Copy),
                             R=[pr], W=[colr])
                    for hh in range(H):
                        K.op(dve, lambda e: e.tensor_scalar(out=dce[:, hh, :], in0=dcr[:], scalar1=ident_f[0:H, hh:hh + 1],
                                                            scalar2=None, op0=ALU.mult), R=[smallr, cfr], W=[smallr])
                    pt, pr = K.psn()
                    K.mm(pt[:, 0:H * 33], [(ones_f4[:], dce[:].rearrange("p h j -> p (h j)"))], R=[smallr, cbr], W=[pr])
                    ptv = pt[:, 0:H * 33].rearrange("p (h j) -> p h j", j=33)
                    K.op(act, lambda e: e.activation(out=dcp[:], in_=ptv[:, :, 0:17], func=AF.Copy), R=[pr], W=[colr])
                    K.op(act, lambda e: e.activation(out=dcs[:], in_=ptv[:, :, 17:33], func=AF.Copy), R=[pr], W=[colr])
                    pt2, pr2 = K.psn()
                    K.tr(pt2[0:NSQ, 0:H], dcr[:, 17:33], ident_f[0:H, 0:H], R=[smallr, cfr], W=[pr2])
                    K.op(act, lambda e: e.activation(out=dcsT[:], in_=pt2[0:NSQ, 0:H], func=AF.Copy), R=[pr2], W=[colr])
                K.barrier()
                if stage < 3:
                    return
                with ExitStack() as hs:
                    S2 = lambda nm, shp, dt=F32: hs.enter_context(nc.sbuf_tensor(UN(nm), list(shp), dt))
                    NB = 5
                    gml = S2("h_gml", [128, 512])
                    gmlr = Res()
                    K.dma(sp, gml[:], gml_d[:, l, :], W=[gmlr])
                    qk = [S2(f"h_qk{i}", [128, 2, 512], BF16) for i in range(3)]
                    qkr = [Res(), Res(), Res()]
                    ktok = [S2(f"h_kt{i}", [128, 128], BF16) for i in range(NB)]
                    v1 = [S2(f"h_v1{i}", [128, 129], BF16) for i in range(NB)]
                    vwx = [S2(f"h_vw{i}", [128, 129], BF16) for i in range(NB)]
                    sgo = [S2(f"h_so{i}", [128, 128]) for i in range(NB)]
                    eo = [S2(f"h_eo{i}", [128, 128]) for i in range(2)]
                    eor = [Res(), Res()]
                    stw = [S2(f"h_sw{i}", [128, 128], BF16) for i in range(NB)]
                    tokr = [RL(5) for _ in range(NB)]
                    hgt = [S2(f"h_hg{i}", [128, 128]) for i in range(2)]
                    hmt = [S2(f"h_hm{i}", [128, 128], BF16) for i in range(2)]
                    junk = [S2(f"h_jk{i}", [128, 128], BF16) for i in range(2)]
                    pcol = [S2(f"h_pc{i}", [128, 8]) for i in range(3)]
                    pcr = [Res() for _ in range(3)]
                    postr = [RL(4), RL(4)]
                    CTn = S2("h_CTn", [128, 129])
                    CTb = S2("h_CTb", [128, 129], BF16)
                    ctr, ctbr = Res(), Res()
                    C0 = S2("h_C0", [128, NSQ, 128])
                    c0r = RL(NSQ)
                    CT0 = S2("h_CT0", [128, NSQ, 130], BF16)
                    ct0r = Res()
                    n0T = S2("h_n0T", [128, NSQ])
                    n0 = S2("h_n0", [NSQ, 128])
                    n0r = Res()
                    qd = S2("h_qd", [128, 16 * 65], BF16)
                    qdr = Res()
                    vwb = [S2(f"h_vwb{i}", [64, 128], BF16) for i in range(NSQ)]
                    vwbr = [Res() for _ in range(NSQ)]
                    ewb = S2("h_ewb", [64, NSQ], BF16)
                    ewbr = Res()
                    for i in range(NB):
                        K.op(dve, lambda e: e.memset(v1[i][:, 128:129], 1.0), W=[tokr[i][1]])
                    K.op(dve, lambda e: e.memset(qd[:], 0.0), W=[qdr])
                    qd_rows = qd[:, 0:1024].rearrange("p (b t) -> p b t", t=64)
                    qd_diag = qd[:, 0:1040].rearrange("p (b u) -> p b u", u=65)[:, :, 0:64:16]
                    rot = {"a": 0, "n": 0}

                    def bank_a():
                        i = rot["a"]
                        rot["a"] = (i + 1) % 2
                        return K.ps[i], K.psr[i]

                    def bank_n():
                        i = 3 + rot["n"]
                        rot["n"] = (rot["n"] + 1) % 3
                        return K.ps[i], K.psr[i]

                    def load_states(h):
                        K.dma(pool, CT0[:, :, 0:128], sCT_d[l, h], W=[ct0r])
                        K.dma(sp, C0[:], sC_d[l, h], W=c0r)
                        K.dma(sp, n0T[:], snT_d[l, h], W=[n0r])
                        K.dma(sp, n0[:], sn_d[l, h], W=[n0r])
                        K.op(dve, lambda e: e.tensor_copy(out=CT0[:, :, 128], in_=n0T[:]), R=[n0r, ct0r], W=[ct0r])

                    its = [(h, ci) for h in range(H) for ci in range(NCH)]
                    ctx = {}
                    wts = {}

                    def stageA(i):
                        h, ci = its[i]
                        c0, n = CHUNKS[ci]
                        issample = (ci == NCH - 1)
                        if ci == 0:
                            wts[h] = r8()
                        wt, wr = wts[h]
                        ti = min(c0 // 512, 4)
                        tc0, tn = TILES[ti]
                        qb = ti % 3

                        def qk_group(tj, part):
                            jc0, jn = TILES[tj]
                            if jn > 256:
                                lo_, hi_ = (0, 256) if part % 2 == 0 else (256, jn)
                            else:
                                if part % 2 == 1:
                                    return
                                lo_, hi_ = 0, jn
                            isk = part >= 2
                            wc = 128 if isk else 0
                            jb = tj % 3
                            pq, pqr = bank_a()
                            K.mm(pq[:, 0:hi_ - lo_], [(wt[:, kc, wc:wc + 128], xn[:, kc, jc0 + lo_:jc0 + hi_]) for kc in range(NKC)],
                                 R=[xnr[kc][tj] for kc in range(NKC)] + [wr], W=[pqr])
                            if isk:
                                K.op(dve, lambda e: e.tensor_scalar(out=qk[jb][:, 1, lo_:hi_], in0=pq[:, 0:hi_ - lo_], scalar1=KSCALE,
                                                                    scalar2=None, op0=ALU.mult), R=[pqr], W=[qkr[jb]])
                            else:
                                K.op(act, lambda e: e.activation(out=qk[jb][:, 0, lo_:hi_], in_=pq[:, 0:hi_ - lo_], func=AF.Copy),
                                     R=[pqr], W=[qkr[jb]])

                        if ci == 0:
                            for part in range(4):
                                qk_group(0, part)
                        if ci < 16:
                            qk_group(ti + 1, ci % 4)
                        lo = c0 - tc0
                        qT = qk[qb][:, 0, lo:lo + n]
                        kT = qk[qb][:, 1, lo:lo + n]
                        b = i % NB
                        tr_ = tokr[b]
                        ea = cols[0:n, ci, h:h + 1]
                        ew = cols[0:n, ci, 4 + h:5 + h]
                        fl = cols[0:n, ci, 8 + h:9 + h]
                        ctx[i] = dict(h=h, ci=ci, c0=c0, n=n, issample=issample, ti=ti, qb=qb, qT=qT, kT=kT, b=b, tr_=tr_,
                                      ea=ea, ew=ew, fl=fl)
                        pt, pr = bank_a()
                        K.mm(pt[0:n, 0:384], [(xn[:, kc, c0:c0 + n], wt[:, kc, 128:512]) for kc in range(NKC)],
                             R=[xnr[kc][ti] for kc in range(NKC)] + [wr], W=[pr])
                        K.op(act, lambda e: e.mul(out=ktok[b][0:n, :], in_=pt[0:n, 0:128], mul=KSCALE), R=[pr], W=[tr_[0]])
                        K.op(act, lambda e: e.activation(out=v1[b][0:n, 0:128], in_=pt[0:n, 128:256], func=AF.Copy), R=[pr], W=[tr_[1]])
                        eb = i % 2
                        K.op(act, lambda e: e.activation(out=eo[eb][0:n, :], in_=pt[0:n, 256:384], func=AF.Exp, scale=-1.0), R=[pr], W=[eor[eb]])
                        K.op(act, lambda e: e.activation(out=eo[eb][0:n, :], in_=eo[eb][0:n, :], func=AF.Ln, bias=1.0, scale=1.0),
                             R=[eor[eb]], W=[eor[eb]])
                        K.op(act, lambda e: e.activation(out=sgo[b][0:n, :], in_=eo[eb][0:n, :], func=AF.Exp, scale=-1.0),
                             R=[eor[eb]], W=[tr_[3]])
                        K.op(dve, lambda e: e.tensor_scalar(out=vwx[b][0:n, 0:128], in0=v1[b][0:n, 0:128], scalar1=ew, scalar2=None,
                                                            op0=ALU.mult), R=[tr_[1], colr], W=[tr_[2]])
                        K.op(act, lambda e: e.activation(out=vwx[b][0:n, 128:129], in_=ew, func=AF.Copy), R=[colr], W=[tr_[2]])
                        ps_, psr_ = K.ps[2], K.psr[2]
                        K.mm(ps_[0:n, 0:n], [(kT, qT)], R=[qkr[qb]], W=[psr_])
                        msk = smask_b if issample else causal_b[0:n, 0:n]
                        if n < 128:
                            K.op(dve, lambda e: e.memset(stw[b][:, 0:n], 0.0), W=[tr_[4]])
                        K.op(dve, lambda e: e.scalar_tensor_tensor(out=stw[b][0:n, 0:n], in0=ps_[0:n, 0:n], scalar=ea,
                                                                   in1=msk, op0=ALU.mult, op1=ALU.mult),
                             R=[psr_, colr, cbr], W=[tr_[4]])
                        if issample:
                            K.op(dve, lambda e: e.tensor_copy(out=qd_diag, in_=qT.rearrange("p (t b) -> p b t", b=NSQ)),
                                 R=[qkr[qb], qdr], W=[qdr])
                        if ci == NCH - 1:
                            r8rel()

                    def prompt_state(c):
                        h, ci, n, b, tr_ = c["h"], c["ci"], c["n"], c["b"], c["tr_"]
                        pu_, pur = K.ps[6], K.psr[6]
                        K.mm(pu_[:, 0:129], [(ktok[b][0:n, :], vwx[b][0:n, :])], R=[tr_[0], tr_[2]], W=[pur])
                        if ci == 0:
                            K.op(dve, lambda e: e.tensor_copy(out=CTn[:], in_=pu_[:, 0:129]), R=[pur, ctr], W=[ctr])
                        else:
                            K.op(dve, lambda e: e.scalar_tensor_tensor(out=CTn[:], in0=CTn[:], scalar=dcp[:, h, ci:ci + 1],
                                                                       in1=pu_[:, 0:129], op0=ALU.mult, op1=ALU.add),
                                 R=[pur, ctr, colr], W=[ctr])
                        if ci == NCH - 2:
                            K.dma(sp, pCT_d[l, h], CTn[:], R=[ctr])
                        else:
                            K.op(act, lambda e: e.activation(out=CTb[:], in_=CTn[:], func=AF.Copy), R=[ctr, ctbr], W=[ctbr])

                    def sample_state(c):
                        h, n, b, tr_, ew = c["h"], c["n"], c["b"], c["tr_"], c["ew"]
                        K.op(dve, lambda e: e.tensor_scalar(out=ewb[:], in0=ind_f, scalar1=ew, scalar2=None,
                                                            op0=ALU.mult), R=[cfr, colr], W=[ewbr])
                        pu_, pur = K.ps[6], K.psr[6]
                        K.mm(pu_[0:NSQ, 0:128], [(ewb[:], ktok[b][0:n, :])], R=[ewbr, tr_[0]], W=[pur])
                        K.op(dve, lambda e: e.scalar_tensor_tensor(out=n0[:], in0=n0[:], scalar=dcsT[:, h:h + 1],
                                                                   in1=pu_[0:NSQ, 0:128], op0=ALU.mult, op1=ALU.add),
                             R=[n0r, colr, pur], W=[n0r])
                        K.dma(sp, on_d[l, h], n0[:], R=[n0r])
                        for bb in range(NSQ):
                            K.op(dve, lambda e: e.tensor_scalar(out=vwb[bb][:], in0=vwx[b][0:n, 0:128],
                                                                scalar1=ind_f[:, bb:bb + 1], scalar2=None, op0=ALU.mult),
                                 R=[tr_[2], cfr], W=[vwbr[bb]])
                        bk = lambda bb: (K.ps[6], K.psr[6]) if bb % 2 == 0 else (K.ps[7], K.psr[7])

                        def mmb(bb):
                            pc_, pcr_ = bk(bb)
                            K.mm(pc_[:, 0:128], [(vwb[bb][:], ktok[b][0:n, :])], R=[vwbr[bb], tr_[0]], W=[pcr_])

                        mmb(0)
                        mmb(1)
                        for bb in range(NSQ):
                            pc_, pcr_ = bk(bb)
                            K.op(dve, lambda e: e.scalar_tensor_tensor(out=C0[:, bb, :], in0=C0[:, bb, :],
                                                                       scalar=dcs[:, h, bb:bb + 1], in1=pc_[:, 0:128],
                                                                       op0=ALU.mult, op1=ALU.add),
                                 R=[c0r[bb], colr, pcr_], W=[c0r[bb]])
                            if bb + 2 < NSQ:
                                mmb(bb + 2)
                        K.dma(sp, oC_d[l, h], C0[:], R=c0r)

                    def stageB(i):
                        c = ctx[i]
                        h, ci, n, b, tr_, qT, qb = c["h"], c["ci"], c["n"], c["b"], c["tr_"], c["qT"], c["qb"]
                        pn_, pnr = bank_n()
                        c["pn"] = (pn_, pnr)
                        if c["issample"]:
                            grp = [(pn_[0:n, 0:129], stw[b][:, 0:n], v1[b][:, :], True, False)]
                            for bb in range(NSQ):
                                grp.append((pn_[0:n, 0:129], qd_rows[:, bb, :], CT0[:, bb, 0:129], False, bb == NSQ - 1))
                            K.mm_multi(grp, R=[tr_[4], tr_[1], qdr, ct0r], W=[pnr])
                        elif ci == 0:
                            K.mm(pn_[0:n, 0:129], [(stw[b][:, 0:n], v1[b][:, :])], R=[tr_[4], tr_[1]], W=[pnr])
                        else:
                            K.mm(pn_[0:n, 0:129], [(stw[b][:, 0:n], v1[b][:, :]), (qT, CTb[:])],
                                 R=[tr_[4], tr_[1], qkr[qb], ctbr], W=[pnr])
                        pb3 = i % 3
                        c["pb3"] = pb3
                        jb = i % 2
                        K.op(act, lambda e: e.activation(out=junk[jb][0:n, :], in_=pn_[0:n, 0:128], func=AF.Square,
                                                         accum_out=pcol[pb3][0:n, 2:3]),
                             R=[pnr, postr[jb][2]], W=[pcr[pb3], postr[jb][2]])
                        K.op(act, lambda e: e.activation(out=pcol[pb3][0:n, 0:1], in_=pn_[0:n, 128:129], func=AF.Square),
                             R=[pnr, pcr[pb3]], W=[pcr[pb3]])
                        if not c["issample"]:
                            prompt_state(c)

                    def stageC1(i):
                        c = ctx[i]
                        n, fl = c["n"], c["fl"]
                        pn_, pnr = c["pn"]
                        pc = pcol[c["pb3"]]
                        r_ = pcr[c["pb3"]]
                        h_, b_, trk = c["h"], c["b"], c["tr_"]
                        pbx = i % 2
                        K.op(dve, lambda e: e.tensor_tensor(out=hgt[pbx][0:n, :], in0=sgo[b_][0:n, :],
                                                            in1=gml[0:n, h_ * 128:(h_ + 1) * 128], op=ALU.mult),
                             R=[trk[3], gmlr, postr[pbx][0]], W=[postr[pbx][0]])
                        K.op(dve, lambda e: e.tensor_tensor(out=pc[0:n, 0:1], in0=pc[0:n, 0:1], in1=fl, op=ALU.max),
                             R=[colr, r_], W=[r_])
                        K.op(dve, lambda e: e.scalar_tensor_tensor(out=pc[0:n, 3:4], in0=pc[0:n, 0:1], scalar=DH * EPS, in1=pc[0:n, 2:3],
                                                                   op0=ALU.mult, op1=ALU.add), R=[r_], W=[r_])
                        K.op(act, lambda e: e.activation(out=pc[0:n, 4:5], in_=pc[0:n, 3:4], func=AF.Ln, scale=1.0 / DH), R=[r_], W=[r_])
                        K.op(act, lambda e: e.activation(out=pc[0:n, 5:6], in_=pc[0:n, 4:5], func=AF.Exp, scale=-0.5), R=[r_], W=[r_])

                    def stageC2a(i):
                        c = ctx[i]
                        h, ci, c0, n, b, tr_, ti = c["h"], c["ci"], c["c0"], c["n"], c["b"], c["tr_"], c["ti"]
                        pn_, pnr = c["pn"]
                        pc = pcol[c["pb3"]]
                        r_ = pcr[c["pb3"]]
                        pb = i % 2
                        po_ = postr[pb]
                        K.op(dve, lambda e: e.scalar_tensor_tensor(out=hmt[pb][0:n, :], in0=pn_[0:n, 0:128], scalar=pc[0:n, 5:6],
                                                                   in1=hgt[pb][0:n, :], op0=ALU.mult, op1=ALU.mult),
                             R=[pnr, r_, po_[0], po_[1]], W=[po_[1]])

                    def stageC2b(i):
                        c = ctx.pop(i)
                        h, c0, n, ti = c["h"], c["c0"], c["n"], c["ti"]
                        pb = i % 2
                        po_ = postr[pb]
                        ph_, phr = K.ps[7], K.psr[7]
                        phb = ph_[:].bitcast(BF16)
                        K.tr(phb[:, 0:n], hmt[pb][0:n, :], ident_b[0:n, 0:n], R=[po_[1], cbr], W=[phr])
                        K.op(act, lambda e: e.activation(out=mixin[:, h, c0:c0 + n], in_=phb[:, 0:n], func=AF.Copy),
                             R=[phr], W=[mixr[h][ti]])
                        if c["issample"]:
                            sample_state(c)
                            if h + 1 < H:
                                load_states(h + 1)

                    load_states(0)
                    NI = len(its)
                    for i in range(NI + 5):
                        if 0 <= i - 5 < NI:
                            stageC2b(i - 5)
                        if 0 <= i - 2 < NI:
                            stageB(i - 2)
                        if 0 <= i - 3 < NI:
                            stageC1(i - 3)
                        if 0 <= i - 4 < NI:
                            stageC2a(i - 4)
                        if i < NI:
                            stageA(i)
                K.barrier()
                if stage < 4:
                    return
                with ExitStack() as ls:
                    S2 = lambda nm, shp, dt=F32: ls.enter_context(nc.sbuf_tensor(UN(nm), list(shp), dt))
                    wrg = S2("l_wrg", [128, 2, 4, 128], BF16)
                    wrgr = Res()
                    K.dma(pool, wrg[:], wrg_d[l], W=[wrgr])
                    wl = [r8(), r8()]
                    sc = S2("l_sc", [128, 4])
                    scr = Res()
                    K.op(act, lambda e: e.activation(out=sc[:], in_=V("lru_lambda", l, 0, 4), func=AF.Exp, scale=-1.0),
                         R=[vecr], W=[scr])
                    K.op(act, lambda e: e.activation(out=sc[:], in_=sc[:], func=AF.Ln, bias=1.0, scale=1.0), R=[scr], W=[scr])
                    K.op(act, lambda e: e.mul(out=sc[:], in_=sc[:], mul=-8.0), R=[scr], W=[scr])
                    sc2 = S2("l_sc2", [128, 4])
                    K.op(act, lambda e: e.mul(out=sc2[:], in_=sc[:], mul=2.0), R=[scr], W=[scr])
                    nbg = S2("l_nbg", [128, 8])
                    K.op(act, lambda e: e.mul(out=nbg[:, 0:4], in_=V("b_rg_a", l, 0, 4), mul=-1.0), R=[vecr, scr], W=[scr])
                    K.op(act, lambda e: e.mul(out=nbg[:, 4:8], in_=V("b_rg_x", l, 0, 4), mul=-1.0), R=[vecr, scr], W=[scr])
                    mk = lambda nm, n_, shp, dt=F32: ([S2(f"{nm}{i}", shp, dt) for i in range(n_)], [Res() for _ in range(n_)])
                    xet, xetr = mk("l_xet", 2, [128, 3 + 512])
                    ygf, ygfr = mk("l_ygf", 2, [128, 512])
                    xc, xcr = mk("l_xc", 3, [128, 512])
                    xcb, xcbr = mk("l_xcb", 2, [128, 512], BF16)
                    tmp, tmpr = mk("l_tmp", 2, [128, 512])
                    glb, glbr = mk("l_glb", 3, [128, 512], BF16)
                    gr, grr = mk("l_gr", 2, [128, 512])
                    gi, gir = mk("l_gi", 2, [128, 512])
                    ts_, tsr = mk("l_ts", 2, [128, 512])
                    hcur, hcurr = mk("l_hc", 2, [128, 512])
                    sqh, sqhr = mk("l_sqh", 2, [128, 512], BF16)
                    car = S2("l_car", [128, 4, 3])
                    carr = RL(4)
                    xes = S2("l_xes", [128, 4, 7, NSQ])
                    xesr = RL(4)
                    hcar = S2("l_hcar", [128, 4])
                    hcr = RL(4)
                    h0s = S2("l_h0s", [128, 4, NSQ])
                    h0r = Res()
                    hs_o = S2("l_hso", [128, 4, NSQ])
                    php = S2("l_php", [128, 4])
                    hsor = Res()
                    rsl = S2("l_rsl", [128, 512])
                    rslr = Res()
                    K.dma(sp, xes[:, :, 0:3, :], scv_d[l], W=xesr)
                    K.dma(sp, h0s[:], sh_d[l], W=[h0r])
                    K.op(dve, lambda e: e.memset(car[:], 0.0), W=carr)
                    items = [(ti, cc) for ti in range(5) for cc in range(4)]
                    NIT = len(items)
                    rotx = {"x": 0, "y": 0}

                    def T1(k):
                        ti, cc = items[k]
                        c0, n = TILES[ti]
                        npr = min(n, TP - c0)
                        b2 = k % 2
                        wt, wr = wl[cc // 2]
                        o_ = (cc % 2) * 256
                        px, pxr = K.ps[rotx["x"]], K.psr[rotx["x"]]
                        rotx["x"] = (rotx["x"] + 1) % 2
                        py, pyr = K.ps[2 + rotx["y"]], K.psr[2 + rotx["y"]]
                        rotx["y"] = (rotx["y"] + 1) % 2
                        rr = [xnr[kc][ti] for kc in range(NKC)] + [wr]
                        K.mm(px[:, 0:n], [(wt[:, kc, o_:o_ + 128], xn[:, kc, c0:c0 + n]) for kc in range(NKC)], R=rr, W=[pxr])
                        K.mm(py[:, 0:n], [(wt[:, kc, o_ + 128:o_ + 256], xn[:, kc, c0:c0 + n]) for kc in range(NKC)], R=rr, W=[pyr])
                        K.op(act, lambda e: e.activation(out=xet[b2][:, 3:3 + npr], in_=px[:, 0:npr], func=AF.Copy),
                             R=[pxr], W=[xetr[b2]])
                        K.op(act, lambda e: e.activation(out=xet[b2][:, 0:3], in_=car[:, cc, :], func=AF.Copy),
                             R=[carr[cc]], W=[xetr[b2]])
                        K.op(act, lambda e: e.activation(out=ygf[b2][:, 0:n], in_=py[:, 0:n], func=AF.Copy), R=[pyr], W=[ygfr[b2]])
                        if ti == 4:
                            K.op(act, lambda e: e.activation(
                                out=xes[:, cc, 3:7, :], in_=px[:, npr:n].rearrange("p (t b) -> p t b", b=NSQ), func=AF.Copy),
                                R=[pxr], W=[xesr[cc]])
                            K.dma(sp, pcv_d[l, :, cc, :], xet[b2][:, npr:npr + 3], R=[xetr[b2]])
                            K.dma(sp, ocv_d[l, :, cc], xes[:, cc, 4:7, :], R=[xesr[cc]])
                        else:
                            K.op(act, lambda e: e.activation(out=car[:, cc, :], in_=xet[b2][:, n:n + 3], func=AF.Copy),
                                 R=[xetr[b2]], W=[carr[cc]])

                    def T2(k):
                        ti, cc = items[k]
                        c0, n = TILES[ti]
                        npr = min(n, TP - c0)
                        b2, b3 = k % 2, k % 3
                        cw = lambda j: V("conv_w", l, j * 4 + cc)
                        K.op(dve, lambda e: e.tensor_scalar(out=xc[b3][:, 0:npr], in0=xet[b2][:, 3:3 + npr], scalar1=cw(3),
                                                            scalar2=V("conv_b", l, cc), op0=ALU.mult, op1=ALU.add),
                             R=[xetr[b2], vecr], W=[xcr[b3]])
                        for j in range(1, 4):
                            K.op(dve, lambda e: e.scalar_tensor_tensor(out=xc[b3][:, 0:npr], in0=xet[b2][:, 3 - j:3 - j + npr],
                                                                       scalar=cw(3 - j), in1=xc[b3][:, 0:npr],
                                                                       op0=ALU.mult, op1=ALU.add),
                                 R=[xetr[b2], vecr, xcr[b3]], W=[xcr[b3]])
                        if ti == 4:
                            xcs = xc[b3][:, npr:n].rearrange("p (t b) -> p t b", b=NSQ)
                            K.op(dve, lambda e: e.tensor_scalar(out=xcs, in0=xes[:, cc, 3:7, :], scalar1=cw(3),
                                                                scalar2=V("conv_b", l, cc), op0=ALU.mult, op1=ALU.add),
                                 R=[xesr[cc], vecr, xcr[b3]], W=[xcr[b3]])
                            for j in range(1, 4):
                                K.op(dve, lambda e: e.scalar_tensor_tensor(out=xcs, in0=xes[:, cc, 3 - j:7 - j, :],
                                                                           scalar=cw(3 - j), in1=xcs, op0=ALU.mult, op1=ALU.add),
                                     R=[xesr[cc], vecr, xcr[b3]], W=[xcr[b3]])
                        K.op(dve, lambda e: e.tensor_copy(out=xcb[b2][:, 0:n], in_=xc[b3][:, 0:n]), R=[xcr[b3]], W=[xcbr[b2]])
                        pa, par = K.ps[4], K.psr[4]
                        pi2, pir2 = K.ps[5], K.psr[5]
                        K.mm(pa[:, 0:n], [(wrg[:, 0, cc, :], xcb[b2][:, 0:n])], R=[wrgr, xcbr[b2]], W=[par])
                        K.mm(pi2[:, 0:n], [(wrg[:, 1, cc, :], xcb[b2][:, 0:n])], R=[wrgr, xcbr[b2]], W=[pir2])
                        K.op(dve, lambda e: e.tensor_tensor(out=tmp[b2][:, 0:n], in0=ygf[b2][:, 0:n], in1=ygf[b2][:, 0:n], op=ALU.mult),
                             R=[ygfr[b2]], W=[tmpr[b2]])
                        K.op(dve, lambda e: e.tensor_scalar(out=tmp[b2][:, 0:n], in0=tmp[b2][:, 0:n], scalar1=0.044715, scalar2=1.0,
                                                            op0=ALU.mult, op1=ALU.add), R=[tmpr[b2]], W=[tmpr[b2]])
                        K.op(dve, lambda e: e.tensor_tensor(out=tmp[b2][:, 0:n], in0=tmp[b2][:, 0:n], in1=ygf[b2][:, 0:n], op=ALU.mult),
                             R=[tmpr[b2], ygfr[b2]], W=[tmpr[b2]])
                        K.op(act, lambda e: e.activation(out=tmp[b2][:, 0:n], in_=tmp[b2][:, 0:n], func=AF.Exp,
                                                         scale=-1.5957691216057308), R=[tmpr[b2]], W=[tmpr[b2]])
                        K.op(act, lambda e: e.activation(out=tmp[b2][:, 0:n], in_=tmp[b2][:, 0:n], func=AF.Ln, bias=1.0, scale=1.0),
                             R=[tmpr[b2]], W=[tmpr[b2]])
                        K.op(act, lambda e: e.activation(out=tmp[b2][:, 0:n], in_=tmp[b2][:, 0:n], func=AF.Exp, scale=-1.0),
                             R=[tmpr[b2]], W=[tmpr[b2]])
                        K.op(dve, lambda e: e.tensor_tensor(out=glb[b3][:, 0:n], in0=ygf[b2][:, 0:n], in1=tmp[b2][:, 0:n], op=ALU.mult),
                             R=[tmpr[b2], ygfr[b2]], W=[glbr[b3]])

                    def T3(k):
                        ti, cc = items[k]
                        c0, n = TILES[ti]
                        b2 = k % 2
                        pa, par = K.ps[4], K.psr[4]
                        pi2, pir2 = K.ps[5], K.psr[5]
                        g_, g_r, i_, i_r, t_, t_r = gr[b2], grr[b2], gi[b2], gir[b2], ts_[b2], tsr[b2]
                        K.op(act, lambda e: e.activation(out=g_[:, 0:n], in_=pa[:, 0:n], func=AF.Exp,
                                                         bias=nbg[:, cc:cc + 1], scale=-1.0), R=[par, scr], W=[g_r])
                        K.op(act, lambda e: e.activation(out=i_[:, 0:n], in_=pi2[:, 0:n], func=AF.Exp,
                                                         bias=nbg[:, 4 + cc:5 + cc], scale=-1.0), R=[pir2, scr], W=[i_r])
                        K.op(act, lambda e: e.activation(out=g_[:, 0:n], in_=g_[:, 0:n], func=AF.Ln, bias=1.0, scale=1.0), R=[g_r], W=[g_r])
                        K.op(act, lambda e: e.activation(out=i_[:, 0:n], in_=i_[:, 0:n], func=AF.Ln, bias=1.0, scale=1.0), R=[i_r], W=[i_r])
                        K.op(act, lambda e: e.activation(out=g_[:, 0:n], in_=g_[:, 0:n], func=AF.Exp, scale=-1.0), R=[g_r], W=[g_r])
                        K.op(act, lambda e: e.activation(out=i_[:, 0:n], in_=i_[:, 0:n], func=AF.Exp, scale=-1.0), R=[i_r], W=[i_r])
                        K.op(act, lambda e: e.activation(out=t_[:, 0:n], in_=g_[:, 0:n], func=AF.Exp, scale=sc2[:, cc:cc + 1]),
                             R=[g_r, scr], W=[t_r])
                        K.op(act, lambda e: e.activation(out=g_[:, 0:n], in_=g_[:, 0:n], func=AF.Exp, scale=sc[:, cc:cc + 1]),
                             R=[g_r, scr], W=[g_r])
                        K.op(act, lambda e: e.activation(out=t_[:, 0:n], in_=t_[:, 0:n], func=AF.Ln, scale=-1.0, bias=1.0), R=[t_r], W=[t_r])
                        K.op(act, lambda e: e.activation(out=t_[:, 0:n], in_=t_[:, 0:n], func=AF.Exp, scale=0.5), R=[t_r], W=[t_r])

                    def T4(k):
                        ti, cc = items[k]
                        c0, n = TILES[ti]
                        npr = min(n, TP - c0)
                        b2, b3 = k % 2, k % 3
                        g_, g_r, i_, i_r, t_, t_r = gr[b2], grr[b2], gi[b2], gir[b2], ts_[b2], tsr[b2]
                        hc, hcr_ = hcur[b2], hcurr[b2]
                        K.op(dve, lambda e: e.tensor_tensor(out=i_[:, 0:n], in0=i_[:, 0:n], in1=xc[b3][:, 0:n], op=ALU.mult),
                             R=[i_r, xcr[b3]], W=[i_r])
                        K.op(dve, lambda e: e.tensor_tensor(out=i_[:, 0:n], in0=i_[:, 0:n], in1=t_[:, 0:n], op=ALU.mult),
                             R=[i_r, t_r], W=[i_r])
                        init = 0.0 if ti == 0 else hcar[:, cc:cc + 1]
                        K.op(dve, lambda e: e.tensor_tensor_scan(out=hc[:, 0:npr], data0=g_[:, 0:npr], data1=i_[:, 0:npr],
                                                                 initial=init, op0=ALU.mult, op1=ALU.add),
                             R=[g_r, i_r, hcr[cc]], W=[hcr_])
                        if ti < 4:
                            K.op(act, lambda e: e.activation(out=hcar[:, cc:cc + 1], in_=hc[:, n - 1:n], func=AF.Copy),
                                 R=[hcr_], W=[hcr[cc]])
                        else:
                            K.op(act, lambda e: e.activation(out=php[:, cc:cc + 1], in_=hc[:, npr - 1:npr], func=AF.Copy),
                                 R=[hcr_], W=[hsor])
                            for t in range(NST):
                                s0 = npr + 16 * t
                                prev = h0s[:, cc, :] if t == 0 else hc[:, s0 - 16:s0]
                                K.op(dve, lambda e: e.tensor_tensor(out=hc[:, s0:s0 + 16], in0=g_[:, s0:s0 + 16], in1=prev,
                                                                    op=ALU.mult), R=[g_r, h0r, hcr_], W=[hcr_])
                                K.op(dve, lambda e: e.tensor_tensor(out=hc[:, s0:s0 + 16], in0=hc[:, s0:s0 + 16],
                                                                    in1=i_[:, s0:s0 + 16], op=ALU.add), R=[i_r, hcr_], W=[hcr_])
                            K.op(act, lambda e: e.activation(out=hs_o[:, cc, :], in_=hc[:, npr + 48:npr + 64], func=AF.Copy),
                                 R=[hcr_], W=[hsor])
                        K.op(dve, lambda e: e.tensor_tensor(out=sqh[b2][:, 0:n], in0=hc[:, 0:n], in1=hc[:, 0:n], op=ALU.mult), R=[hcr_], W=[sqhr[b2]])
                        pt, pr = K.ps[6], K.psr[6]
                        K.mm_multi([(pt[:, 0:n], ones_b, sqh[b2][:, 0:n], cc == 0, cc == 3)], R=[sqhr[b2], cbr], W=[pr])
                        K.op(dve, lambda e: e.tensor_tensor(out=mixin[:, 4 + cc, c0:c0 + n], in0=hc[:, 0:n], in1=glb[b3][:, 0:n],
                                                            op=ALU.mult), R=[hcr_, glbr[b3]], W=[mixr[4 + cc][ti]])
                        if cc == 3:
                            K.op(act, lambda e: e.activation(out=rsl[:, 0:n], in_=pt[:, 0:n], func=AF.Ln, scale=1.0 / DLRU,
                                                             bias=epsc[:, 0:1]), R=[pr, cbr], W=[rslr])
                            K.op(act, lambda e: e.activation(out=rsl[:, 0:n], in_=rsl[:, 0:n], func=AF.Exp, scale=-0.5),
                                 R=[rslr], W=[rslr])

                    def T5(k):
                        ti, cc = items[k]
                        if cc != 3:
                            return
                        c0, n = TILES[ti]
                        for c2 in range(4):
                            K.op(dve, lambda e: e.scalar_tensor_tensor(out=mixin[:, 4 + c2, c0:c0 + n], in0=mixin[:, 4 + c2, c0:c0 + n],
                                                                       scalar=V("g_lru_out", l, c2), in1=rsl[:, 0:n],
                                                                       op0=ALU.mult, op1=ALU.mult),
                                 R=[mixr[4 + c2][ti], vecr, rslr], W=[mixr[4 + c2][ti]])

                    for m in range(NIT + 4):
                        if 0 <= m - 4 < NIT:
                            T5(m - 4)
                        if 0 <= m - 3 < NIT:
                            T4(m - 3)
                        if 0 <= m - 2 < NIT:
                            T3(m - 2)
                        if 0 <= m - 1 < NIT:
                            T2(m - 1)
                        if m < NIT:
                            T1(m)
                    K.dma(sp, oh_d[l], hs_o[:], R=[hsor])
                    K.dma(sp, ph_d[l], php[:], R=[hsor])
                    r8rel()
                    r8rel()
                K.barrier()
                if dbg_d is not None and l == 0:
                    K.dma(sp, dbg_d, mixin[:], R=mixr)
                if stage < 5:
                    return
                wo = [r8(), r8()]
                with ExitStack() as os_:
                    S2 = lambda nm, shp, dt=F32: os_.enter_context(nc.sbuf_tensor(UN(nm), list(shp), dt))
                    if post_norm:
                        sq = S2("w_sq", [128, NKC, 512], BF16)
                        sqr = Res()
                        rs = [S2("w_rs0", [128, 512]), S2("w_rs1", [128, 512])]
                        rsr = [Res(), Res()]
                    for ti, (c0, n) in enumerate(TILES):
                        for m in range(NKC):
                            wt, wr = wo[m // 4]
                            mo = m % 4
                            pt, pr = K.psn()
                            K.mm(pt[:, 0:n], [(wt[:, kc, mo * 128:(mo + 1) * 128], mixin[:, kc, c0:c0 + n]) for kc in range(NKC)],
                                 R=[mixr[kc][ti] for kc in range(NKC)] + [wr], W=[pr])
                            K.op(dve, lambda e: e.tensor_tensor(out=x[:, m, c0:c0 + n], in0=pt[:, 0:n], in1=x[:, m, c0:c0 + n],
                                                                op=ALU.add), R=[pr, xr_[m][ti]], W=[xr_[m][ti]])
                        if post_norm and ti >= 1:
                            rmsnorm("g_ff2", l, xn, xnr, sq, sqr, rs, rsr, tiles=[ti - 1])
                    r8rel()
                    r8rel()
                    if post_norm:
                        rmsnorm("g_ff2", l, xn, xnr, sq, sqr, rs, rsr, tiles=[len(TILES) - 1])
            K.barrier()

        ones_f4 = SB("ones_f4", [H, 128])
        K.op(dve, lambda e: e.memset(ones_f4[:], 1.0), W=[cbr])

        MERGE = (stage >= 99) and not skip_ffn
        if MERGE:
            def post_norm_fn(gname, l_):
                return lambda sq, sqr, rs, rsr, ti: rmsnorm(gname, l_, xn, xnr, sq, sqr, rs, rsr, tiles=[ti])

            def post_final(sq, sqr, rs, rsr, ti):
                rmsnorm("g_final", 0, x, xr_, sq, sqr, rs, rsr, tiles=[ti])
                c0, n = TILES[ti]
                for kc in range(NKC):
                    K.dma(sp, yT_d[:, kc, c0:c0 + n], x[:, kc, c0:c0 + n], R=[xr_[kc][ti]])

            for l in range(DEPTH):
                ffn(l, 0, pre_norm=(l == 0), post=post_norm_fn("g_mix", l))
                mixer(l, pre_norm=False, post_norm=True)
                ffn(l, 1, pre_norm=False, post=(post_norm_fn("g_ff1", l + 1) if l + 1 < DEPTH else post_final))
            K.finish()
        else:
            for l in range(DEPTH):
                if stage >= 1 and not skip_ffn:
                    ffn(l, 0)
                if stage >= 2:
                    mixer(l)
                if stage >= 6 and not skip_ffn:
                    ffn(l, 1)
                if stage < 7:
                    break
            with ExitStack() as fs:
                S = lambda nm, shp, dt=F32: fs.enter_context(nc.sbuf_tensor(UN(nm), list(shp), dt))
                sq = S("o_sq", [128, NKC, 512], BF16)
                sqr = Res()
                rs = [S("o_rs0", [128, 512]), S("o_rs1", [128, 512])]
                rsr = [Res(), Res()]
                rmsnorm("g_final", 0, x, xr_, sq, sqr, rs, rsr)
                for kc in range(NKC):
                    K.dma(sp, yT_d[:, kc, :], x[:, kc, :], R=xr_[kc])
                K.finish()
    return nc


def _consts():
    cf = np.zeros((128, 512), np.float32)
    cf[:, 0:128] = np.eye(128, dtype=np.float32)
    s = np.arange(128)
    cf[:, 128:256] = (s[:, None] <= s[None, :]).astype(np.float32)
    i = np.arange(64)
    same = (i[:, None] % 16) == (i[None, :] % 16)
    caus = (i[:, None] // 16) <= (i[None, :] // 16)
    cf[0:64, 256:320] = (same & caus).astype(np.float32)
    cf[0:64, 320:336] = ((i[:, None] % 16) == np.arange(16)[None, :]).astype(np.float32)
    return cf


def _prep_shared(W):
    f = lambda a: np.ascontiguousarray(a, dtype=np.float32)
    out = {}
    wgu = np.empty((DEPTH, 2, NJP, 128, NKC, 512), np.float32)
    wdn = np.empty((DEPTH, 2, NJP, 128, 2, 1024), np.float32)
    for fi, (gn, un, dn) in enumerate((("w_ff1_gate", "w_ff1_up", "w_ff1_down"), ("w_ff2_gate", "w_ff2_up", "w_ff2_down"))):
        g = W[gn].reshape(DEPTH, NKC, 128, NJP, 256).transpose(0, 3, 2, 1, 4)
        u = W[un].reshape(DEPTH, NKC, 128, NJP, 256).transpose(0, 3, 2, 1, 4)
        wgu[:, fi, :, :, :, 0:256] = g
        wgu[:, fi, :, :, :, 256:512] = u
        wdn[:, fi] = W[dn].reshape(DEPTH, NJP, 2, 128, 1024).transpose(0, 1, 3, 2, 4)
    out["wgu"] = wgu
    out["wdn"] = wdn
    win = W["w_in"].reshape(DEPTH, NKC, 128, 3080)
    wq = np.empty((DEPTH, H, 128, NKC, 512), np.float32)
    for h in range(H):
        for qi in range(4):
            wq[:, h, :, :, qi * 128:(qi + 1) * 128] = win[:, :, :, qi * 512 + h * 128: qi * 512 + (h + 1) * 128].transpose(0, 2, 1, 3)
    out["win"] = wq
    out["wgt"] = f(win[:, :, :, 2048:2056].transpose(0, 2, 1, 3))
    wl = np.empty((DEPTH, 2, 128, NKC, 512), np.float32)
    for cc in range(4):
        o = (cc % 2) * 256
        wl[:, cc // 2, :, :, o:o + 128] = win[:, :, :, 2056 + cc * 128:2056 + (cc + 1) * 128].transpose(0, 2, 1, 3)
        wl[:, cc // 2, :, :, o + 128:o + 256] = win[:, :, :, 2568 + cc * 128:2568 + (cc + 1) * 128].transpose(0, 2, 1, 3)
    out["wlr"] = wl
    wrg = np.zeros((DEPTH, 128, 2, 4, 128), np.float32)
    for gi, nm in enumerate(("w_rg_a", "w_rg_x")):
        for nb in range(8):
            cc, half = nb // 2, nb % 2
            wrg[:, half * 64:(half + 1) * 64, gi, cc, half * 64:(half + 1) * 64] = W[nm][:, nb]
    out["wrg"] = wrg
    out["wout"] = f(W["w_out"].reshape(DEPTH, NKC, 128, 2, 512).transpose(0, 3, 2, 1, 4))
    vec = np.zeros((128, NV), np.float32)
    for l in range(DEPTH):
        for nm in ("g_ff1", "g_mix", "g_ff2"):
            vec[:, VOFF[(nm, l)]:VOFF[(nm, l)] + 8] = W[nm][l].reshape(8, 128).T
        o = VOFF[("conv_w", l)]
        vec[:, o:o + 16] = W["conv_w"][l].reshape(4, 4, 128).transpose(2, 0, 1).reshape(128, 16)
        for nm in ("conv_b", "b_rg_a", "b_rg_x", "lru_lambda", "g_lru_out"):
            vec[:, VOFF[(nm, l)]:VOFF[(nm, l)] + 4] = W[nm][l].reshape(4, 128).T
        vec[0:4, VOFF[("b_i", l)]] = W["b_gates"][l, 0:4]
        vec[0:4, VOFF[("b_f", l)]] = W["b_gates"][l, 4:8]
    vec[:, VOFF[("g_final", 0)]:VOFF[("g_final", 0)] + 8] = W["g_final"].reshape(8, 128).T
    out["vec"] = vec
    out["gml"] = f(np.broadcast_to(W["g_mlstm_out"][None], (128, DEPTH, 512)))
    out["cf"] = _consts()
    return out


def _prep_core(c, A):
    sl = slice(NSQ * c, NSQ * (c + 1))
    X = np.concatenate([A["meta_tokens"], A["x_prompt"][c],
                        A["x_sample"][sl].transpose(1, 0, 2).reshape(NS, D)], axis=0)
    m = {}
    m["xT"] = np.ascontiguousarray(X.T.reshape(NKC, 128, T).transpose(1, 0, 2))
    C = A["state_mlstm_C"][:, sl]
    m["sCT"] = np.ascontiguousarray(C.transpose(0, 2, 4, 1, 3))
    m["sC"] = np.ascontiguousarray(C.transpose(0, 2, 3, 1, 4))
    n = A["state_mlstm_n"][:, sl]
    m["snT"] = np.ascontiguousarray(n.transpose(0, 2, 3, 1))
    m["sn"] = np.ascontiguousarray(n.transpose(0, 2, 1, 3))
    m["sm"] = np.ascontiguousarray(A["state_mlstm_m"][:, sl].transpose(0, 2, 1))
    m["sh"] = np.ascontiguousarray(A["state_lru_h"][:, sl].reshape(DEPTH, NSQ, 4, 128).transpose(0, 3, 2, 1))
    m["scv"] = np.ascontiguousarray(A["state_conv"][:, sl].reshape(DEPTH, NSQ, 3, 4, 128).transpose(0, 4, 3, 2, 1))
    return m


_NC_CACHE = {}


def kernel(**inputs):
    A = {k: np.asarray(v, dtype=np.float32) for k, v in inputs.items()}
    shared = _prep_shared(A)
    in_maps = []
    for c in range(NCORES):
        m = dict(shared)
        m.update(_prep_core(c, A))
        in_maps.append(m)
    if "nc" not in _NC_CACHE:
        _NC_CACHE["nc"] = build_program()
    nc = _NC_CACHE["nc"]
    res = run_bass_kernel_spmd(nc, in_maps, core_ids=list(range(NCORES)))
    R = res.results
    B = NCORES
    y_prompt = np.empty((B, SEQ, D), np.float32)
    y_sample = np.empty((B * NSQ, NST, D), np.float32)
    pC = np.empty((DEPTH, B, H, DH, DH), np.float32)
    pn = np.empty((DEPTH, B, H, DH), np.float32)
    pm = np.empty((DEPTH, B, H), np.float32)
    ph = np.empty((DEPTH, B, DLRU), np.float32)
    pcv = np.empty((DEPTH, B, 3, DLRU), np.float32)
    sC = np.empty((DEPTH, B * NSQ, H, DH, DH), np.float32)
    sn = np.empty((DEPTH, B * NSQ, H, DH), np.float32)
    sm = np.empty((DEPTH, B * NSQ, H), np.float32)
    sh = np.empty((DEPTH, B * NSQ, DLRU), np.float32)
    scv = np.empty((DEPTH, B * NSQ, 3, DLRU), np.float32)
    for c in range(B):
        r = R[c]
        sl = slice(NSQ * c, NSQ * (c + 1))
        Y = r["yT"].transpose(1, 0, 2).reshape(D, T).T
        y_prompt[c] = Y[NMETA:TP]
        y_sample[sl] = Y[TP:].reshape(NST, NSQ, D).transpose(1, 0, 2)
        pct = r["pCT"]
        pC[:, c] = pct[:, :, :, 0:128].transpose(0, 1, 3, 2)
        pn[:, c] = pct[:, :, :, 128]
        pm[:, c] = r["pm"][:, :, 0]
        ph[:, c] = r["ph"].transpose(0, 2, 1).reshape(DEPTH, DLRU)
        pcv[:, c] = r["pcv"].transpose(0, 3, 2, 1).reshape(DEPTH, 3, DLRU)
        sC[:, sl] = r["oC"].transpose(0, 3, 1, 2, 4)
        sn[:, sl] = r["on"].transpose(0, 2, 1, 3)
        sm[:, sl] = r["om"].transpose(0, 2, 1)
        sh[:, sl] = r["oh"].transpose(0, 3, 2, 1).reshape(DEPTH, NSQ, DLRU)
        scv[:, sl] = r["ocv"].transpose(0, 4, 3, 2, 1).reshape(DEPTH, NSQ, 3, DLRU)
    return (y_prompt, y_sample, pC, pn, pm, ph, pcv, sC, sn, sm, sh, scv)
```
